# Optimizing a Trainium2 kernel written in Bass

```python
import functools
import jax, jax.numpy as jnp
from jax import lax
import numpy as np

D_MODEL = 1024
BATCH = 16
SEQ = 256
DEPTH = 2
DEC_BATCH = 8
DEC_SEQ = 4096
PAST_LEN = 512

GRID_W = 64
HEAD_DIM = 64
HQ_ATTN = D_MODEL // (2 * HEAD_DIM)
KV_ATTN = HQ_ATTN // 4
HQ_SWA = D_MODEL // (2 * HEAD_DIM)
KV_SWA = HQ_SWA // 4
WINDOW = 128
Q_BLOCK = 128
ROPE_THETA = 10000.0
RET_DK = 256
RET_DV = 512
RET_HEADS = D_MODEL // RET_DK
RET_CHUNK = 128
D_FF = 4 * D_MODEL
EPS = 1e-6
NEG_INF = -1e30
ADA_CHUNKS = 6

ATTN_Q_W = HQ_ATTN * HEAD_DIM
ATTN_KV_W = KV_ATTN * HEAD_DIM
SWA_Q_W = HQ_SWA * HEAD_DIM
SWA_KV_W = KV_SWA * HEAD_DIM
ATTN_SPLITS = [ATTN_Q_W, ATTN_Q_W + ATTN_KV_W, ATTN_Q_W + 2 * ATTN_KV_W,
               ATTN_Q_W + 2 * ATTN_KV_W + SWA_Q_W, ATTN_Q_W + 2 * ATTN_KV_W + SWA_Q_W + SWA_KV_W]
ATTN_IN_W = ATTN_Q_W + 2 * ATTN_KV_W + SWA_Q_W + 2 * SWA_KV_W
ATTN_OUT_W = ATTN_Q_W + SWA_Q_W
RET_QK_W = RET_HEADS * RET_DK
RET_V_W = RET_HEADS * RET_DV
RET_SPLITS = [RET_QK_W, 2 * RET_QK_W, 2 * RET_QK_W + RET_V_W]
RET_IN_W = 2 * RET_QK_W + 2 * RET_V_W

kernel_name = 'hybrid_diffusion_prefix_step'


def _rms(x, gain):
    x32 = x.astype(jnp.float32)
    y = x32 * lax.rsqrt(jnp.mean(x32 * x32, axis=-1, keepdims=True) + EPS)
    return (y * gain.astype(jnp.float32)).astype(x.dtype)


def _rope_1d(x, ang):
    cos = jnp.cos(ang)[None, :, None, :]
    sin = jnp.sin(ang)[None, :, None, :]
    x1, x2 = jnp.split(x, 2, axis=-1)
    return jnp.concatenate([x1 * cos - x2 * sin, x1 * sin + x2 * cos], axis=-1)


def _rope_2d(x):
    seq_len, d = x.shape[1], x.shape[-1]
    rows = seq_len // GRID_W
    row = jnp.repeat(jnp.arange(rows, dtype=jnp.float32), GRID_W)
    col = jnp.tile(jnp.arange(GRID_W, dtype=jnp.float32), rows)
    n_freq = d // 4
    inv = ROPE_THETA ** (-jnp.arange(n_freq, dtype=jnp.float32) / n_freq)
    x32 = x.astype(jnp.float32)
    half = d // 2
    xr = _rope_1d(x32[..., :half], row[:, None] * inv[None, :])
    xc = _rope_1d(x32[..., half:], col[:, None] * inv[None, :])
    return jnp.concatenate([xr, xc], axis=-1).astype(x.dtype)


def _attend(qb, k, v, sink=None, valid=None):
    s = jnp.einsum('bqhgd,bkhd->bhgqk', qb, k).astype(jnp.float32) * (HEAD_DIM ** -0.5)
    if valid is not None:
        s = jnp.where(valid, s, NEG_INF)
    m = jnp.max(s, axis=-1, keepdims=True)
    if sink is None:
        p = jnp.exp(s - m)
        denom = jnp.sum(p, axis=-1, keepdims=True)
    else:
        sk = sink.astype(jnp.float32)[None, :, :, None, None]
        m = jnp.maximum(m, sk)
        p = jnp.exp(s - m)
        denom = jnp.sum(p, axis=-1, keepdims=True) + jnp.exp(sk - m)
    p = (p / denom).astype(v.dtype)
    return jnp.einsum('bhgqk,bkhd->bqhgd', p, v)


def _blocked(q, n_kv, fn):
    bsz, seq_len, hq, d = q.shape
    nb = seq_len // Q_BLOCK
    qb = q.reshape(bsz, nb, Q_BLOCK, n_kv, hq // n_kv, d).transpose(1, 0, 2, 3, 4, 5)
    out = lax.map(lambda a: fn(a[0], a[1]), (qb, jnp.arange(nb)))
    return out.transpose(1, 0, 2, 3, 4, 5).reshape(bsz, seq_len, hq * d)


def _retention_scan(q, k, v, log_gamma, s0):
    bsz, seq_len, n_h, _ = q.shape
    dv = v.shape[-1]
    nc = seq_len // RET_CHUNK

    def chunks(t):
        return t.astype(jnp.float32).reshape(bsz, nc, RET_CHUNK, n_h, t.shape[-1]).transpose(1, 0, 3, 2, 4)

    pos = jnp.arange(RET_CHUNK, dtype=jnp.float32)
    diff = pos[:, None] - pos[None, :]
    lg = log_gamma[:, None, None]
    decay = jnp.exp(jnp.where(diff[None] >= 0, diff[None] * lg, NEG_INF))
    q_decay = jnp.exp((pos + 1.0)[None, :] * log_gamma[:, None])[..., None]
    k_decay = jnp.exp((RET_CHUNK - 1.0 - pos)[None, :] * log_gamma[:, None])[..., None]
    chunk_decay = jnp.exp(RET_CHUNK * log_gamma)[:, None, None]

    def step(s, xs):
        qc, kc, vc = xs
        att = jnp.einsum('bhid,bhjd->bhij', qc, kc) * decay
        o = jnp.einsum('bhij,bhje->bhie', att, vc) + jnp.einsum('bhid,bhde->bhie', qc * q_decay, s)
        s = chunk_decay * s + jnp.einsum('bhjd,bhje->bhde', kc * k_decay, vc)
        return s, o

    s_final, o = lax.scan(step, s0.astype(jnp.float32), (chunks(q), chunks(k), chunks(v)))
    o = o.transpose(1, 0, 3, 2, 4).reshape(bsz, seq_len, n_h, dv)
    return o.astype(v.dtype), s_final.astype(s0.dtype)


def _attn_mixer(h, w_in, q_gain, k_gain, sink, w_out, ctx):
    bsz, seq_len, _ = h.shape
    qa, ka, va, qs, ks, vs = jnp.split(h @ w_in, ATTN_SPLITS, axis=-1)
    qa = _rms(qa.reshape(bsz, seq_len, HQ_ATTN, HEAD_DIM), q_gain)
    ka = _rms(ka.reshape(bsz, seq_len, KV_ATTN, HEAD_DIM), k_gain)
    va = va.reshape(bsz, seq_len, KV_ATTN, HEAD_DIM)
    qs = qs.reshape(bsz, seq_len, HQ_SWA, HEAD_DIM)
    ks = ks.reshape(bsz, seq_len, KV_SWA, HEAD_DIM)
    vs = vs.reshape(bsz, seq_len, KV_SWA, HEAD_DIM)
    sink_g = sink.reshape(KV_SWA, HQ_SWA // KV_SWA)
    if ctx is None:
        out_a = _blocked(qa, KV_ATTN, lambda qb, idx: _attend(qb, ka, va))
        out_s = _blocked(qs, KV_SWA, lambda qb, idx: _attend(qb, ks, vs, sink=sink_g))
        state = (ka, va, ks, vs)
    else:
        ka_ctx, va_ctx, ks_ctx, vs_ctx = ctx
        qa, ka, qs, ks = _rope_2d(qa), _rope_2d(ka), _rope_2d(qs), _rope_2d(ks)
        k_all = jnp.concatenate([ka_ctx, ka], axis=1)
        v_all = jnp.concatenate([va_ctx, va], axis=1)
        out_a = _blocked(qa, KV_ATTN, lambda qb, idx: _attend(qb, k_all, v_all))
        pad = ((0, 0), (WINDOW, WINDOW), (0, 0), (0, 0))
        ks_pad = jnp.pad(ks, pad)
        vs_pad = jnp.pad(vs, pad)
        band = Q_BLOCK + 2 * WINDOW
        ctx_valid = jnp.ones((Q_BLOCK, ks_ctx.shape[1]), dtype=bool)

        def band_block(qb, idx):
            start = idx * Q_BLOCK
            kb = lax.dynamic_slice_in_dim(ks_pad, start, band, axis=1)
            vb = lax.dynamic_slice_in_dim(vs_pad, start, band, axis=1)
            q_pos = start + jnp.arange(Q_BLOCK)
            k_pos = start - WINDOW + jnp.arange(band)
            valid = ((jnp.abs(q_pos[:, None] - k_pos[None, :]) <= WINDOW)
                     & (k_pos >= 0)[None, :] & (k_pos < seq_len)[None, :])
            return _attend(qb, jnp.concatenate([ks_ctx, kb], axis=1), jnp.concatenate([vs_ctx, vb], axis=1),
                           sink=sink_g, valid=jnp.concatenate([ctx_valid, valid], axis=1))

        out_s = _blocked(qs, KV_SWA, band_block)
        state = ()
    return jnp.concatenate([out_a, out_s], axis=-1) @ w_out, state


def _ret_mixer(h, w_in, decay_fwd, decay_bwd, gn_gain, w_out, ctx):
    bsz, seq_len, _ = h.shape
    q, k, v, g = jnp.split(h @ w_in, RET_SPLITS, axis=-1)
    q = q.reshape(bsz, seq_len, RET_HEADS, RET_DK)
    k = k.reshape(bsz, seq_len, RET_HEADS, RET_DK)
    v = v.reshape(bsz, seq_len, RET_HEADS, RET_DV)
    if ctx is None:
        s0_f = jnp.zeros((bsz, RET_HEADS, RET_DK, RET_DV), h.dtype)
        s0_b = jnp.zeros((bsz, RET_HEADS, RET_DK, RET_DV), h.dtype)
    else:
        q, k = _rope_2d(q), _rope_2d(k)
        s0_f, s0_b = ctx
    k = k * (RET_DK ** -0.5)
    o_f, s_f = _retention_scan(q, k, v, jax.nn.log_sigmoid(decay_fwd.astype(jnp.float32)), s0_f)
    o_b, s_b = _retention_scan(jnp.flip(q, 1), jnp.flip(k, 1), jnp.flip(v, 1),
                               jax.nn.log_sigmoid(decay_bwd.astype(jnp.float32)), s0_b)
    o = _rms(o_f + jnp.flip(o_b, 1), gn_gain).reshape(bsz, seq_len, RET_V_W)
    out = (jax.nn.silu(g) * o) @ w_out
    state = (s_f, s_b) if ctx is None else ()
    return out, state


def _layer(x, cond, ada_w, ada_b, norm_mix, norm_mlp, mlp_w1, mlp_w2, mixer):
    mod = jax.nn.silu(cond) @ ada_w + ada_b
    sh1, sc1, g1, sh2, sc2, g2 = jnp.split(mod[..., None, :], ADA_CHUNKS, axis=-1)
    y, state = mixer(_rms(x, norm_mix) * (1.0 + sc1) + sh1)
    x = x + g1 * y
    h = _rms(x, norm_mlp) * (1.0 + sc2) + sh2
    x = x + g2 * (jnp.square(jax.nn.relu(h @ mlp_w1)) @ mlp_w2)
    return x, state


def setup_inputs(seed: int = 0) -> dict:
    key = jax.random.key(seed)
    keys = jax.random.split(key, 40)
    counter = [0]

    def nrm(shape, scale=1.0):
        sub = keys[counter[0]]
        counter[0] += 1
        return scale * jax.random.normal(sub, shape, jnp.float32)

    def gain(shape):
        return 1.0 + 0.1 * nrm(shape)

    decay_init = jnp.log(2.0 ** (5.0 + jnp.arange(RET_HEADS, dtype=jnp.float32)) - 1.0)
    return {
        'x_prompt': nrm((BATCH, SEQ, D_MODEL)),
        'x_sample': nrm((DEC_BATCH, DEC_SEQ, D_MODEL)),
        'c': nrm((DEC_BATCH, D_MODEL)),
        'cache_l0_attn_k': nrm((DEC_BATCH, PAST_LEN, KV_ATTN, HEAD_DIM)),
        'cache_l0_attn_v': nrm((DEC_BATCH, PAST_LEN, KV_ATTN, HEAD_DIM)),
        'cache_l0_swa_k': nrm((DEC_BATCH, PAST_LEN, KV_SWA, HEAD_DIM)),
        'cache_l0_swa_v': nrm((DEC_BATCH, PAST_LEN, KV_SWA, HEAD_DIM)),
        'state_l1_ret_fwd': nrm((DEC_BATCH, RET_HEADS, RET_DK, RET_DV), 0.5),
        'state_l1_ret_bwd': nrm((DEC_BATCH, RET_HEADS, RET_DK, RET_DV), 0.5),
        'c_ctx': nrm((D_MODEL,)),
        'l0_ada_w': nrm((D_MODEL, ADA_CHUNKS * D_MODEL), 0.5 * D_MODEL ** -0.5),
        'l0_ada_b': nrm((ADA_CHUNKS * D_MODEL,), 0.02),
        'l0_norm_mix': gain((D_MODEL,)),
        'l0_norm_mlp': gain((D_MODEL,)),
        'l0_w_in': nrm((D_MODEL, ATTN_IN_W), D_MODEL ** -0.5),
        'l0_q_norm': gain((HEAD_DIM,)),
        'l0_k_norm': gain((HEAD_DIM,)),
        'l0_sink': nrm((HQ_SWA,), 0.5),
        'l0_w_out': nrm((ATTN_OUT_W, D_MODEL), ATTN_OUT_W ** -0.5),
        'l0_mlp_w1': nrm((D_MODEL, D_FF), D_MODEL ** -0.5),
        'l0_mlp_w2': nrm((D_FF, D_MODEL), D_FF ** -0.5),
        'l1_ada_w': nrm((D_MODEL, ADA_CHUNKS * D_MODEL), 0.5 * D_MODEL ** -0.5),
        'l1_ada_b': nrm((ADA_CHUNKS * D_MODEL,), 0.02),
        'l1_norm_mix': gain((D_MODEL,)),
        'l1_norm_mlp': gain((D_MODEL,)),
        'l1_w_in': nrm((D_MODEL, RET_IN_W), D_MODEL ** -0.5),
        'l1_ret_decay_fwd': decay_init + nrm((RET_HEADS,), 0.1),
        'l1_ret_decay_bwd': decay_init + nrm((RET_HEADS,), 0.1),
        'l1_ret_gn': gain((RET_HEADS, RET_DV)),
        'l1_w_out': nrm((RET_V_W, D_MODEL), RET_V_W ** -0.5),
        'l1_mlp_w1': nrm((D_MODEL, D_FF), D_MODEL ** -0.5),
        'l1_mlp_w2': nrm((D_FF, D_MODEL), D_FF ** -0.5),
        'final_norm': gain((D_MODEL,)),
    }


def reference(x_prompt, x_sample, c, cache_l0_attn_k, cache_l0_attn_v, cache_l0_swa_k, cache_l0_swa_v,
              state_l1_ret_fwd, state_l1_ret_bwd, c_ctx,
              l0_ada_w, l0_ada_b, l0_norm_mix, l0_norm_mlp, l0_w_in, l0_q_norm, l0_k_norm, l0_sink, l0_w_out,
              l0_mlp_w1, l0_mlp_w2,
              l1_ada_w, l1_ada_b, l1_norm_mix, l1_norm_mlp, l1_w_in, l1_ret_decay_fwd, l1_ret_decay_bwd,
              l1_ret_gn, l1_w_out, l1_mlp_w1, l1_mlp_w2, final_norm):
    layer_common = [
        (l0_ada_w, l0_ada_b, l0_norm_mix, l0_norm_mlp, l0_mlp_w1, l0_mlp_w2),
        (l1_ada_w, l1_ada_b, l1_norm_mix, l1_norm_mlp, l1_mlp_w1, l1_mlp_w2),
    ]
    layer_mixer = [
        functools.partial(_attn_mixer, w_in=l0_w_in, q_gain=l0_q_norm, k_gain=l0_k_norm,
                          sink=l0_sink, w_out=l0_w_out),
        functools.partial(_ret_mixer, w_in=l1_w_in, decay_fwd=l1_ret_decay_fwd, decay_bwd=l1_ret_decay_bwd,
                          gn_gain=l1_ret_gn, w_out=l1_w_out),
    ]
    layer_cache = [
        (cache_l0_attn_k, cache_l0_attn_v, cache_l0_swa_k, cache_l0_swa_v),
        (state_l1_ret_fwd, state_l1_ret_bwd),
    ]
    x_p, x_s = x_prompt, x_sample
    new_state = []
    for layer in range(DEPTH):
        mixer = layer_mixer[layer]
        x_p, st = _layer(x_p, c_ctx, *layer_common[layer], mixer=functools.partial(mixer, ctx=None))
        new_state.extend(st)
        x_s, _ = _layer(x_s, c, *layer_common[layer], mixer=functools.partial(mixer, ctx=layer_cache[layer]))
    y_prompt = _rms(x_p, final_norm)
    y_sample = _rms(x_s, final_norm)
    new_l0_attn_k, new_l0_attn_v, new_l0_swa_k, new_l0_swa_v, new_l1_ret_fwd, new_l1_ret_bwd = new_state
    return (y_prompt, y_sample, new_l0_attn_k, new_l0_attn_v, new_l0_swa_k, new_l0_swa_v, new_l1_ret_fwd, new_l1_ret_bwd)
```

```python
import numpy as np
from contextlib import ExitStack
import concourse.bass as bass
import concourse.mybir as mybir
from concourse.bass_utils import run_bass_kernel_spmd

F32 = mybir.dt.float32
BF16 = mybir.dt.bfloat16
AF = mybir.ActivationFunctionType
ALU = mybir.AluOpType

D = 1024
NT = 9
TT = 512
EPS = 1e-6
SBW = 52000


class Op:
    __slots__ = ("eng", "fn", "deps", "dma_key", "signal", "sigval", "idx", "line")


class Sched:
    ENGS = ("pe", "act", "dve", "pool", "sp")

    def __init__(self, nc, es):
        self.nc = nc
        self.es = es
        self.ops = {e: [] for e in self.ENGS}
        self.lastw = {}
        self.readers = {}
        self.sem = {e: es.enter_context(nc.semaphore("d_" + e)) for e in self.ENGS}
        self.dsem = {}
        self.dcount = {}
        self.all_ops = []
        self.barrier_ops = None

    def op(self, eng, fn, reads=(), writes=(), dma=None):
        o = Op()
        import sys as _sys
        o.line = _sys._getframe(1).f_lineno
        o.eng = eng
        o.fn = fn
        o.dma_key = dma
        o.signal = False
        o.sigval = None
        writes = list(writes) + [r for r in reads if isinstance(r, str) and r.startswith("PS")]
        reads = [r for r in reads if not (isinstance(r, str) and r.startswith("PS"))]
        deps = set()
        for r in reads:
            w = self.lastw.get(r)
            if w is not None:
                deps.add(w)
        for w_ in writes:
            w = self.lastw.get(w_)
            if w is not None:
                deps.add(w)
            for rd in self.readers.get(w_, ()):
                deps.add(rd)
        if self.barrier_ops:
            deps.update(self.barrier_ops)
        deps.discard(o)
        o.deps = deps
        o.idx = len(self.ops[eng])
        self.ops[eng].append(o)
        self.all_ops.append(o)
        for r in reads:
            self.readers.setdefault(r, []).append(o)
        for w_ in writes:
            self.lastw[w_] = o
            self.readers[w_] = []
        if dma is not None and dma not in self.dsem:
            self.dsem[dma] = self.es.enter_context(self.nc.semaphore("q_" + dma))
            self.dcount[dma] = 0
        return o

    def barrier(self):
        print("BARRIER", {e: len(self.ops[e]) for e in self.ENGS}, "ARENA", getattr(self, "arena_top", None))
        last = set()
        for e in self.ENGS:
            if self.ops[e]:
                last.add(self.ops[e][-1])
        lastdma = {}
        for o in self.all_ops:
            if o.dma_key is not None:
                lastdma[o.dma_key] = o
        last.update(lastdma.values())
        self.barrier_ops = last
        self.lastw = {}
        self.readers = {}

    def emit(self):
        import os
        lim = int(os.environ.get("SCHED_LIMIT", "0"))
        if lim:
            keep = set(id(o) for o in self.all_ops[:lim])
            self.all_ops = self.all_ops[:lim]
            for e in self.ENGS:
                self.ops[e] = [o for o in self.ops[e] if id(o) in keep]
        print("SCHED ops:", len(self.all_ops), {e: len(self.ops[e]) for e in self.ENGS})
        if os.environ.get("SCHED_DUMP"):
            a, b = [int(x) for x in os.environ["SCHED_DUMP"].split(":")]
            for i, o in enumerate(self.all_ops[a:b]):
                print("OP", a + i, o.eng, o.line, o.dma_key)
        for o in self.all_ops:
            for d in o.deps:
                if d.eng == "pe" and o.eng == "pe" and d.dma_key is None:
                    continue
                d.signal = True
        cnt = {e: 0 for e in self.ENGS}
        for e in self.ENGS:
            for o in self.ops[e]:
                if o.dma_key is not None:
                    self.dcount[o.dma_key] += 16
                    o.sigval = (self.dsem[o.dma_key], self.dcount[o.dma_key])
                elif o.signal:
                    cnt[e] += 1
                    o.sigval = (self.sem[e], cnt[e])
        engobj = {"pe": None, "act": None, "dve": None, "pool": None, "sp": None}
        block = self.es.enter_context(self.nc.Block())

        def run(ename):
            def body(eng):
                waited = {}
                for o in self.ops[ename]:
                    need = {}
                    for d in o.deps:
                        if d.eng == "pe" and ename == "pe" and d.dma_key is None:
                            continue
                        s, v = d.sigval
                        k = id(s)
                        if waited.get(k, 0) >= v:
                            continue
                        if k not in need or need[k][1] < v:
                            need[k] = (s, v)
                    for k, (s, v) in need.items():
                        eng.wait_ge(s, v)
                        waited[k] = v
                    ins = o.fn(eng)
                    if o.dma_key is not None:
                        ins.then_inc(o.sigval[0], 16)
                    elif o.signal:
                        ins.then_inc(o.sigval[0], 1)
                if ename == "sp":
                    for key, s in self.dsem.items():
                        if self.dcount[key] > 0:
                            eng.wait_ge(s, self.dcount[key])
            return body

        block.tensor(run("pe"))
        block.scalar(run("act"))
        block.vector(run("dve"))
        block.gpsimd(run("pool"))
        block.sync(run("sp"))


def _consts():
    c = {}
    c["c_ident"] = np.eye(128, dtype=np.float32)
    mats = np.zeros((4, 128, 128), np.float32)
    mats[0] = 1.0
    mats[1, :64, :64] = 1.0
    mats[1, 64:, 64:] = 1.0
    for hb in (0, 64):
        for off in (0, 32):
            for i in range(16):
                mats[2, hb + off + 16 + i, hb + off + i] = -1.0
                mats[2, hb + off + i, hb + off + 16 + i] = 1.0
    for i in range(64):
        mats[3, 64 + i, i] = -1.0
        mats[3, i, 64 + i] = 1.0
    c["c_mats"] = mats
    t = np.arange(4096)
    row = (t // 64).astype(np.float32)
    col = (t % 64).astype(np.float32)
    inv0 = (10000.0 ** (-np.arange(16, dtype=np.float32) / 16)).astype(np.float32)
    ang = np.zeros((64, 4096), np.float32)
    for d in range(64):
        pos = row if d < 32 else col
        ang[d] = pos * inv0[d % 16]
    ang = np.concatenate([ang, ang], 0)
    c["c_cs0"] = np.stack([np.cos(ang), np.sin(ang)]).astype(np.float32)
    inv1 = (10000.0 ** (-np.arange(64, dtype=np.float32) / 64)).astype(np.float32)
    ang1 = np.zeros((128, 2, 4096), np.float32)
    for p in range(128):
        ang1[p, 0] = row * inv1[p % 64]
        ang1[p, 1] = col * inv1[p % 64]
    c["c_cs1"] = np.stack([np.cos(ang1), np.sin(ang1)]).astype(np.float32)
    kp = np.arange(128)[:, None]
    qf = np.arange(128)[None, :]
    m1 = (qf <= kp).astype(np.float32)
    m2 = (kp <= qf).astype(np.float32)
    c["c_mask"] = np.stack([np.tile(m1, (1, 4)), np.tile(m2, (1, 4))]).astype(np.float32)
    j = np.arange(128, dtype=np.float32)[:, None]
    i = np.arange(128, dtype=np.float32)[None, :]
    ret = np.zeros((128, 6 * 128 + 2), np.float32)
    ret[:, 0:128] = np.maximum(i - j, 0)
    ret[:, 128:256] = np.maximum(j - i, 0)
    ret[:, 256:384] = (i + 1) + 0 * j
    ret[:, 384:512] = (128 - i) + 0 * j
    ret[:, 512:640] = (i >= j)
    ret[:, 640:768] = (j >= i)
    ret[:, 768] = 127 - np.arange(128)
    ret[:, 769] = np.arange(128)
    c["c_ret"] = ret
    return c


def _fm(v, k):
    return np.ascontiguousarray(np.asarray(v, np.float32).reshape(k, 128).T)


def build(debug=False, phases=None):
    nc = bass.Bass("TRN2", target_bir_lowering=False)
    es = ExitStack()

    def din(name, shape, dt=F32):
        return nc.dram_tensor(name, list(shape), dt, kind="ExternalInput").ap()

    def dout(name, shape, dt=F32):
        return nc.dram_tensor(name, list(shape), dt, kind="ExternalOutput").ap()

    def dscr(name, shape, dt=F32):
        kind = "ExternalOutput" if debug else "Internal"
        return nc.dram_tensor(name, list(shape), dt, kind=kind).ap()

    xin = din("xin", [NT * TT, D])
    cond = din("cond", [128, 8, 2])
    kctx = din("kctx", [2, 512, 128])
    vctx = din("vctx", [2, 512, 128])
    s0 = din("s0", [2, 4, 256, 512])
    adaw = [din("adaw0", [D, 6 * D]), din("adaw1", [D, 6 * D])]
    adab = din("adab", [128, 2, 48])
    gains = din("gains", [128, 5, 8])
    qkg = din("qkg", [128, 2])
    sinkr = din("sinkr", [128, 8])
    decr = din("decr", [128, 8])
    gng = din("gng", [128, 16])
    w_in0 = din("w_in0", [D, 1536])
    w_out0 = din("w_out0", [D, D])
    w1 = [din("w1_0", [D, 4 * D]), din("w1_1", [D, 4 * D])]
    w2 = [din("w2_0", [4 * D, D]), din("w2_1", [4 * D, D])]
    w_in1 = din("w_in1", [D, 6 * D])
    w_out1 = din("w_out1", [2 * D, D])
    c_ident = din("c_ident", [128, 128])
    c_mats = din("c_mats", [4, 128, 128])
    c_cs0 = din("c_cs0", [2, 128, 4096])
    c_cs1 = din("c_cs1", [2, 128, 2, 4096])
    c_mask = din("c_mask", [2, 128, 512])
    c_ret = din("c_ret", [128, 770])

    y_out = dout("y_out", [NT * TT, D])
    nk = dout("nk", [2, 512, 128])
    nv = dout("nv", [2, 512, 128])
    ns = dout("ns", [2, 2, 4, 256, 512])

    xT = [dscr("xT%d" % i, [NT, 128, 8, TT]) for i in range(4)]
    hTs = [dscr("hT%d" % i, [NT, 128, 8, TT], BF16) for i in range(3)]
    qTs = dscr("qTs", [NT, 8, 128, TT], BF16)
    kTs = dscr("kTs", [NT, 8, 128, TT], BF16)
    kts = dscr("kts", [NT, 128, 4, 1024], BF16)
    vs_ = dscr("vs", [NT, 4, 128, 2048], BF16)
    gTs = dscr("gTs", [NT, 16, 128, TT], BF16)
    ofs = dscr("ofs", [NT, 4, 128, 16, 128])
    uTs = dscr("uTs", [NT, 4, 128, 16, 128], BF16)

    wbf = {"w1_0": dscr("w1_0_bf", [D, 4 * D], BF16), "w2_0": dscr("w2_0_bf", [4 * D, D], BF16), "w_in1": dscr("w_in1_bf", [D, 6 * D], BF16),
           "w_out1": dscr("w_out1_bf", [2 * D, D], BF16), "w1_1": dscr("w1_1_bf", [D, 4 * D], BF16), "w2_1": dscr("w2_1_bf", [4 * D, D], BF16),
           "wq": dscr("wq_bf", [D, D], BF16), "wo": dscr("wo_bf", [D, D], BF16)}
    wsrc = {"w1_0": w1[0], "w2_0": w2[0], "w_in1": w_in1, "w_out1": w_out1, "w1_1": w1[1], "w2_1": w2[1], "wq": w_in0[:, 0:1024], "wo": w_out0}

    big = es.enter_context(nc.sbuf_tensor("big", [128, SBW], F32))
    PP = [es.enter_context(nc.psum_tensor("pp%d" % i, [128, 1024], F32))[:] for i in range(4)]
    PS = [PP[i // 2][:, (i % 2) * 512:(i % 2) * 512 + 512] for i in range(8)]
    S = Sched(nc, es)

    class Arena:
        top = 0

    def af32(n):
        a = big[:, Arena.top:Arena.top + n]
        Arena.top += n
        S.arena_top = Arena.top
        assert Arena.top <= SBW, Arena.top
        return a

    def abf(n):
        w = (n + 1) // 2
        a = big[:, Arena.top:Arena.top + w].bitcast(BF16)
        Arena.top += w
        S.arena_top = Arena.top
        assert Arena.top <= SBW, Arena.top
        return a

    uid = [0]

    def rk(prefix="r"):
        uid[0] += 1
        return "%s%d" % (prefix, uid[0])

    class Rot:
        def __init__(self, aps, name, keys=None):
            self.aps = aps
            self.keys = keys if keys is not None else [name + str(i) for i in range(len(aps))]
            self.i = 0

        def next(self):
            k = self.i % len(self.aps)
            self.i += 1
            return self.aps[k], self.keys[k]

    psrot = Rot([p for p in PS], "PS")
    cur = {"rot": psrot}

    def bank():
        return cur["rot"].next()

    ident = af32(128)
    mats_bf = abf(4 * 128).rearrange("p (m n) -> p m n", m=4)
    ones_bf, bones_bf, rot0_bf, rot1_bf = (mats_bf[:, i, :] for i in range(4))
    modv = af32(2 * 6 * 16).rearrange("p (l s k c) -> p l s k c", l=2, s=6, k=8)
    gains_sb = af32(40).rearrange("p (a k) -> p a k", a=5)
    qkg_sb = af32(2)
    esink = af32(8)
    gng_sb = af32(16)
    PERSIST_TOP = Arena.top

    S.op("sp", lambda e: e.dma_start(out=ident, in_=c_ident), writes=["ident"], dma="c0")
    S.op("pool", lambda e: e.dma_start(out=mats_bf, in_=c_mats.rearrange("m p n -> p m n")), writes=["mats"], dma="c1")
    S.op("sp", lambda e: e.dma_start(out=gains_sb, in_=gains), writes=["gains"], dma="c0")
    S.op("sp", lambda e: e.dma_start(out=qkg_sb, in_=qkg), writes=["qkg"], dma="c0")
    S.op("sp", lambda e: e.dma_start(out=esink, in_=sinkr), writes=["esink"], dma="c0")
    S.op("sp", lambda e: e.dma_start(out=gng_sb, in_=gng), writes=["gng"], dma="c0")
    S.op("act", lambda e: e.activation(out=esink, in_=esink, func=AF.Exp), reads=["esink"], writes=["esink"])

    def phase_mod():
        Arena.top = PERSIST_TOP
        cnd = af32(16).rearrange("p (k c) -> p k c", k=8)
        scn = af32(16).rearrange("p (k c) -> p k c", k=8)
        tmp = af32(16).rearrange("p (k c) -> p k c", k=8)
        adab_sb = af32(96).rearrange("p (l j) -> p l j", l=2)
        acc = af32(2 * 96).rearrange("p (l j c) -> p l j c", l=2, j=48)
        wb = [af32(6144), af32(6144)]
        S.op("sp", lambda e: e.dma_start(out=cnd, in_=cond), writes=["cnd"], dma="c0")
        S.op("sp", lambda e: e.dma_start(out=adab_sb, in_=adab), writes=["adab"], dma="c0")
        S.op("act", lambda e: e.activation(out=tmp, in_=cnd, func=AF.Exp, scale=-1.0), reads=["cnd"], writes=["mtmp"])
        S.op("dve", lambda e: e.tensor_scalar(out=tmp, in0=tmp, scalar1=1.0, scalar2=None, op0=ALU.add), reads=["mtmp"], writes=["mtmp"])
        S.op("dve", lambda e: e.reciprocal(out=tmp, in_=tmp), reads=["mtmp"], writes=["mtmp"])
        S.op("dve", lambda e: e.tensor_tensor(out=scn, in0=cnd, in1=tmp, op=ALU.mult), reads=["mtmp", "cnd"], writes=["scn"])
        n = 0
        for l in range(2):
            for k in range(8):
                w_ap, wkey = wb[n % 2], "adw%d" % (n % 2)
                n += 1
                S.op("sp", lambda e, w_ap=w_ap, l=l, k=k: e.dma_start(out=w_ap, in_=adaw[l][k * 128:(k + 1) * 128, :]),
                     writes=[wkey], dma=wkey)
                pb, pk = bank()
                for j in range(48):
                    S.op("pe", lambda e, pb=pb, w_ap=w_ap, j=j, k=k: e.matmul(pb[:, 2 * j:2 * j + 2], lhsT=w_ap[:, j * 128:(j + 1) * 128],
                                                                          rhs=scn[:, k, :], start=True, stop=True),
                         reads=[wkey, "scn"], writes=[pk])
                pv = pb[:, 0:96].rearrange("p (j c) -> p j c", j=48)
                if k == 0:
                    S.op("dve", lambda e, pv=pv, l=l: e.tensor_copy(out=acc[:, l], in_=pv), reads=[pk], writes=["acc%d" % l])
                else:
                    S.op("dve", lambda e, pv=pv, l=l: e.tensor_tensor(out=acc[:, l], in0=acc[:, l], in1=pv, op=ALU.add),
                         reads=[pk, "acc%d" % l], writes=["acc%d" % l])
            S.op("dve", lambda e, l=l: e.tensor_tensor(out=acc[:, l], in0=acc[:, l],
                                                       in1=adab_sb[:, l, :].unsqueeze(2).to_broadcast([128, 48, 2]), op=ALU.add),
                 reads=["acc%d" % l, "adab"], writes=["acc%d" % l])
            a6 = acc[:, l].rearrange("p (s k) c -> p s k c", s=6)
            for s_i in (0, 2, 3, 5):
                S.op("dve", lambda e, l=l, s_i=s_i, a6=a6: e.tensor_copy(out=modv[:, l, s_i], in_=a6[:, s_i]),
                     reads=["acc%d" % l], writes=["modv"])
            for s_i, g_i in ((1, 2 * l), (4, 2 * l + 1)):
                S.op("dve", lambda e, l=l, s_i=s_i, a6=a6: e.tensor_scalar(out=modv[:, l, s_i], in0=a6[:, s_i], scalar1=1.0, scalar2=None, op0=ALU.add),
                     reads=["acc%d" % l], writes=["modv"])
                S.op("dve", lambda e, l=l, s_i=s_i, g_i=g_i: e.tensor_tensor(out=modv[:, l, s_i], in0=modv[:, l, s_i],
                                                                            in1=gains_sb[:, g_i, :].unsqueeze(2).to_broadcast([128, 8, 2]), op=ALU.mult),
                     reads=["modv", "gains"], writes=["modv"])
        S.barrier()

    def precast_all(names=("w1_0", "w2_0", "w_in1", "w_out1", "w1_1", "w2_1")):
        for name in names:
            src = wsrc[name]
            for k in range(src.shape[0] // 128):
                S.op("pool", lambda e, name=name, src=src, k=k: e.dma_start(out=wbf[name][k * 128:(k + 1) * 128, :], in_=src[k * 128:(k + 1) * 128, :]),
                     writes=[("pc", name, k)], dma="pc_" + name)

    def load_w(dst, src_rows, k_chunks, key, col_groups=None, per_k=False, pc=None):
        if pc is not None:
            eng_, src_rows, rd = "sp", wbf[pc], (lambda k: [("pc", pc, k)])
        else:
            eng_, rd = "pool", (lambda k: [])
        return _load_w(dst, src_rows, k_chunks, key, col_groups, per_k, eng_, rd)

    hwq = [0]

    def _load_w(dst, src_rows, k_chunks, key, col_groups, per_k, eng_, rd):
        if eng_ == "sp":
            srcv = src_rows.rearrange("(k p) n -> p k n", p=128)
            if col_groups is not None:
                for gi_, (c0, c1) in enumerate(col_groups):
                    q_ = ("sp", "act")[hwq[0] % 2]
                    hwq[0] += 1
                    S.op(q_, lambda e, c0=c0, c1=c1: e.dma_start(out=dst[:, :, c0:c1], in_=srcv[:, :, c0:c1]),
                         reads=[r_ for k in range(k_chunks) for r_ in rd(k)], writes=[(key, gi_)], dma="%s_g%d%s" % (key, gi_, q_))
            else:
                for g0 in range(0, k_chunks, 8):
                    q_ = ("sp", "act")[hwq[0] % 2]
                    hwq[0] += 1
                    S.op(q_, lambda e, g0=g0: e.dma_start(out=dst[:, g0:g0 + 8, :], in_=srcv[:, g0:g0 + 8, :]),
                         reads=[r_ for k in range(g0, g0 + 8) for r_ in rd(k)], writes=[(key, g0 // 8) if per_k else key], dma="%s_k%d%s" % (key, g0 // 8, q_))
            return
        if col_groups is not None:
            for gi_, (c0, c1) in enumerate(col_groups):
                for k in range(k_chunks):
                    S.op(eng_, lambda e, k=k, c0=c0, c1=c1: e.dma_start(out=dst[:, k, c0:c1], in_=src_rows[k * 128:(k + 1) * 128, c0:c1]),
                         reads=rd(k), writes=[(key, gi_)], dma="%s_g%d" % (key, gi_))
            return
        for k in range(k_chunks):
            S.op(eng_, lambda e, k=k: e.dma_start(out=dst[:, k, :], in_=src_rows[k * 128:(k + 1) * 128, :]),
                 reads=rd(k), writes=[(key, k // 8) if per_k else key], dma=("%s_k%d" % (key, k // 8)) if per_k else key)

    def rms_rstd(xchunks, xkeys, nchunk, inv_n, sqrot, rstd, rstd_key, onesm=None, sq_eng="act"):
        onesm = ones_bf if onesm is None else onesm
        pb, pk = bank()
        N = xchunks[0].shape[-1]
        for c in range(nchunk):
            sq, sk = sqrot.next()
            if sq_eng == "act":
                S.op("act", lambda e, sq=sq, c=c: e.activation(out=sq[:, 0:N], in_=xchunks[c], func=AF.Square), reads=[xkeys[c]], writes=[sk])
            else:
                S.op(sq_eng, lambda e, sq=sq, c=c: e.tensor_tensor(out=sq[:, 0:N], in0=xchunks[c], in1=xchunks[c], op=ALU.mult), reads=[xkeys[c]], writes=[sk])
            S.op("pe", lambda e, sq=sq, c=c, pb=pb: e.matmul(pb[:, 0:N], lhsT=onesm, rhs=sq[:, 0:N], start=(c == 0), stop=(c == nchunk - 1)),
                 reads=[sk, "mats"], writes=[pk])
        S.op("act", lambda e, pb=pb: e.activation(out=rstd[:, 0:N], in_=pb[:, 0:N], func=AF.Ln, scale=inv_n, bias=EPS), reads=[pk], writes=[rstd_key])
        S.op("act", lambda e: e.activation(out=rstd[:, 0:N], in_=rstd[:, 0:N], func=AF.Exp, scale=-0.5), reads=[rstd_key], writes=[rstd_key])

    def modulate(xchunks, xkeys, rstd, rstd_key, l, gi, ci, tmprot, outs, outkeys, after=None, chunks=None, add_eng="act"):
        for c in (range(8) if chunks is None else chunks):
            tp, tk = tmprot.next()
            if add_eng == "dve":
                S.op("dve", lambda e, c=c, tp=tp: e.tensor_tensor(out=tp, in0=xchunks[c], in1=rstd, op=ALU.mult), reads=[xkeys[c], rstd_key], writes=[tk])
                S.op("dve", lambda e, c=c, tp=tp: e.tensor_scalar(out=outs[c], in0=tp, scalar1=modv[:, l, gi + 1, c, ci:ci + 1], scalar2=modv[:, l, gi, c, ci:ci + 1],
                                                                op0=ALU.mult, op1=ALU.add),
                     reads=[tk, "modv"], writes=[outkeys[c]])
            else:
                S.op("dve", lambda e, c=c, tp=tp: e.scalar_tensor_tensor(out=tp, in0=xchunks[c], scalar=modv[:, l, gi + 1, c, ci:ci + 1], in1=rstd,
                                                                        op0=ALU.mult, op1=ALU.mult),
                     reads=[xkeys[c], rstd_key, "modv"], writes=[tk])
                S.op("act", lambda e, c=c, tp=tp: e.activation(out=outs[c], in_=tp, func=AF.Identity, bias=modv[:, l, gi, c, ci:ci + 1], scale=1.0),
                     reads=[tk, "modv"], writes=[outkeys[c]])
            if after is not None:
                after(c)

    def tile_cond(t):
        return 0 if t < 8 else 1

    res = {}

    def phase_A1():
        Arena.top = PERSIST_TOP
        KA = abf(5120)
        KS = abf(5120)
        VA = abf(40 * 192).rearrange("p (t c) -> p t c", t=40)
        VS = abf(40 * 192).rearrange("p (t c) -> p t c", t=40)
        res.update(KA=KA, KS=KS, VA=VA, VS=VS)
        res["A1_TOP"] = Arena.top
        wkv = abf(8 * 512).rearrange("p (k n) -> p k n", k=8)
        xin_b = [af32(4096).rearrange("p (s d) -> p s d", s=4) for _ in range(2)]
        xt_b = [af32(4096).rearrange("p (c n) -> p c n", c=8) for _ in range(2)]
        hT = abf(4096).rearrange("p (c n) -> p c n", c=8)
        sqrot = Rot([abf(512) for _ in range(2)], "sq")
        rstd = af32(512)
        tmprot = Rot([af32(512) for _ in range(2)], "tmp")
        kn_rot = Rot([af32(512) for _ in range(2)], "kn")
        knb_rot = Rot([abf(512) for _ in range(2)], "knb")
        t1_rot = Rot([af32(512) for _ in range(2)], "t1")
        cs_b = [af32(1024).rearrange("p (a n) -> p a n", a=2) for _ in range(2)]
        vst = af32(1024).rearrange("p (s n) -> p s n", s=4)
        kst = af32(1024).rearrange("p (a s n) -> p a s n", a=2, s=4)[:, :, :, :]
        kst = af32(1024).rearrange("p (a s n) -> p a s n", a=2, s=4)
        cst = af32(4 * 128 * 2).rearrange("p (a s n) -> p a s n", a=2, s=4)
        load_w(wkv, w_in0[:, 1024:1536], 8, "wkv")
        precast_all(("wq", "wo"))
        S.op("dve", lambda e: e.memset(VA[:, :, 64:128], 1.0), writes=["VAones"])
        S.op("dve", lambda e: e.memset(VS[:, :, 64:128], 1.0), writes=["VSones"])
        for a, (Kdst, Vdst) in enumerate(((KA, VA), (KS, VS))):
            S.op("sp", lambda e, a=a: e.dma_start(out=cst[:, 0], in_=kctx[a].rearrange("(s p) n -> p s n", p=128)), writes=["cstk"], dma="cst")
            S.op("sp", lambda e, a=a: e.dma_start(out=cst[:, 1], in_=vctx[a].rearrange("(s p) n -> p s n", p=128)), writes=["cstv"], dma="cst")
            pb, pk = bank()
            for s_ in range(4):
                S.op("pe", lambda e, s_=s_, pb=pb: e.transpose(pb[:, s_ * 128:(s_ + 1) * 128], cst[:, 0, s_, :], ident),
                     reads=["cstk", "ident"], writes=[pk])
            S.op("act", lambda e, pb=pb, Kdst=Kdst: e.copy(out=Kdst[:, 0:512], in_=pb), reads=[pk], writes=["Kctx%d" % a])
            S.op("dve", lambda e, Vdst=Vdst: e.tensor_copy(out=Vdst[:, 0:4, 0:64], in_=cst[:, 1, :, 0:64]), reads=["cstv"], writes=["Vctx%d" % a])
            S.op("dve", lambda e, Vdst=Vdst: e.tensor_copy(out=Vdst[:, 0:4, 128:192], in_=cst[:, 1, :, 64:128]), reads=["cstv"], writes=["Vctx%da" % a])
        xin_v = xin.rearrange("(t s p) d -> t p s d", s=4, p=128)
        for t in range(NT):
            ci = tile_cond(t)
            xi, xik = xin_b[t % 2], "xin%d" % (t % 2)
            xt_, xtk = xt_b[t % 2], ["xt%d_%d" % (t % 2, c) for c in range(8)]
            S.op("sp", lambda e, t=t, xi=xi: e.dma_start(out=xi, in_=xin_v[t]), writes=[xik], dma=xik)
            if t < 8:
                cs_, csk = cs_b[t % 2], "cs%d" % (t % 2)
                S.op("sp", lambda e, t=t, cs_=cs_: e.dma_start(out=cs_, in_=c_cs0[:, :, t * 512:(t + 1) * 512].rearrange("a p n -> p a n")),
                     writes=[csk], dma=csk)
            for c in range(8):
                pb, pk = bank()
                for s_ in range(4):
                    S.op("pe", lambda e, c=c, s_=s_, pb=pb, xi=xi: e.transpose(pb[:, s_ * 128:(s_ + 1) * 128], xi[:, s_, c * 128:(c + 1) * 128], ident),
                         reads=[xik, "ident"], writes=[pk])
                eng = "act" if c % 2 == 0 else "dve"
                if eng == "act":
                    S.op("act", lambda e, c=c, pb=pb, xt_=xt_: e.copy(out=xt_[:, c, :], in_=pb), reads=[pk], writes=[xtk[c]])
                else:
                    S.op("dve", lambda e, c=c, pb=pb, xt_=xt_: e.tensor_copy(out=xt_[:, c, :], in_=pb), reads=[pk], writes=[xtk[c]])
            S.op("pool", lambda e, t=t, xt_=xt_: e.dma_start(out=xT[0][t], in_=xt_), reads=xtk, writes=[("xT0", t)], dma="st_xt%d" % (t % 2))
            xch = [xt_[:, c, :] for c in range(8)]
            rms_rstd(xch, xtk, 8, 1.0 / D, sqrot, rstd, "rstd")
            hk = ["hT%d" % c for c in range(8)]
            modulate(xch, xtk, rstd, "rstd", 0, 0, ci, tmprot, [hT[:, c, :] for c in range(8)], hk)
            for a, Kdst in enumerate((KA, KS)):
                pb, pk = bank()
                for k in range(8):
                    S.op("pe", lambda e, k=k, a=a, pb=pb: e.matmul(pb, lhsT=wkv[:, k, a * 128:(a + 1) * 128], rhs=hT[:, k, :], start=(k == 0), stop=(k == 7)),
                         reads=[hk[k], "wkv"], writes=[pk])
                kn, knk = kn_rot.next()
                if a == 0:
                    S.op("act", lambda e, pb=pb, kn=kn: e.copy(out=kn, in_=pb), reads=[pk], writes=[knk])
                    rs, rsk = t1_rot.next()
                    rms_rstd([kn], [knk], 1, 1.0 / 64, sqrot, rs, rsk, onesm=bones_bf)
                    S.op("dve", lambda e, kn=kn, rs=rs: e.scalar_tensor_tensor(out=kn, in0=kn, scalar=qkg_sb[:, 1:2], in1=rs, op0=ALU.mult, op1=ALU.mult),
                         reads=[knk, rsk, "qkg"], writes=[knk])
                else:
                    S.op("act", lambda e, pb=pb, kn=kn: e.copy(out=kn, in_=pb), reads=[pk], writes=[knk])
                col0 = 512 + t * 512
                if t < 8:
                    knb, knbk = knb_rot.next()
                    S.op("act", lambda e, kn=kn, knb=knb: e.copy(out=knb, in_=kn), reads=[knk], writes=[knbk])
                    pb2, pk2 = bank()
                    S.op("pe", lambda e, pb2=pb2, knb=knb: e.matmul(pb2, lhsT=rot0_bf, rhs=knb, start=True, stop=True), reads=[knbk, "mats"], writes=[pk2])
                    t1, t1k = t1_rot.next()
                    S.op("dve", lambda e, t1=t1, pb2=pb2, cs_=cs_: e.tensor_tensor(out=t1, in0=pb2, in1=cs_[:, 1, :], op=ALU.mult), reads=[pk2, csk], writes=[t1k])
                    S.op("pool", lambda e, kn=kn, cs_=cs_: e.tensor_tensor(out=kn, in0=kn, in1=cs_[:, 0, :], op=ALU.mult), reads=[knk, csk], writes=[knk])
                    S.op("dve", lambda e, kn=kn, t1=t1, Kdst=Kdst, col0=col0: e.tensor_tensor(out=Kdst[:, col0:col0 + 512], in0=kn, in1=t1, op=ALU.add),
                         reads=[knk, t1k], writes=[("K%d" % a, t)])
                else:
                    S.op("act", lambda e, kn=kn, Kdst=Kdst, col0=col0: e.copy(out=Kdst[:, col0:col0 + 512], in_=kn), reads=[knk], writes=[("K%d" % a, t)])
                    pb2, pk2 = bank()
                    for s_ in range(4):
                        S.op("pe", lambda e, s_=s_, pb2=pb2, kn=kn: e.transpose(pb2[:, s_ * 128:(s_ + 1) * 128], kn[:, s_ * 128:(s_ + 1) * 128], ident),
                             reads=[knk, "ident"], writes=[pk2])
                    S.op("dve", lambda e, a=a, pb2=pb2: e.tensor_copy(out=kst[:, a], in_=pb2.rearrange("p (s n) -> p s n", s=4)), reads=[pk2], writes=["kst%d" % a])
                    S.op("pool", lambda e, a=a: e.dma_start(out=nk[a].rearrange("(s p) n -> p s n", p=128), in_=kst[:, a]), reads=["kst%d" % a], writes=[("nk", a)], dma="st_k%d" % a)
            for s_ in range(4):
                pb, pk = bank()
                for k in range(8):
                    S.op("pe", lambda e, k=k, s_=s_, pb=pb: e.matmul(pb[:, 0:256], lhsT=hT[:, k, s_ * 128:(s_ + 1) * 128], rhs=wkv[:, k, 256:512], start=(k == 0), stop=(k == 7)),
                         reads=[hk[k], "wkv"], writes=[pk])
                kt = 4 + t * 4 + s_
                for a, Vdst in enumerate((VA, VS)):
                    for g in range(2):
                        if (a + g) % 2 == 0:
                            S.op("act", lambda e, a=a, g=g, Vdst=Vdst, pb=pb, kt=kt: e.copy(out=Vdst[:, kt, g * 128:g * 128 + 64], in_=pb[:, a * 128 + g * 64:a * 128 + g * 64 + 64]),
                                 reads=[pk], writes=[("V%d_%d" % (a, g), kt)])
                        else:
                            S.op("dve", lambda e, a=a, g=g, Vdst=Vdst, pb=pb, kt=kt: e.tensor_copy(out=Vdst[:, kt, g * 128:g * 128 + 64], in_=pb[:, a * 128 + g * 64:a * 128 + g * 64 + 64]),
                                 reads=[pk], writes=[("V%d_%d" % (a, g), kt)])
                if t == 8:
                    S.op("dve", lambda e, s_=s_, pb=pb: e.tensor_copy(out=vst[:, s_, :], in_=pb[:, 0:256]), reads=[pk], writes=[("vst", s_)])
            if t == 8:
                for a in range(2):
                    S.op("pool", lambda e, a=a: e.dma_start(out=nv[a].rearrange("(s p) n -> p s n", p=128), in_=vst[:, :, a * 128:(a + 1) * 128]),
                         reads=[("vst", s_) for s_ in range(4)], writes=[("nv", a)], dma="st_v%d" % a)
        S.barrier()


    def phase_A2():
        Arena.top = res["A1_TOP"]
        KA, KS, VA, VS = res["KA"], res["KS"], res["VA"], res["VS"]
        wq = abf(8 * 1024).rearrange("p (k n) -> p k n", k=8)
        wo = abf(8 * 1024).rearrange("p (k n) -> p k n", k=8)
        masks = abf(2 * 512).rearrange("p (a n) -> p a n", a=2)
        xF = af32(4096).rearrange("p (c n) -> p c n", c=8)
        xB = af32(4096).rearrange("p (c n) -> p c n", c=8)
        hT = abf(4096).rearrange("p (c n) -> p c n", c=8)
        QT_b = [abf(4096).rearrange("p (c n) -> p c n", c=8) for _ in range(2)]
        OT_b = [abf(4096).rearrange("p (c n) -> p c n", c=8) for _ in range(2)]
        horot = Rot([abf(512) for _ in range(2)], "ho")
        PTrot = Rot([abf(1024) for _ in range(3)], "PT")
        sqrot = Rot([abf(512) for _ in range(2)], "sq")
        rstdF = af32(512)
        rstdB = af32(512)
        tmprot = Rot([af32(512) for _ in range(2)], "tmp")
        kn_rot = Rot([af32(512) for _ in range(2)], "kn")
        knb_rot = Rot([abf(512) for _ in range(2)], "knb")
        t1_rot = Rot([af32(512) for _ in range(2)], "t1")
        recrot = Rot([af32(512) for _ in range(2)], "rec")
        rec2rot = Rot([af32(512) for _ in range(2)], "rec2")
        osbrot = Rot([af32(512) for _ in range(2)], "osb")
        cs_ = af32(1024).rearrange("p (a n) -> p a n", a=2)
        esk2 = af32(4)
        S.op("dve", lambda e: e.tensor_copy(out=esk2[0:64, :], in_=esink[0:64, 0:4]), reads=["esink"], writes=["esk2"])
        S.op("dve", lambda e: e.tensor_copy(out=esk2[64:128, :], in_=esink[64:128, 4:8]), reads=["esink"], writes=["esk2"])
        psG = Rot(PS[4:6], "PS", ["PS4", "PS5"])
        psP = Rot([0, 1], "pair")
        cur["rot"] = psG
        load_w(wq, None, 8, "wq", pc="wq")
        load_w(wo, None, 8, "wo", pc="wo")
        S.op("pool", lambda e: e.dma_start(out=masks, in_=c_mask.rearrange("a p n -> p a n")), writes=["masks"], dma="c1")
        precast_all()
        xFk = ["xF_%d" % c for c in range(8)]
        xBk = ["xB_%d" % c for c in range(8)]
        hk = ["hT%d" % c for c in range(8)]

        def stats_gen(xch, xk, rstd_, rkey):
            pb, pk = bank()
            prev = []
            for c0 in range(0, 8, 2):
                curl = []
                for c in (c0, c0 + 1):
                    sq, sk = sqrot.next()
                    S.op("pool", lambda e, sq=sq, c=c: e.tensor_tensor(out=sq, in0=xch[c], in1=xch[c], op=ALU.mult), reads=[xk[c]], writes=[sk])
                    curl.append((c, sq, sk))
                yield
                for c, sq, sk in curl:
                    S.op("pe", lambda e, sq=sq, c=c: e.matmul(pb, lhsT=ones_bf, rhs=sq, start=(c == 0), stop=(c == 7)), reads=[sk, "mats"], writes=[pk])
            yield
            S.op("act", lambda e: e.activation(out=rstd_, in_=pb, func=AF.Ln, scale=1.0 / D, bias=EPS), reads=[pk], writes=[rkey])
            S.op("act", lambda e: e.activation(out=rstd_, in_=rstd_, func=AF.Exp, scale=-0.5), reads=[rkey], writes=[rkey])
            yield

        def front(t):
            ci = tile_cond(t)
            QT = QT_b[t % 2]
            S.op("sp", lambda e: e.dma_start(out=xF, in_=xT[0][t]), writes=xFk, dma="ld_xF")
            if t < 8:
                S.op("sp", lambda e: e.dma_start(out=cs_, in_=c_cs0[:, :, t * 512:(t + 1) * 512].rearrange("a p n -> p a n")), writes=["cs"], dma="ld_cs")
            yield
            xch = [xF[:, c, :] for c in range(8)]
            yield from stats_gen(xch, xFk, rstdF, "rstdF")
            for c_ in range(8):
                modulate(xch, xFk, rstdF, "rstdF", 0, 0, ci, tmprot, [hT[:, c, :] for c in range(8)], hk, chunks=[c_], add_eng="dve")
                if c_ % 2 == 1:
                    yield
            for qc in range(8):
                pb, pk = bank()
                for k in range(8):
                    S.op("pe", lambda e, k=k, qc=qc, pb=pb: e.matmul(pb, lhsT=wq[:, k, qc * 128:(qc + 1) * 128], rhs=hT[:, k, :], start=(k == 0), stop=(k == 7)),
                         reads=[hk[k], "wq"], writes=[pk])
                qn, qnk = kn_rot.next()
                S.op("dve", lambda e, pb=pb, qn=qn: e.tensor_copy(out=qn, in_=pb), reads=[pk], writes=[qnk])
                if qc < 4:
                    sq, sk = sqrot.next()
                    S.op("pool", lambda e, sq=sq, qn=qn: e.tensor_tensor(out=sq, in0=qn, in1=qn, op=ALU.mult), reads=[qnk], writes=[sk])
                    yield
                    pbs, pks = bank()
                    S.op("pe", lambda e, sq=sq, pbs=pbs: e.matmul(pbs, lhsT=bones_bf, rhs=sq, start=True, stop=True), reads=[sk, "mats"], writes=[pks])
                    yield
                    rs, rsk = t1_rot.next()
                    S.op("act", lambda e, pbs=pbs, rs=rs: e.activation(out=rs, in_=pbs, func=AF.Ln, scale=1.0 / 64, bias=EPS), reads=[pks], writes=[rsk])
                    S.op("act", lambda e, rs=rs: e.activation(out=rs, in_=rs, func=AF.Exp, scale=-0.5), reads=[rsk], writes=[rsk])
                    S.op("dve", lambda e, qn=qn, rs=rs: e.scalar_tensor_tensor(out=qn, in0=qn, scalar=qkg_sb[:, 0:1], in1=rs, op0=ALU.mult, op1=ALU.mult),
                         reads=[qnk, rsk, "qkg"], writes=[qnk])
                if t < 8:
                    knb, knbk = knb_rot.next()
                    S.op("dve", lambda e, qn=qn, knb=knb: e.tensor_copy(out=knb, in_=qn), reads=[qnk], writes=[knbk])
                    yield
                    pb2, pk2 = bank()
                    S.op("pe", lambda e, pb2=pb2, knb=knb: e.matmul(pb2, lhsT=rot0_bf, rhs=knb, start=True, stop=True), reads=[knbk, "mats"], writes=[pk2])
                    S.op("pool", lambda e, qn=qn: e.tensor_tensor(out=qn, in0=qn, in1=cs_[:, 0, :], op=ALU.mult), reads=[qnk, "cs"], writes=[qnk])
                    yield
                    t1, t1k = t1_rot.next()
                    S.op("dve", lambda e, t1=t1, pb2=pb2: e.tensor_tensor(out=t1, in0=pb2, in1=cs_[:, 1, :], op=ALU.mult), reads=[pk2, "cs"], writes=[t1k])
                    S.op("pool", lambda e, qn=qn, t1=t1, qc=qc: e.tensor_tensor(out=QT[:, qc, :], in0=qn, in1=t1, op=ALU.add),
                         reads=[qnk, t1k], writes=[("QT", t % 2, qc)])
                else:
                    S.op("pool", lambda e, qn=qn, qc=qc: e.tensor_copy(out=QT[:, qc, :], in_=qn), reads=[qnk], writes=[("QT", t % 2, qc)])
                yield

        def back(t):
            ci = tile_cond(t)
            OT = OT_b[t % 2]
            S.op("sp", lambda e: e.dma_start(out=xB, in_=xT[0][t]), writes=xBk, dma="ld_xB")
            yield
            otk = [("OT", t % 2, typ, g, qb) for typ in range(2) for g in range(2) for qb in range(4)]
            for m in range(8):
                pb, pk = bank()
                for ch in range(8):
                    S.op("pe", lambda e, ch=ch, m=m, pb=pb: e.matmul(pb, lhsT=wo[:, ch, m * 128:(m + 1) * 128], rhs=OT[:, ch, :], start=(ch == 0), stop=(ch == 7)),
                         reads=otk + ["wo"], writes=[pk])
                S.op("dve", lambda e, m=m, pb=pb: e.scalar_tensor_tensor(out=xB[:, m, :], in0=pb, scalar=modv[:, 0, 2, m, ci:ci + 1], in1=xB[:, m, :],
                                                                       op0=ALU.mult, op1=ALU.add),
                     reads=[pk, xBk[m], "modv"], writes=[xBk[m]])
                yield
            S.op("sp", lambda e: e.dma_start(out=xT[1][t], in_=xB), reads=xBk, writes=[("xT1", t)], dma="st_xB")
            xch = [xB[:, c, :] for c in range(8)]
            yield from stats_gen(xch, xBk, rstdB, "rstdB")
            bufs = [horot.next() for _ in range(8)]

            def after(c):
                S.op("sp", lambda e, c=c: e.dma_start(out=hTs[0][t][:, c, :], in_=bufs[c][0]), reads=[bufs[c][1]], writes=[("hTs0", t, c)], dma="st2_" + bufs[c][1])
            for c_ in range(8):
                modulate(xch, xBk, rstdB, "rstdB", 0, 3, ci, tmprot, [b_[0] for b_ in bufs], [b_[1] for b_ in bufs], after=after, chunks=[c_], add_eng="dve")
                if c_ % 2 == 1:
                    yield

        def attn(t, filler):
            QT = QT_b[t % 2]
            OT = OT_b[t % 2]
            cnt = [0]

            def fill():
                cnt[0] += 1
                if cnt[0] % 2 == 0:
                    next(filler, None)
            for qb in range(4):
                for typ in range(2):
                    Ksrc, Vsrc = (KA, VA) if typ == 0 else (KS, VS)
                    kname = "K%d" % typ
                    base = 4 * typ
                    kl = []
                    if t < 8:
                        for j in range(4):
                            kl.append((j * 128, j, None, ["Kctx%d" % typ, "Vctx%d" % typ, "Vctx%da" % typ]))
                        if typ == 0:
                            for j in range(32):
                                kl.append((512 + j * 128, 4 + j, None, [(kname, j // 4)] + [("V%d_%d" % (typ, g_), 4 + j) for g_ in range(2)]))
                        else:
                            qbg = t * 4 + qb
                            for dj, mi in ((-1, 0), (0, None), (1, 1)):
                                j = qbg + dj
                                if 0 <= j < 32:
                                    kl.append((512 + j * 128, 4 + j, mi, [(kname, j // 4)] + [("V%d_%d" % (typ, g_), 4 + j) for g_ in range(2)]))
                    else:
                        sq_ = qb // 2
                        for j in range(2):
                            jj = 32 + sq_ * 2 + j
                            kl.append((512 + jj * 128, 4 + jj, None, [(kname, 8)] + [("V%d_%d" % (typ, g_), 4 + jj) for g_ in range(2)]))

                    def attn_pair(qb=qb, typ=typ, Ksrc=Ksrc, Vsrc=Vsrc, kname=kname, base=base, kl=kl):
                        nk_ = len(kl)
                        obs = [(PS[6], "PS6"), (PS[7], "PS7")]
                        pts = [None] * nk_

                        def isk(d_):
                            return (isinstance(d_, tuple) and d_[0] == kname) or (isinstance(d_, str) and d_.startswith("Kctx"))

                        def emit_s(j):
                            kcol, vt, mi, deps = kl[j]
                            pi = psP.next()[0]
                            keys = ["PS%d" % (2 * pi), "PS%d" % (2 * pi + 1)]
                            for g in range(2):
                                rows = slice(64 * g, 64 * g + 64)
                                S.op("pe", lambda e, g=g, rows=rows, kcol=kcol, pi=pi: e.matmul(PS[2 * pi + g].rearrange("p (h n) -> p h n", h=4), lhsT=Ksrc[rows, kcol:kcol + 128],
                                                                                              rhs=QT[rows, base:base + 4, qb * 128:(qb + 1) * 128], start=True, stop=True),
                                     reads=[d_ for d_ in deps if isk(d_)] + [("QT", t % 2, base + h_) for h_ in range(4)], writes=[keys[g]])
                            pt, ptk = PTrot.next()
                            S.op("act", lambda e, pi=pi, pt=pt: e.activation(out=pt, in_=PP[pi], func=AF.Exp, scale=0.125), reads=keys, writes=[ptk])
                            if mi is not None:
                                S.op("dve", lambda e, pt=pt, mi=mi: e.tensor_tensor(out=pt.rearrange("p (g n) -> p g n", g=2), in0=pt.rearrange("p (g n) -> p g n", g=2),
                                                                                 in1=masks[:, mi, :].unsqueeze(1).to_broadcast([128, 2, 512]), op=ALU.mult),
                                     reads=[ptk, "masks"], writes=[ptk])
                            pts[j] = (pt, ptk)

                        def emit_pv(j):
                            kcol, vt, mi, deps = kl[j]
                            pt, ptk = pts[j]
                            vdeps = [d_ for d_ in deps if not isk(d_)]
                            for g in range(2):
                                S.op("pe", lambda e, g=g, pt=pt, vt=vt, j=j: e.matmul(obs[g][0], lhsT=Vsrc[:, vt, 64 * g:64 * g + 128], rhs=pt[:, g * 512:(g + 1) * 512],
                                                                                   start=(j == 0), stop=(j == nk_ - 1)),
                                     reads=[ptk] + vdeps + ["V%sones" % ("A" if typ == 0 else "S")], writes=[obs[g][1]])
                        LA = 2
                        for j in range(min(LA, nk_)):
                            emit_s(j)
                        for j in range(nk_):
                            if j + LA < nk_:
                                emit_s(j + LA)
                            emit_pv(j)
                            fill()
                        rec, reck = recrot.next()
                        osb, osbk = osbrot.next()
                        for g in range(2):
                            rows = slice(64 * g, 64 * g + 64)
                            drows = slice(64 * (1 - g), 64 * (1 - g) + 64)
                            ob, obk = obs[g]
                            S.op("dve", lambda e, ob=ob, rows=rows, drows=drows: e.tensor_copy(out=rec[rows, :], in_=ob[drows, :]), reads=[obk], writes=[(reck, g)])
                            S.op("dve", lambda e, ob=ob, rows=rows: e.tensor_copy(out=osb[rows, :], in_=ob[rows, :]), reads=[obk], writes=[(osbk, g)])
                        if typ == 1:
                            S.op("dve", lambda e: e.tensor_tensor(out=rec.rearrange("p (h n) -> p h n", h=4), in0=rec.rearrange("p (h n) -> p h n", h=4),
                                                                  in1=esk2.unsqueeze(2).to_broadcast([128, 4, 128]), op=ALU.add),
                                 reads=[(reck, 0), (reck, 1), "esk2"], writes=[(reck, 0), (reck, 1)])
                        rec2, rec2k = rec2rot.next()
                        S.op("dve", lambda e: e.reciprocal(out=rec2, in_=rec), reads=[(reck, 0), (reck, 1)], writes=[rec2k])
                        S.op("pool", lambda e: e.tensor_tensor(out=OT[:, base:base + 4, qb * 128:(qb + 1) * 128],
                                                               in0=osb.rearrange("p (h n) -> p h n", h=4),
                                                               in1=rec2.rearrange("p (h n) -> p h n", h=4), op=ALU.mult),
                             reads=[(osbk, 0), (osbk, 1), rec2k], writes=[("OT", t % 2, typ, 0, qb), ("OT", t % 2, typ, 1, qb)])
                    attn_pair()

        def chain(*gens):
            for g_ in gens:
                if g_ is not None:
                    yield from g_

        for _ in front(0):
            pass
        for t in range(NT):
            filler = chain(back(t - 1) if t >= 1 else None, front(t + 1) if t + 1 < NT else None)
            attn(t, filler)
            for _ in filler:
                pass
        for _ in back(NT - 1):
            pass
        cur["rot"] = psrot
        S.barrier()

    def phase_mlp(l):
        Arena.top = PERSIST_TOP
        w1s = abf(8 * 4096).rearrange("p (k n) -> p k n", k=8)
        w2s = abf(32 * 1024).rearrange("p (k n) -> p k n", k=32)
        hT = abf(4096).rearrange("p (c n) -> p c n", c=8)
        aT = abf(32 * 512).rearrange("p (c n) -> p c n", c=32)
        xt_ = af32(4096).rearrange("p (c n) -> p c n", c=8)
        rrot = Rot([abf(512) for _ in range(2)], "rr")
        sqrot = Rot([abf(512) for _ in range(2)], "sq")
        rstd = af32(512)
        if l == 0:
            tmprot = Rot([af32(512) for _ in range(2)], "tmp")
            horot = Rot([abf(512) for _ in range(2)], "ho")
        else:
            ytrot = Rot([af32(1024) for _ in range(2)], "yt")
        load_w(w1s, None, 8, "w1s", col_groups=[(i * 1024, (i + 1) * 1024) for i in range(4)], pc="w1_%d" % l)
        load_w(w2s, None, 32, "w2s", per_k=True, pc="w2_%d" % l)
        hsrc = hTs[0] if l == 0 else hTs[2]
        xsrc = xT[1] if l == 0 else xT[3]
        xtk = ["xt_%d" % c for c in range(8)]
        hk = ["hT%d" % c for c in range(8)]

        def L1(t):
            S.op("sp", lambda e: e.dma_start(out=hT, in_=hsrc[t]), writes=hk, dma="ld_h")
            for f in range(32):
                pb, pk = bank()
                for k in range(8):
                    S.op("pe", lambda e, k=k, f=f, pb=pb: e.matmul(pb, lhsT=w1s[:, k, f * 128:(f + 1) * 128], rhs=hT[:, k, :], start=(k == 0), stop=(k == 7)),
                         reads=[hk[k], ("w1s", f // 8)], writes=[pk])
                r, rk_ = rrot.next()
                S.op("act", lambda e, pb=pb, r=r: e.activation(out=r, in_=pb, func=AF.Relu), reads=[pk], writes=[rk_])
                S.op("dve" if f % 2 == 0 else "pool", lambda e, r=r, f=f: e.tensor_tensor(out=aT[:, f, :], in0=r, in1=r, op=ALU.mult), reads=[rk_], writes=[("aT", f)])

        def L2(t):
            ci = tile_cond(t)
            S.op("sp", lambda e: e.dma_start(out=xt_, in_=xsrc[t]), writes=xtk, dma="ld_x")
            for m in range(8):
                pb, pk = bank()
                for f in range(32):
                    S.op("pe", lambda e, f=f, m=m, pb=pb: e.matmul(pb, lhsT=w2s[:, f, m * 128:(m + 1) * 128], rhs=aT[:, f, :], start=(f == 0), stop=(f == 31)),
                         reads=[("aT", f), ("w2s", f // 8)], writes=[pk])
                S.op("dve", lambda e, m=m, pb=pb: e.scalar_tensor_tensor(out=xt_[:, m, :], in0=pb, scalar=modv[:, l, 5, m, ci:ci + 1], in1=xt_[:, m, :],
                                                                       op0=ALU.mult, op1=ALU.add),
                     reads=[pk, xtk[m], "modv"], writes=[xtk[m]])

        def epi(t):
            ci = tile_cond(t)
            xch = [xt_[:, c, :] for c in range(8)]
            if l == 0:
                S.op("pool", lambda e: e.dma_start(out=xT[2][t], in_=xt_), reads=xtk, writes=[("xT2", t)], dma="st_x")
                rms_rstd(xch, xtk, 8, 1.0 / D, sqrot, rstd, "rstd")
                bufs = [horot.next() for _ in range(8)]

                def after(c):
                    S.op("pool", lambda e, c=c: e.dma_start(out=hTs[1][t][:, c, :], in_=bufs[c][0]), reads=[bufs[c][1]], writes=[("hTs1", t, c)], dma="st_" + bufs[c][1])
                modulate(xch, xtk, rstd, "rstd", 1, 0, ci, tmprot, [b_[0] for b_ in bufs], [b_[1] for b_ in bufs], after=after)
            else:
                rms_rstd(xch, xtk, 8, 1.0 / D, sqrot, rstd, "rstd")
                for c in range(8):
                    S.op("dve", lambda e, c=c: e.scalar_tensor_tensor(out=xt_[:, c, :], in0=xt_[:, c, :], scalar=gains_sb[:, 4, c:c + 1], in1=rstd, op0=ALU.mult, op1=ALU.mult),
                         reads=[xtk[c], "rstd", "gains"], writes=[xtk[c]])
                for s_ in range(4):
                    yt, ytk = ytrot.next()
                    for hf in range(2):
                        pb, pk = bank()
                        for cc in range(4):
                            c = 4 * hf + cc
                            S.op("pe", lambda e, c=c, cc=cc, pb=pb, s_=s_: e.transpose(pb[:, cc * 128:(cc + 1) * 128], xt_[:, c, s_ * 128:(s_ + 1) * 128], ident),
                                 reads=[xtk[c], "ident"], writes=[pk])
                        if hf == 0:
                            S.op("act", lambda e, pb=pb, yt=yt: e.copy(out=yt[:, 0:512], in_=pb), reads=[pk], writes=[ytk + "a"])
                        else:
                            S.op("dve", lambda e, pb=pb, yt=yt: e.tensor_copy(out=yt[:, 512:1024], in_=pb), reads=[pk], writes=[ytk + "b"])
                    r0 = t * 512 + s_ * 128
                    S.op("pool", lambda e, yt=yt, r0=r0: e.dma_start(out=y_out[r0:r0 + 128, :], in_=yt), reads=[ytk + "a", ytk + "b"],
                         writes=[("y", t, s_)], dma="st_" + ytk)

        L1(0)
        L2(0)
        for t in range(1, NT):
            L1(t)
            epi(t - 1)
            L2(t)
        epi(NT - 1)
        S.barrier()


    def phase_B1():
        Arena.top = PERSIST_TOP
        wi = abf(8 * 6144).rearrange("p (k n) -> p k n", k=8)
        hT_b = [abf(4096).rearrange("p (c n) -> p c n", c=8) for _ in range(2)]
        cs_b = [af32(2048).rearrange("p (a r n) -> p a r n", a=2, r=2)]
        kn_rot = Rot([af32(512) for _ in range(4)], "kn")
        knb_rot = Rot([abf(512) for _ in range(3)], "knb")
        t1_rot = Rot([af32(512) for _ in range(2)], "t1")
        qo_rot = Rot([abf(512) for _ in range(4)], "qo")
        go_rot = Rot([abf(512) for _ in range(4)], "go")
        ktok_b = [abf(4096).rearrange("p (s n) -> p s n", s=4)]
        vb_rot = Rot([abf(2048) for _ in range(2)], "vb")
        wi_groups = [(0, 1024), (1024, 2048), (2048, 3072), (3072, 4096), (4096, 5120), (5120, 6144)]
        load_w(wi, None, 8, "wi", col_groups=wi_groups, pc="w_in1")

        def b1_tile(t):
            hT, hk = hT_b[t % 2], ["hT%d_%d" % (t % 2, c) for c in range(8)]
            S.op("sp", lambda e: e.dma_start(out=hT, in_=hTs[1][t]), writes=hk, dma="ld_h%d" % (t % 2))
            cs_ = cs_b[0]
            if t < 8:
                for a_ in range(2):
                    S.op("sp", lambda e, a_=a_: e.dma_start(out=cs_[:, a_], in_=c_cs1[a_][:, :, t * 512:(t + 1) * 512]), writes=["cs"], dma="ld_cs")
            ktok = ktok_b[0]
            chunks = [(typ, qc) for typ in range(2) for qc in range(8)]
            st = {}

            def stage_proj(i):
                typ, qc = chunks[i]
                pb, pk = bank()
                c0 = typ * 1024 + qc * 128
                for k in range(8):
                    S.op("pe", lambda e, k=k, pb=pb, c0=c0: e.matmul(pb, lhsT=wi[:, k, c0:c0 + 128], rhs=hT[:, k, :], start=(k == 0), stop=(k == 7)),
                         reads=[hk[k], ("wi", c0 // 1024)], writes=[pk])
                qn, qnk = kn_rot.next()
                sc = 1.0 if typ == 0 else 1.0 / 16.0
                S.op("act", lambda e, pb=pb, qn=qn, sc=sc: e.activation(out=qn, in_=pb, func=AF.Identity, scale=sc), reads=[pk], writes=[qnk])
                st[i] = dict(qn=qn, qnk=qnk)
                if t < 8:
                    knb, knbk = knb_rot.next()
                    S.op("dve", lambda e, qn=qn, knb=knb: e.tensor_copy(out=knb, in_=qn), reads=[qnk], writes=[knbk])
                    st[i].update(knb=knb, knbk=knbk)

            def stage_rope(i):
                typ, qc = chunks[i]
                dc = qc % 2
                qn, qnk = st[i]["qn"], st[i]["qnk"]
                qo, qok = qo_rot.next()
                if t < 8:
                    knb, knbk = st[i]["knb"], st[i]["knbk"]
                    pb2, pk2 = bank()
                    S.op("pe", lambda e, pb2=pb2, knb=knb: e.matmul(pb2, lhsT=rot1_bf, rhs=knb, start=True, stop=True), reads=[knbk, "mats"], writes=[pk2])
                    t1, t1k = t1_rot.next()
                    S.op("dve", lambda e, t1=t1, pb2=pb2, dc=dc: e.tensor_tensor(out=t1, in0=pb2, in1=cs_[:, 1, dc, :], op=ALU.mult), reads=[pk2, "cs"], writes=[t1k])
                    S.op("pool", lambda e, qn=qn, dc=dc: e.tensor_tensor(out=qn, in0=qn, in1=cs_[:, 0, dc, :], op=ALU.mult), reads=[qnk, "cs"], writes=[qnk])
                    if typ == 0:
                        S.op("dve", lambda e, qn=qn, t1=t1, qo=qo: e.tensor_tensor(out=qo, in0=qn, in1=t1, op=ALU.add), reads=[qnk, t1k], writes=[qok])
                    else:
                        S.op("dve", lambda e, qn=qn, t1=t1: e.tensor_tensor(out=qn, in0=qn, in1=t1, op=ALU.add), reads=[qnk, t1k], writes=[qnk])
                        S.op("act", lambda e, qn=qn, qo=qo: e.copy(out=qo, in_=qn), reads=[qnk], writes=[qok])
                else:
                    S.op("act", lambda e, qn=qn, qo=qo: e.copy(out=qo, in_=qn), reads=[qnk], writes=[qok])
                dst = (qTs if typ == 0 else kTs)
                S.op("act", lambda e, qo=qo, dst=dst, qc=qc: e.dma_start(out=dst[t][qc], in_=qo), reads=[qok], writes=[("qk", typ, t, qc)], dma="st_" + qok)

            def stage_tr(i):
                typ, qc = chunks[i]
                if typ != 1:
                    return
                qn, qnk = st[i]["qn"], st[i]["qnk"]
                pb3, pk3 = bank()
                for s_ in range(4):
                    S.op("pe", lambda e, s_=s_, pb3=pb3, qn=qn: e.transpose(pb3[:, s_ * 128:(s_ + 1) * 128], qn[:, s_ * 128:(s_ + 1) * 128], ident),
                         reads=[qnk, "ident"], writes=[pk3])
                S.op("dve", lambda e, pb3=pb3, qc=qc: e.tensor_copy(out=ktok[:, :, qc * 128:(qc + 1) * 128], in_=pb3.rearrange("p (s n) -> p s n", s=4)),
                     reads=[pk3], writes=[("ktok", qc)])
            nch = len(chunks)
            for i in range(nch + 2):
                if i < nch:
                    stage_proj(i)
                if 0 <= i - 1 < nch:
                    stage_rope(i - 1)
                if 0 <= i - 2 < nch:
                    stage_tr(i - 2)
            S.op("act", lambda e: e.dma_start(out=kts[t], in_=ktok), reads=[("ktok", qc) for qc in range(8)], writes=[("kts", t)], dma="st_ktok")
            for s_ in range(4):
                vb, vbk = vb_rot.next()
                for vg in range(4):
                    pb, pk = bank()
                    for k in range(8):
                        S.op("pe", lambda e, k=k, pb=pb, s_=s_, vg=vg: e.matmul(pb, lhsT=hT[:, k, s_ * 128:(s_ + 1) * 128], rhs=wi[:, k, 2048 + vg * 512:2048 + (vg + 1) * 512],
                                                                            start=(k == 0), stop=(k == 7)),
                             reads=[hk[k], ("wi", 2 + vg // 2)], writes=[pk])
                    if vg % 2 == 0:
                        S.op("act", lambda e, pb=pb, vb=vb, vg=vg: e.copy(out=vb[:, vg * 512:(vg + 1) * 512], in_=pb), reads=[pk], writes=[(vbk, vg)])
                    else:
                        S.op("dve", lambda e, pb=pb, vb=vb, vg=vg: e.tensor_copy(out=vb[:, vg * 512:(vg + 1) * 512], in_=pb), reads=[pk], writes=[(vbk, vg)])
                S.op("act", lambda e, vb=vb, s_=s_: e.dma_start(out=vs_[t][s_], in_=vb), reads=[(vbk, vg) for vg in range(4)], writes=[("vs", t, s_)], dma="st_" + vbk)
            for gc in range(16):
                pb, pk = bank()
                c0 = 4096 + gc * 128
                for k in range(8):
                    S.op("pe", lambda e, k=k, pb=pb, c0=c0: e.matmul(pb, lhsT=wi[:, k, c0:c0 + 128], rhs=hT[:, k, :], start=(k == 0), stop=(k == 7)),
                         reads=[hk[k], ("wi", c0 // 1024)], writes=[pk])
                go, gok = go_rot.next()
                S.op("act", lambda e, pb=pb, go=go: e.activation(out=go, in_=pb, func=AF.Silu), reads=[pk], writes=[gok])
                S.op("act", lambda e, go=go, gc=gc: e.dma_start(out=gTs[t][gc], in_=go), reads=[gok], writes=[("gTs", t, gc)], dma="st_" + gok)
        for t in range(NT):
            b1_tile(t)
        S.barrier()

    def ret_tables():
        rt = af32(770)
        lg = af32(8)
        c128 = af32(1)
        decT = af32(8 * 128).rearrange("p (a n) -> p a n", a=8)
        qdec = af32(8 * 128).rearrange("p (a n) -> p a n", a=8)
        kdec = af32(8)
        cdec = af32(8)
        S.op("sp", lambda e: e.dma_start(out=rt, in_=c_ret), writes=["rt"], dma="c0")
        S.op("sp", lambda e: e.dma_start(out=lg, in_=decr), writes=["lg"], dma="c0")
        S.op("dve", lambda e: e.memset(c128, 128.0), writes=["c128"])
        S.op("act", lambda e: e.activation(out=lg, in_=lg, func=AF.Exp, scale=-1.0), reads=["lg"], writes=["lg"])
        S.op("dve", lambda e: e.tensor_scalar(out=lg, in0=lg, scalar1=1.0, scalar2=None, op0=ALU.add), reads=["lg"], writes=["lg"])
        S.op("act", lambda e: e.activation(out=lg, in_=lg, func=AF.Ln), reads=["lg"], writes=["lg"])
        S.op("dve", lambda e: e.tensor_scalar(out=lg, in0=lg, scalar1=-1.0, scalar2=None, op0=ALU.mult), reads=["lg"], writes=["lg"])
        for d in range(2):
            for h in range(4):
                a = 4 * d + h
                S.op("act", lambda e, a=a, d=d: e.activation(out=decT[:, a, :], in_=rt[:, d * 128:(d + 1) * 128], func=AF.Exp, scale=lg[:, a:a + 1]), reads=["rt", "lg"], writes=[("decT", a)])
                S.op("dve", lambda e, a=a, d=d: e.tensor_tensor(out=decT[:, a, :], in0=decT[:, a, :], in1=rt[:, 512 + d * 128:512 + (d + 1) * 128], op=ALU.mult),
                     reads=[("decT", a), "rt"], writes=[("decT", a)])
                S.op("act", lambda e, a=a, d=d: e.activation(out=qdec[:, a, :], in_=rt[:, 256 + d * 128:256 + (d + 1) * 128], func=AF.Exp, scale=lg[:, a:a + 1]), reads=["rt", "lg"], writes=[("qdec", a)])
                S.op("act", lambda e, a=a, d=d: e.activation(out=kdec[:, a:a + 1], in_=rt[:, 768 + d:769 + d], func=AF.Exp, scale=lg[:, a:a + 1]), reads=["rt", "lg"], writes=[("kdec", a)])
                S.op("act", lambda e, a=a: e.activation(out=cdec[:, a:a + 1], in_=c128, func=AF.Exp, scale=lg[:, a:a + 1]), reads=["c128", "lg"], writes=[("cdec", a)])
        return decT, qdec, kdec, cdec

    def phase_scan(d):
        Arena.top = PERSIST_TOP
        decT, qdec, kdec, cdec = ret_tables()
        S32 = af32(4096).rearrange("p (h c e) -> p h c e", h=4, c=2)
        Sbf = [abf(4096).rearrange("p (h c e) -> p h c e", h=4, c=2) for _ in range(2)]
        qT_b = [abf(4096).rearrange("p (c n) -> p c n", c=8) for _ in range(2)]
        kT_b = [abf(4096).rearrange("p (c n) -> p c n", c=8) for _ in range(2)]
        kt_b = [abf(4096).rearrange("p (s n) -> p s n", s=4) for _ in range(2)]
        v_b = [abf(8192).rearrange("p (s n) -> p s n", s=4) for _ in range(2)]
        attm_rot = Rot([abf(128) for _ in range(4)], "attm")
        qs_rot = Rot([abf(256).rearrange("p (c n) -> p c n", c=2) for _ in range(4)], "qs")
        kf_rot = Rot([abf(256) for _ in range(4)], "kf")
        if d == 0:
            of_rot = Rot([af32(2048).rearrange("p (c n) -> p c n", c=16) for _ in range(2)], "ofst")
        else:
            gT = abf(16 * 512).rearrange("p (c n) -> p c n", c=16)
            of_rot = Rot([af32(2048).rearrange("p (c n) -> p c n", c=16) for _ in range(2)], "ofld")
            osum = af32(2048).rearrange("p (c n) -> p c n", c=16)
            sq4 = [abf(512).rearrange("p (c n) -> p c n", c=4) for _ in range(4)]
            rs_rot = Rot([af32(128) for _ in range(4)], "rsh")
            tmp_rot = Rot([af32(512).rearrange("p (c n) -> p c n", c=4) for _ in range(4)], "gtmp")
            u_rot = Rot([abf(2048).rearrange("p (c n) -> p c n", c=16) for _ in range(2)], "ust")
        sidx = [0]

        def scan_tile(t, n):
            b = n % 2
            qT, kT, kt, v = qT_b[b], kT_b[b], kt_b[b], v_b[b]
            S.op("sp", lambda e: e.dma_start(out=qT, in_=qTs[t].rearrange("c p n -> p c n")), writes=["qT%d" % b], dma="ld_q%d" % b)
            S.op("sp", lambda e: e.dma_start(out=kT, in_=kTs[t].rearrange("c p n -> p c n")), writes=["kT%d" % b], dma="ld_k%d" % b)
            S.op("sp", lambda e: e.dma_start(out=kt, in_=kts[t]), writes=["kt%d" % b], dma="ld_kt%d" % b)
            S.op("sp", lambda e: e.dma_start(out=v, in_=vs_[t].rearrange("s p n -> p s n")), writes=["v%d" % b], dma="ld_v%d" % b)
            if d == 1:
                S.op("sp", lambda e: e.dma_start(out=gT, in_=gTs[t].rearrange("c p n -> p c n")), writes=["gT"], dma="ld_g")
                S.op("pool", lambda e: e.tensor_tensor(out=gT, in0=gT, in1=gng_sb.unsqueeze(2).to_broadcast([128, 16, 512]), op=ALU.mult), reads=["gT", "gng"], writes=["gT"])
            if t < 8:
                seqs = [([0, 1, 2, 3] if d == 0 else [3, 2, 1, 0], None)]
            else:
                seqs = [([0, 1] if d == 0 else [1, 0], 0), ([2, 3] if d == 0 else [3, 2], 1)]
            def chunk(order, pseq, ci_, s_):
                if True:
                    first = (t == (0 if d == 0 else 7) and ci_ == 0) if t < 8 else (ci_ == 0)
                    has_state = True if t < 8 else (ci_ > 0)
                    last_sample = (t == (7 if d == 0 else 0)) and ci_ == len(order) - 1 and t < 8
                    if t < 8 and first:
                        S.op("sp", lambda e: e.dma_start(out=S32, in_=s0[d].rearrange("h (c p) e -> p h c e", p=128)), writes=[("S32", h, c) for h in range(4) for c in range(2)], dma="ld_s0")
                        nb = Sbf[sidx[0] % 2]
                        for h in range(4):
                            S.op("act", lambda e, h=h, nb=nb: e.copy(out=nb[:, h], in_=S32[:, h]), reads=[("S32", h, 0), ("S32", h, 1)], writes=[("Sbf", sidx[0] % 2, h)])
                    curS = Sbf[sidx[0] % 2]
                    curk = sidx[0] % 2
                    nxtS = Sbf[(sidx[0] + 1) % 2]
                    nxtk = (sidx[0] + 1) % 2
                    sidx[0] += 1
                    cols = slice(s_ * 128, (s_ + 1) * 128)
                    per_h = []
                    for h in range(4):
                        a = 4 * d + h
                        pb, pk = bank()
                        for dc in range(2):
                            S.op("pe", lambda e, pb=pb, h=h, dc=dc: e.matmul(pb[:, 0:128], lhsT=kT[:, 2 * h + dc, cols], rhs=qT[:, 2 * h + dc, cols], start=(dc == 0), stop=(dc == 1)),
                                 reads=["kT%d" % b, "qT%d" % b], writes=[pk])
                        am, amk = attm_rot.next()
                        S.op("dve", lambda e, pb=pb, am=am, a=a: e.tensor_tensor(out=am, in0=pb[:, 0:128], in1=decT[:, a, :], op=ALU.mult), reads=[pk, ("decT", a)], writes=[amk])
                        qs, qsk = qs_rot.next()
                        if has_state:
                            S.op("pool", lambda e, qs=qs, h=h, a=a: e.tensor_tensor(out=qs, in0=qT[:, 2 * h:2 * h + 2, cols], in1=qdec[:, a, :].unsqueeze(1).to_broadcast([128, 2, 128]), op=ALU.mult),
                                 reads=["qT%d" % b, ("qdec", a)], writes=[qsk])
                        kf, kfk = kf_rot.next()
                        if not last_sample:
                            S.op("act", lambda e, kf=kf, h=h, a=a: e.activation(out=kf, in_=kt[:, s_, h * 256:(h + 1) * 256], func=AF.Identity, scale=kdec[:, a:a + 1]),
                                 reads=["kt%d" % b, ("kdec", a)], writes=[kfk])
                        per_h.append((am, amk, qs, qsk, kf, kfk))
                    if d == 0:
                        ost, ostk = of_rot.next()
                    else:
                        ofl, oflk = of_rot.next()
                        S.op("sp", lambda e, ofl=ofl: e.dma_start(out=ofl, in_=ofs[t][s_]), writes=[oflk], dma="ld_" + oflk)
                    for h in range(4):
                        am, amk, qs, qsk, kf, kfk = per_h[h]
                        po, pok = bank()
                        for ec in range(4):
                            S.op("pe", lambda e, po=po, ec=ec, h=h, am=am: e.matmul(po[:, ec * 128:(ec + 1) * 128], lhsT=v[:, s_, h * 512 + ec * 128:h * 512 + (ec + 1) * 128], rhs=am,
                                                                                start=True, stop=(not has_state)),
                                 reads=["v%d" % b, amk], writes=[pok])
                            if has_state:
                                for dc in range(2):
                                    S.op("pe", lambda e, po=po, ec=ec, h=h, dc=dc, qs=qs: e.matmul(po[:, ec * 128:(ec + 1) * 128], lhsT=curS[:, h, dc, ec * 128:(ec + 1) * 128], rhs=qs[:, dc, :],
                                                                                              start=False, stop=(dc == 1)),
                                         reads=[("Sbf", curk, h), qsk], writes=[pok])
                        pov = po.rearrange("p (c n) -> p c n", c=4)
                        if d == 0:
                            S.op("act", lambda e, pov=pov, ost=ost, h=h: e.copy(out=ost[:, 4 * h:4 * h + 4, :], in_=pov), reads=[pok], writes=[(ostk, h)])
                        else:
                            S.op("dve", lambda e, pov=pov, ofl=ofl, h=h: e.tensor_tensor(out=osum[:, 4 * h:4 * h + 4, :], in0=pov, in1=ofl[:, 4 * h:4 * h + 4, :], op=ALU.add),
                                 reads=[pok, oflk], writes=[("osum", h)])
                    if d == 0:
                        S.op("act", lambda e, ost=ost: e.dma_start(out=ofs[t][s_], in_=ost), reads=[(ostk, h) for h in range(4)], writes=[("ofs", t, s_)], dma="st_" + ostk)
                    if not last_sample:
                        for h in range(4):
                            a = 4 * d + h
                            am, amk, qs, qsk, kf, kfk = per_h[h]
                            for dc in range(2):
                                pb, pk = bank()
                                S.op("pe", lambda e, pb=pb, kf=kf, h=h, dc=dc: e.matmul(pb, lhsT=kf[:, dc * 128:(dc + 1) * 128], rhs=v[:, s_, h * 512:(h + 1) * 512], start=True, stop=True),
                                     reads=[kfk, "v%d" % b], writes=[pk])
                                if has_state:
                                    S.op("dve", lambda e, pb=pb, h=h, dc=dc, a=a: e.scalar_tensor_tensor(out=S32[:, h, dc, :], in0=S32[:, h, dc, :], scalar=cdec[:, a:a + 1], in1=pb, op0=ALU.mult, op1=ALU.add),
                                         reads=[pk, ("S32", h, dc), ("cdec", a)], writes=[("S32", h, dc)])
                                else:
                                    S.op("dve", lambda e, pb=pb, h=h, dc=dc: e.tensor_copy(out=S32[:, h, dc, :], in_=pb), reads=[pk], writes=[("S32", h, dc)])
                            S.op("act", lambda e, h=h, nxtS=nxtS: e.copy(out=nxtS[:, h], in_=S32[:, h]), reads=[("S32", h, 0), ("S32", h, 1)], writes=[("Sbf", nxtk, h)])
                    if t == 8 and ci_ == len(order) - 1:
                        S.op("pool", lambda e, pseq=pseq: e.dma_start(out=ns[d][pseq].rearrange("h (c p) e -> p h c e", p=128), in_=S32),
                             reads=[("S32", h, c) for h in range(4) for c in range(2)], writes=[("ns", d, pseq)], dma="st_ns")
                    if d == 1:
                        ust, ustk = u_rot.next()
                        sqs = []
                        for h in range(4):
                            sqh, sqk = sq4[h], "gsq%d" % h
                            S.op("act", lambda e, h=h, sqh=sqh: e.activation(out=sqh, in_=osum[:, 4 * h:4 * h + 4, :], func=AF.Square), reads=[("osum", h)], writes=[sqk])
                            sqs.append((sqh, sqk))
                        pbs = []
                        for h in range(4):
                            sqh, sqk = sqs[h]
                            pb, pk = bank()
                            for ec in range(4):
                                S.op("pe", lambda e, pb=pb, ec=ec, sqh=sqh: e.matmul(pb[:, 0:128], lhsT=ones_bf, rhs=sqh[:, ec, :], start=(ec == 0), stop=(ec == 3)), reads=[sqk, "mats"], writes=[pk])
                            pbs.append((pb, pk))
                        rss = []
                        for h in range(4):
                            pb, pk = pbs[h]
                            rs, rsk = rs_rot.next()
                            S.op("act", lambda e, pb=pb, rs=rs: e.activation(out=rs, in_=pb[:, 0:128], func=AF.Ln, scale=1.0 / 512, bias=EPS), reads=[pk], writes=[rsk])
                            S.op("act", lambda e, rs=rs: e.activation(out=rs, in_=rs, func=AF.Exp, scale=-0.5), reads=[rsk], writes=[rsk])
                            rss.append((rs, rsk))
                        for h in range(4):
                            rs, rsk = rss[h]
                            tp, tpk = tmp_rot.next()
                            S.op("dve", lambda e, tp=tp, rs=rs, h=h: e.tensor_tensor(out=tp, in0=osum[:, 4 * h:4 * h + 4, :], in1=rs.unsqueeze(1).to_broadcast([128, 4, 128]), op=ALU.mult),
                                 reads=[("osum", h), rsk], writes=[tpk])
                            S.op("dve", lambda e, tp=tp, ust=ust, h=h: e.tensor_tensor(out=ust[:, 4 * h:4 * h + 4, :], in0=tp, in1=gT[:, 4 * h:4 * h + 4, cols], op=ALU.mult),
                                 reads=[tpk, "gT"], writes=[(ustk, h)])
                        S.op("act", lambda e, ust=ust: e.dma_start(out=uTs[t][s_], in_=ust), reads=[(ustk, h) for h in range(4)], writes=[("uTs", t, s_)], dma="st_" + ustk)
            for order, pseq in seqs:
                for ci_, s_ in enumerate(order):
                    chunk(order, pseq, ci_, s_)

        order_t = list(range(8)) if d == 0 else list(range(7, -1, -1))
        for n, t in enumerate(order_t + [8]):
            scan_tile(t, n)
        S.barrier()

    def phase_B2c():
        Arena.top = PERSIST_TOP
        wo = abf(16 * 1024).rearrange("p (k n) -> p k n", k=16)
        uT_b = [abf(16 * 512).rearrange("p (c n) -> p c n", c=16) for _ in range(2)]
        xt_b = [af32(4096).rearrange("p (c n) -> p c n", c=8) for _ in range(2)]
        hout = abf(4096).rearrange("p (c n) -> p c n", c=8)
        sqrot = Rot([abf(512) for _ in range(2)], "sq")
        rstd = af32(512)
        tmprot = Rot([af32(512) for _ in range(2)], "tmp")
        load_w(wo, None, 16, "wo1", pc="w_out1")

        def c_tile(t):
            ci = tile_cond(t)
            b = t % 2
            uT, xt_ = uT_b[b], xt_b[b]
            xtk = ["xt%d_%d" % (b, c) for c in range(8)]
            for s_ in range(4):
                S.op("sp", lambda e, s_=s_: e.dma_start(out=uT[:, :, s_ * 128:(s_ + 1) * 128], in_=uTs[t][s_]), writes=[("uT", b, s_)], dma="ld_u%d" % b)
            S.op("sp", lambda e: e.dma_start(out=xt_, in_=xT[2][t]), writes=xtk, dma="ld_xt%d" % b)
            for m in range(8):
                pb, pk = bank()
                for ch in range(16):
                    S.op("pe", lambda e, ch=ch, m=m, pb=pb: e.matmul(pb, lhsT=wo[:, ch, m * 128:(m + 1) * 128], rhs=uT[:, ch, :], start=(ch == 0), stop=(ch == 15)),
                         reads=[("uT", b, s_) for s_ in range(4)] + ["wo1"], writes=[pk])
                S.op("dve", lambda e, m=m, pb=pb: e.scalar_tensor_tensor(out=xt_[:, m, :], in0=pb, scalar=modv[:, 1, 2, m, ci:ci + 1], in1=xt_[:, m, :], op0=ALU.mult, op1=ALU.add),
                     reads=[pk, xtk[m], "modv"], writes=[xtk[m]])
            S.op("pool", lambda e: e.dma_start(out=xT[3][t], in_=xt_), reads=xtk, writes=[("xT3", t)], dma="st_xt%d" % b)
            xch = [xt_[:, c, :] for c in range(8)]
            rms_rstd(xch, xtk, 8, 1.0 / D, sqrot, rstd, "rstd")
            hok = ["ho%d" % c for c in range(8)]
            modulate(xch, xtk, rstd, "rstd", 1, 3, ci, tmprot, [hout[:, c, :] for c in range(8)], hok)
            S.op("pool", lambda e: e.dma_start(out=hTs[2][t], in_=hout), reads=hok, writes=[("hTs2", t)], dma="st_ho")
        for t in range(NT):
            c_tile(t)
        S.barrier()

    allp = [("mod", phase_mod), ("A1", phase_A1), ("A2", phase_A2), ("A3", lambda: phase_mlp(0)), ("B1", phase_B1), ("B2f", lambda: phase_scan(0)),
            ("B2b", lambda: phase_scan(1)), ("B2c", phase_B2c), ("B3", lambda: phase_mlp(1))]
    for nm, fnp in allp:
        if phases is None or nm in phases:
            fnp()
    S.emit()
    return nc, es


def _prep_inputs(inp, b, consts):
    f = lambda a: np.ascontiguousarray(np.asarray(a, np.float32))
    m = {}
    m["xin"] = f(np.concatenate([inp["x_sample"][b], inp["x_prompt"][2 * b].reshape(256, D), inp["x_prompt"][2 * b + 1].reshape(256, D)], 0))
    m["cond"] = f(np.stack([_fm(inp["c"][b], 8), _fm(inp["c_ctx"], 8)], -1))
    m["kctx"] = f(np.stack([inp["cache_l0_attn_k"][b].reshape(512, 128), inp["cache_l0_swa_k"][b].reshape(512, 128)]))
    m["vctx"] = f(np.stack([inp["cache_l0_attn_v"][b].reshape(512, 128), inp["cache_l0_swa_v"][b].reshape(512, 128)]))
    m["s0"] = f(np.stack([inp["state_l1_ret_fwd"][b], inp["state_l1_ret_bwd"][b]]))
    m["adaw0"] = f(inp["l0_ada_w"])
    m["adaw1"] = f(inp["l1_ada_w"])
    m["adab"] = f(np.stack([_fm(inp["l0_ada_b"], 48), _fm(inp["l1_ada_b"], 48)], 1))
    m["gains"] = f(np.stack([_fm(inp[k], 8) for k in ("l0_norm_mix", "l0_norm_mlp", "l1_norm_mix", "l1_norm_mlp", "final_norm")], 1))
    m["qkg"] = f(np.stack([np.tile(inp["l0_q_norm"], 2), np.tile(inp["l0_k_norm"], 2)], -1))
    m["sinkr"] = f(np.tile(np.asarray(inp["l0_sink"])[None, :], (128, 1)))
    m["decr"] = f(np.tile(np.concatenate([inp["l1_ret_decay_fwd"], inp["l1_ret_decay_bwd"]])[None, :], (128, 1)))
    m["gng"] = f(_fm(np.asarray(inp["l1_ret_gn"]).reshape(-1), 16))
    w = np.asarray(inp["l0_w_in"], np.float32)
    cols = []
    for base in (0, 768):
        for c in range(4):
            cols += list(range(base + c * 64, base + (c + 1) * 64)) + list(range(base + (4 + c) * 64, base + (5 + c) * 64))
    cols += list(range(512, 640)) + list(range(1280, 1408)) + list(range(640, 768)) + list(range(1408, 1536))
    m["w_in0"] = f(w[:, cols])
    rows = []
    for base in (0, 512):
        for c in range(4):
            rows += list(range(base + c * 64, base + (c + 1) * 64)) + list(range(base + (4 + c) * 64, base + (5 + c) * 64))
    m["w_out0"] = f(np.asarray(inp["l0_w_out"], np.float32)[rows, :])
    m["w1_0"] = f(inp["l0_mlp_w1"])
    m["w2_0"] = f(inp["l0_mlp_w2"])
    m["w1_1"] = f(inp["l1_mlp_w1"])
    m["w2_1"] = f(inp["l1_mlp_w2"])
    m["w_in1"] = f(inp["l1_w_in"])
    m["w_out1"] = f(inp["l1_w_out"])
    m.update(consts)
    return m


_CACHE = {}


def kernel(**inputs):
    inp = {k: np.asarray(v) for k, v in inputs.items()}
    consts = _consts()
    if "nc" not in _CACHE:
        _CACHE["nc"] = build()
    nc, _es = _CACHE["nc"]
    in_maps = [_prep_inputs(inp, b, consts) for b in range(8)]
    r = run_bass_kernel_spmd(nc, in_maps, core_ids=list(range(8)))
    outs = r.results
    y_prompt = np.zeros((16, 256, D), np.float32)
    y_sample = np.zeros((8, 4096, D), np.float32)
    nk = [np.zeros((16, 256, 2, 64), np.float32) for _ in range(2)]
    nv = [np.zeros((16, 256, 2, 64), np.float32) for _ in range(2)]
    ns = [np.zeros((16, 4, 256, 512), np.float32) for _ in range(2)]
    for b in range(8):
        o = outs[b]
        y_sample[b] = o["y_out"][:4096]
        y_prompt[2 * b] = o["y_out"][4096:4352]
        y_prompt[2 * b + 1] = o["y_out"][4352:4608]
        for a in range(2):
            nk[a][2 * b:2 * b + 2] = o["nk"][a].reshape(2, 256, 2, 64)
            nv[a][2 * b:2 * b + 2] = o["nv"][a].reshape(2, 256, 2, 64)
            ns[a][2 * b:2 * b + 2] = o["ns"][a]
    return (y_prompt, y_sample, nk[0], nv[0], nk[1], nv[1], ns[0], ns[1])
```

```python
import numpy as np
from contextlib import ExitStack
import concourse.bass as bass
import concourse.mybir as mybir
from concourse.bass_utils import run_bass_kernel_spmd

F32 = mybir.dt.float32
BF16 = mybir.dt.bfloat16
AF = mybir.ActivationFunctionType
ALU = mybir.AluOpType

D = 1024
NT = 9
TT = 512
EPS = 1e-6
SBW = 53000


class Op:
    __slots__ = ("eng", "fn", "deps", "dma_key", "signal", "sigval", "idx", "line")


class Sched:
    ENGS = ("pe", "act", "dve", "pool", "sp")

    def __init__(self, nc, es):
        self.nc = nc
        self.es = es
        self.ops = {e: [] for e in self.ENGS}
        self.lastw = {}
        self.readers = {}
        self.sem = {e: es.enter_context(nc.semaphore("d_" + e)) for e in self.ENGS}
        self.dsem = {}
        self.dcount = {}
        self.all_ops = []
        self.barrier_ops = None

    def op(self, eng, fn, reads=(), writes=(), dma=None):
        o = Op()
        import sys as _sys
        o.line = _sys._getframe(1).f_lineno
        o.eng = eng
        o.fn = fn
        o.dma_key = dma
        o.signal = False
        o.sigval = None
        writes = list(writes) + [r for r in reads if isinstance(r, str) and r.startswith("PS")]
        reads = [r for r in reads if not (isinstance(r, str) and r.startswith("PS"))]
        deps = set()
        for r in reads:
            w = self.lastw.get(r)
            if w is not None:
                deps.add(w)
        for w_ in writes:
            w = self.lastw.get(w_)
            if w is not None:
                deps.add(w)
            for rd in self.readers.get(w_, ()):
                deps.add(rd)
        if self.barrier_ops:
            deps.update(self.barrier_ops)
        deps.discard(o)
        o.deps = deps
        o.idx = len(self.ops[eng])
        self.ops[eng].append(o)
        self.all_ops.append(o)
        for r in reads:
            self.readers.setdefault(r, []).append(o)
        for w_ in writes:
            self.lastw[w_] = o
            self.readers[w_] = []
        if dma is not None and dma not in self.dsem:
            self.dsem[dma] = self.es.enter_context(self.nc.semaphore("q_" + dma))
            self.dcount[dma] = 0
        return o

    def barrier(self):
        print("BARRIER", {e: len(self.ops[e]) for e in self.ENGS}, "ARENA", getattr(self, "arena_top", None))
        last = set()
        for e in self.ENGS:
            if self.ops[e]:
                last.add(self.ops[e][-1])
        lastdma = {}
        for o in self.all_ops:
            if o.dma_key is not None:
                lastdma[o.dma_key] = o
        last.update(lastdma.values())
        self.barrier_ops = last
        self.lastw = {}
        self.readers = {}

    def emit(self):
        import os
        lim = int(os.environ.get("SCHED_LIMIT", "0"))
        if lim:
            keep = set(id(o) for o in self.all_ops[:lim])
            self.all_ops = self.all_ops[:lim]
            for e in self.ENGS:
                self.ops[e] = [o for o in self.ops[e] if id(o) in keep]
        print("SCHED ops:", len(self.all_ops), {e: len(self.ops[e]) for e in self.ENGS})
        if os.environ.get("SCHED_DUMP"):
            a, b = [int(x) for x in os.environ["SCHED_DUMP"].split(":")]
            for i, o in enumerate(self.all_ops[a:b]):
                print("OP", a + i, o.eng, o.line, o.dma_key)
        for o in self.all_ops:
            for d in o.deps:
                if d.eng == "pe" and o.eng == "pe" and d.dma_key is None:
                    continue
                d.signal = True
        cnt = {e: 0 for e in self.ENGS}
        for e in self.ENGS:
            for o in self.ops[e]:
                if o.dma_key is not None:
                    self.dcount[o.dma_key] += 16
                    o.sigval = (self.dsem[o.dma_key], self.dcount[o.dma_key])
                elif o.signal:
                    cnt[e] += 1
                    o.sigval = (self.sem[e], cnt[e])
        engobj = {"pe": None, "act": None, "dve": None, "pool": None, "sp": None}
        block = self.es.enter_context(self.nc.Block())

        def run(ename):
            def body(eng):
                waited = {}
                for o in self.ops[ename]:
                    need = {}
                    for d in o.deps:
                        if d.eng == "pe" and ename == "pe" and d.dma_key is None:
                            continue
                        s, v = d.sigval
                        k = id(s)
                        if waited.get(k, 0) >= v:
                            continue
                        if k not in need or need[k][1] < v:
                            need[k] = (s, v)
                    for k, (s, v) in need.items():
                        eng.wait_ge(s, v)
                        waited[k] = v
                    ins = o.fn(eng)
                    if o.dma_key is not None:
                        ins.then_inc(o.sigval[0], 16)
                    elif o.signal:
                        ins.then_inc(o.sigval[0], 1)
                if ename == "sp":
                    for key, s in self.dsem.items():
                        if self.dcount[key] > 0:
                            eng.wait_ge(s, self.dcount[key])
            return body

        block.tensor(run("pe"))
        block.scalar(run("act"))
        block.vector(run("dve"))
        block.gpsimd(run("pool"))
        block.sync(run("sp"))


def _consts():
    c = {}
    c["c_ident"] = np.eye(128, dtype=np.float32)
    mats = np.zeros((4, 128, 128), np.float32)
    mats[0] = 1.0
    mats[1, :64, :64] = 1.0
    mats[1, 64:, 64:] = 1.0
    for hb in (0, 64):
        for off in (0, 32):
            for i in range(16):
                mats[2, hb + off + 16 + i, hb + off + i] = -1.0
                mats[2, hb + off + i, hb + off + 16 + i] = 1.0
    for i in range(64):
        mats[3, 64 + i, i] = -1.0
        mats[3, i, 64 + i] = 1.0
    c["c_mats"] = mats
    t = np.arange(4096)
    row = (t // 64).astype(np.float32)
    col = (t % 64).astype(np.float32)
    inv0 = (10000.0 ** (-np.arange(16, dtype=np.float32) / 16)).astype(np.float32)
    ang = np.zeros((64, 4096), np.float32)
    for d in range(64):
        pos = row if d < 32 else col
        ang[d] = pos * inv0[d % 16]
    ang = np.concatenate([ang, ang], 0)
    c["c_cs0"] = np.stack([np.cos(ang), np.sin(ang)]).astype(np.float32)
    inv1 = (10000.0 ** (-np.arange(64, dtype=np.float32) / 64)).astype(np.float32)
    ang1 = np.zeros((128, 2, 4096), np.float32)
    for p in range(128):
        ang1[p, 0] = row * inv1[p % 64]
        ang1[p, 1] = col * inv1[p % 64]
    c["c_cs1"] = np.stack([np.cos(ang1), np.sin(ang1)]).astype(np.float32)
    kp = np.arange(128)[:, None]
    qf = np.arange(128)[None, :]
    m1 = (qf <= kp).astype(np.float32)
    m2 = (kp <= qf).astype(np.float32)
    c["c_mask"] = np.stack([np.tile(m1, (1, 4)), np.tile(m2, (1, 4))]).astype(np.float32)
    j = np.arange(128, dtype=np.float32)[:, None]
    i = np.arange(128, dtype=np.float32)[None, :]
    ret = np.zeros((128, 6 * 128 + 2), np.float32)
    ret[:, 0:128] = np.maximum(i - j, 0)
    ret[:, 128:256] = np.maximum(j - i, 0)
    ret[:, 256:384] = (i + 1) + 0 * j
    ret[:, 384:512] = (128 - i) + 0 * j
    ret[:, 512:640] = (i >= j)
    ret[:, 640:768] = (j >= i)
    ret[:, 768] = 127 - np.arange(128)
    ret[:, 769] = np.arange(128)
    c["c_ret"] = ret
    return c


def _fm(v, k):
    return np.ascontiguousarray(np.asarray(v, np.float32).reshape(k, 128).T)


def build(debug=False, phases=None):
    nc = bass.Bass("TRN2", target_bir_lowering=False)
    es = ExitStack()

    def din(name, shape, dt=F32):
        return nc.dram_tensor(name, list(shape), dt, kind="ExternalInput").ap()

    def dout(name, shape, dt=F32):
        return nc.dram_tensor(name, list(shape), dt, kind="ExternalOutput").ap()

    def dscr(name, shape, dt=F32):
        kind = "ExternalOutput" if debug else "Internal"
        return nc.dram_tensor(name, list(shape), dt, kind=kind).ap()

    xin = din("xin", [NT * TT, D])
    cond = din("cond", [128, 8, 2])
    kctx = din("kctx", [2, 512, 128])
    vctx = din("vctx", [2, 512, 128])
    s0 = din("s0", [2, 4, 256, 512])
    adaw = [din("adaw0", [D, 6 * D]), din("adaw1", [D, 6 * D])]
    adab = din("adab", [128, 2, 48])
    gains = din("gains", [128, 5, 8])
    qkg = din("qkg", [128, 2])
    sinkr = din("sinkr", [128, 8])
    decr = din("decr", [128, 8])
    gng = din("gng", [128, 16])
    w_in0 = din("w_in0", [D, 1536])
    w_out0 = din("w_out0", [D, D])
    w1 = [din("w1_0", [D, 4 * D]), din("w1_1", [D, 4 * D])]
    w2 = [din("w2_0", [4 * D, D]), din("w2_1", [4 * D, D])]
    w_in1 = din("w_in1", [D, 6 * D])
    w_out1 = din("w_out1", [2 * D, D])
    c_ident = din("c_ident", [128, 128])
    c_mats = din("c_mats", [4, 128, 128])
    c_cs0 = din("c_cs0", [2, 128, 4096])
    c_cs1 = din("c_cs1", [2, 128, 2, 4096])
    c_mask = din("c_mask", [2, 128, 512])
    c_ret = din("c_ret", [128, 770])

    y_out = dout("y_out", [NT * TT, D])
    nk = dout("nk", [2, 512, 128])
    nv = dout("nv", [2, 512, 128])
    ns = dout("ns", [2, 2, 4, 256, 512])

    xT = [dscr("xT%d" % i, [NT, 128, 8, TT]) for i in range(4)]
    hTs = [dscr("hT%d" % i, [NT, 128, 8, TT], BF16) for i in range(3)]
    qTs = dscr("qTs", [NT, 8, 128, TT], BF16)
    kTs = dscr("kTs", [NT, 8, 128, TT], BF16)
    kts = dscr("kts", [NT, 128, 4, 1024], BF16)
    vs_ = dscr("vs", [NT, 4, 128, 2048], BF16)
    gTs = dscr("gTs", [NT, 16, 128, TT], BF16)
    ofs = dscr("ofs", [NT, 4, 128, 16, 128])
    uTs = dscr("uTs", [NT, 4, 128, 16, 128], BF16)

    wbf = {"w1_0": dscr("w1_0_bf", [D, 4 * D], BF16), "w2_0": dscr("w2_0_bf", [4 * D, D], BF16), "w_in1": dscr("w_in1_bf", [D, 6 * D], BF16),
           "w_out1": dscr("w_out1_bf", [2 * D, D], BF16), "w1_1": dscr("w1_1_bf", [D, 4 * D], BF16), "w2_1": dscr("w2_1_bf", [4 * D, D], BF16),
           "wq": dscr("wq_bf", [D, D], BF16), "wo": dscr("wo_bf", [D, D], BF16)}
    wsrc = {"w1_0": w1[0], "w2_0": w2[0], "w_in1": w_in1, "w_out1": w_out1, "w1_1": w1[1], "w2_1": w2[1], "wq": w_in0[:, 0:1024], "wo": w_out0}

    big = es.enter_context(nc.sbuf_tensor("big", [128, SBW], F32))
    PP = [es.enter_context(nc.psum_tensor("pp%d" % i, [128, 1024], F32))[:] for i in range(4)]
    PS = [PP[i // 2][:, (i % 2) * 512:(i % 2) * 512 + 512] for i in range(8)]
    S = Sched(nc, es)

    class Arena:
        top = 0

    def af32(n):
        a = big[:, Arena.top:Arena.top + n]
        Arena.top += n
        S.arena_top = Arena.top
        assert Arena.top <= SBW, Arena.top
        return a

    def abf(n):
        w = (n + 1) // 2
        a = big[:, Arena.top:Arena.top + w].bitcast(BF16)
        Arena.top += w
        S.arena_top = Arena.top
        assert Arena.top <= SBW, Arena.top
        return a

    uid = [0]

    def rk(prefix="r"):
        uid[0] += 1
        return "%s%d" % (prefix, uid[0])

    class Rot:
        def __init__(self, aps, name, keys=None):
            self.aps = aps
            self.keys = keys if keys is not None else [name + str(i) for i in range(len(aps))]
            self.i = 0

        def next(self):
            k = self.i % len(self.aps)
            self.i += 1
            return self.aps[k], self.keys[k]

    psrot = Rot([p for p in PS], "PS")
    cur = {"rot": psrot}

    def bank():
        return cur["rot"].next()

    ident = af32(128)
    mats_bf = abf(4 * 128).rearrange("p (m n) -> p m n", m=4)
    ones_bf, bones_bf, rot0_bf, rot1_bf = (mats_bf[:, i, :] for i in range(4))
    modv = af32(2 * 6 * 16).rearrange("p (l s k c) -> p l s k c", l=2, s=6, k=8)
    gains_sb = af32(40).rearrange("p (a k) -> p a k", a=5)
    qkg_sb = af32(2)
    esink = af32(8)
    gng_sb = af32(16)
    PERSIST_TOP = Arena.top

    S.op("sp", lambda e: e.dma_start(out=ident, in_=c_ident), writes=["ident"], dma="c0")
    S.op("pool", lambda e: e.dma_start(out=mats_bf, in_=c_mats.rearrange("m p n -> p m n")), writes=["mats"], dma="c1")
    S.op("sp", lambda e: e.dma_start(out=gains_sb, in_=gains), writes=["gains"], dma="c0")
    S.op("sp", lambda e: e.dma_start(out=qkg_sb, in_=qkg), writes=["qkg"], dma="c0")
    S.op("sp", lambda e: e.dma_start(out=esink, in_=sinkr), writes=["esink"], dma="c0")
    S.op("sp", lambda e: e.dma_start(out=gng_sb, in_=gng), writes=["gng"], dma="c0")
    S.op("act", lambda e: e.activation(out=esink, in_=esink, func=AF.Exp), reads=["esink"], writes=["esink"])

    def phase_mod():
        Arena.top = PERSIST_TOP
        cnd = af32(16).rearrange("p (k c) -> p k c", k=8)
        scn = af32(16).rearrange("p (k c) -> p k c", k=8)
        tmp = af32(16).rearrange("p (k c) -> p k c", k=8)
        adab_sb = af32(96).rearrange("p (l j) -> p l j", l=2)
        acc = af32(2 * 96).rearrange("p (l j c) -> p l j c", l=2, j=48)
        wb = [af32(4096).rearrange("p (k n) -> p k n", k=8) for _ in range(2)]
        modrow = af32(6144)
        S.op("sp", lambda e: e.dma_start(out=cnd, in_=cond), writes=["cnd"], dma="c0")
        S.op("sp", lambda e: e.dma_start(out=adab_sb, in_=adab), writes=["adab"], dma="c0")
        S.op("act", lambda e: e.activation(out=tmp, in_=cnd, func=AF.Exp, scale=-1.0), reads=["cnd"], writes=["mtmp"])
        S.op("dve", lambda e: e.tensor_scalar(out=tmp, in0=tmp, scalar1=1.0, scalar2=None, op0=ALU.add), reads=["mtmp"], writes=["mtmp"])
        S.op("dve", lambda e: e.reciprocal(out=tmp, in_=tmp), reads=["mtmp"], writes=["mtmp"])
        S.op("dve", lambda e: e.tensor_tensor(out=scn, in0=cnd, in1=tmp, op=ALU.mult), reads=["mtmp", "cnd"], writes=["scn"])
        n = 0
        for l in range(2):
            wv = adaw[l].rearrange("(k p) n -> p k n", p=128)
            for cg in range(12):
                w_ap, wkey = wb[n % 2], "adw%d" % (n % 2)
                q_ = ("sp", "act")[n % 2]
                n += 1
                S.op(q_, lambda e, w_ap=w_ap, cg=cg, wv=wv: e.dma_start(out=w_ap, in_=wv[:, :, cg * 512:(cg + 1) * 512]), writes=[wkey], dma=wkey)
                pb, pk = bank()
                for k in range(8):
                    S.op("pe", lambda e, pb=pb, w_ap=w_ap, k=k: e.matmul(pb[0:2, :], lhsT=scn[:, k, :], rhs=w_ap[:, k, :], start=(k == 0), stop=(k == 7)),
                         reads=[wkey, "scn"], writes=[pk])
                S.op("dve", lambda e, pb=pb, cg=cg: e.tensor_copy(out=modrow[0:2, cg * 512:(cg + 1) * 512], in_=pb[0:2, :]), reads=[pk], writes=[("modrow", cg)])
            pb, pk = bank()
            for j in range(48):
                S.op("pe", lambda e, pb=pb, j=j: e.transpose(pb[:, 2 * j:2 * j + 2], modrow[0:2, j * 128:(j + 1) * 128], ident[0:2, 0:2]),
                     reads=[("modrow", j // 4), "ident"], writes=[pk])
            S.op("dve", lambda e, pb=pb, l=l: e.tensor_copy(out=acc[:, l], in_=pb[:, 0:96].rearrange("p (j c) -> p j c", j=48)), reads=[pk], writes=["acc%d" % l])
            S.op("dve", lambda e, l=l: e.tensor_tensor(out=acc[:, l], in0=acc[:, l],
                                                       in1=adab_sb[:, l, :].unsqueeze(2).to_broadcast([128, 48, 2]), op=ALU.add),
                 reads=["acc%d" % l, "adab"], writes=["acc%d" % l])
            a6 = acc[:, l].rearrange("p (s k) c -> p s k c", s=6)
            for s_i in (0, 2, 3, 5):
                S.op("dve", lambda e, l=l, s_i=s_i, a6=a6: e.tensor_copy(out=modv[:, l, s_i], in_=a6[:, s_i]),
                     reads=["acc%d" % l], writes=["modv"])
            for s_i, g_i in ((1, 2 * l), (4, 2 * l + 1)):
                S.op("dve", lambda e, l=l, s_i=s_i, a6=a6: e.tensor_scalar(out=modv[:, l, s_i], in0=a6[:, s_i], scalar1=1.0, scalar2=None, op0=ALU.add),
                     reads=["acc%d" % l], writes=["modv"])
                S.op("dve", lambda e, l=l, s_i=s_i, g_i=g_i: e.tensor_tensor(out=modv[:, l, s_i], in0=modv[:, l, s_i],
                                                                            in1=gains_sb[:, g_i, :].unsqueeze(2).to_broadcast([128, 8, 2]), op=ALU.mult),
                     reads=["modv", "gains"], writes=["modv"])
        S.barrier()

    def precast_all(names=("w1_0", "w2_0", "w_in1", "w_out1", "w1_1", "w2_1")):
        for name in names:
            src = wsrc[name]
            for k in range(src.shape[0] // 128):
                S.op("pool", lambda e, name=name, src=src, k=k: e.dma_start(out=wbf[name][k * 128:(k + 1) * 128, :], in_=src[k * 128:(k + 1) * 128, :]),
                     writes=[("pc", name, k)], dma="pc_" + name)

    def load_w(dst, src_rows, k_chunks, key, col_groups=None, per_k=False, pc=None):
        if pc is not None:
            eng_, src_rows, rd = "sp", wbf[pc], (lambda k: [("pc", pc, k)])
        else:
            eng_, rd = "pool", (lambda k: [])
        return _load_w(dst, src_rows, k_chunks, key, col_groups, per_k, eng_, rd)

    hwq = [0]

    def _load_w(dst, src_rows, k_chunks, key, col_groups, per_k, eng_, rd):
        if eng_ == "sp":
            srcv = src_rows.rearrange("(k p) n -> p k n", p=128)
            if col_groups is not None:
                for gi_, (c0, c1) in enumerate(col_groups):
                    q_ = ("sp", "act")[hwq[0] % 2]
                    hwq[0] += 1
                    S.op(q_, lambda e, c0=c0, c1=c1: e.dma_start(out=dst[:, :, c0:c1], in_=srcv[:, :, c0:c1]),
                         reads=[r_ for k in range(k_chunks) for r_ in rd(k)], writes=[(key, gi_)], dma="%s_g%d%s" % (key, gi_, q_))
            else:
                for g0 in range(0, k_chunks, 8):
                    q_ = ("sp", "act")[hwq[0] % 2]
                    hwq[0] += 1
                    S.op(q_, lambda e, g0=g0: e.dma_start(out=dst[:, g0:g0 + 8, :], in_=srcv[:, g0:g0 + 8, :]),
                         reads=[r_ for k in range(g0, g0 + 8) for r_ in rd(k)], writes=[(key, g0 // 8) if per_k else key], dma="%s_k%d%s" % (key, g0 // 8, q_))
            return
        if col_groups is not None:
            for gi_, (c0, c1) in enumerate(col_groups):
                for k in range(k_chunks):
                    S.op(eng_, lambda e, k=k, c0=c0, c1=c1: e.dma_start(out=dst[:, k, c0:c1], in_=src_rows[k * 128:(k + 1) * 128, c0:c1]),
                         reads=rd(k), writes=[(key, gi_)], dma="%s_g%d" % (key, gi_))
            return
        for k in range(k_chunks):
            S.op(eng_, lambda e, k=k: e.dma_start(out=dst[:, k, :], in_=src_rows[k * 128:(k + 1) * 128, :]),
                 reads=rd(k), writes=[(key, k // 8) if per_k else key], dma=("%s_k%d" % (key, k // 8)) if per_k else key)

    def rms_rstd(xchunks, xkeys, nchunk, inv_n, sqrot, rstd, rstd_key, onesm=None, sq_eng="act"):
        onesm = ones_bf if onesm is None else onesm
        pb, pk = bank()
        N = xchunks[0].shape[-1]
        for c in range(nchunk):
            sq, sk = sqrot.next()
            if sq_eng == "act":
                S.op("act", lambda e, sq=sq, c=c: e.activation(out=sq[:, 0:N], in_=xchunks[c], func=AF.Square), reads=[xkeys[c]], writes=[sk])
            else:
                S.op(sq_eng, lambda e, sq=sq, c=c: e.tensor_tensor(out=sq[:, 0:N], in0=xchunks[c], in1=xchunks[c], op=ALU.mult), reads=[xkeys[c]], writes=[sk])
            S.op("pe", lambda e, sq=sq, c=c, pb=pb: e.matmul(pb[:, 0:N], lhsT=onesm, rhs=sq[:, 0:N], start=(c == 0), stop=(c == nchunk - 1)),
                 reads=[sk, "mats"], writes=[pk])
        S.op("act", lambda e, pb=pb: e.activation(out=rstd[:, 0:N], in_=pb[:, 0:N], func=AF.Ln, scale=inv_n, bias=EPS), reads=[pk], writes=[rstd_key])
        S.op("act", lambda e: e.activation(out=rstd[:, 0:N], in_=rstd[:, 0:N], func=AF.Exp, scale=-0.5), reads=[rstd_key], writes=[rstd_key])

    def modulate(xchunks, xkeys, rstd, rstd_key, l, gi, ci, tmprot, outs, outkeys, after=None, chunks=None, add_eng="act"):
        for c in (range(8) if chunks is None else chunks):
            tp, tk = tmprot.next()
            if add_eng == "dve":
                S.op("dve", lambda e, c=c, tp=tp: e.tensor_tensor(out=tp, in0=xchunks[c], in1=rstd, op=ALU.mult), reads=[xkeys[c], rstd_key], writes=[tk])
                S.op("dve", lambda e, c=c, tp=tp: e.tensor_scalar(out=outs[c], in0=tp, scalar1=modv[:, l, gi + 1, c, ci:ci + 1], scalar2=modv[:, l, gi, c, ci:ci + 1],
                                                                op0=ALU.mult, op1=ALU.add),
                     reads=[tk, "modv"], writes=[outkeys[c]])
            else:
                S.op("dve", lambda e, c=c, tp=tp: e.scalar_tensor_tensor(out=tp, in0=xchunks[c], scalar=modv[:, l, gi + 1, c, ci:ci + 1], in1=rstd,
                                                                        op0=ALU.mult, op1=ALU.mult),
                     reads=[xkeys[c], rstd_key, "modv"], writes=[tk])
                S.op("act", lambda e, c=c, tp=tp: e.activation(out=outs[c], in_=tp, func=AF.Identity, bias=modv[:, l, gi, c, ci:ci + 1], scale=1.0),
                     reads=[tk, "modv"], writes=[outkeys[c]])
            if after is not None:
                after(c)

    def tile_cond(t):
        return 0 if t < 8 else 1

    res = {}

    def phase_A1():
        Arena.top = PERSIST_TOP
        KA = abf(5120)
        KS = abf(5120)
        VA = abf(40 * 192).rearrange("p (t c) -> p t c", t=40)
        VS = abf(40 * 192).rearrange("p (t c) -> p t c", t=40)
        res.update(KA=KA, KS=KS, VA=VA, VS=VS)
        res["A1_TOP"] = Arena.top
        wkv = abf(8 * 512).rearrange("p (k n) -> p k n", k=8)
        xin_b = [af32(4096).rearrange("p (s d) -> p s d", s=4) for _ in range(2)]
        xt_b = [af32(4096).rearrange("p (c n) -> p c n", c=8) for _ in range(2)]
        hT = abf(4096).rearrange("p (c n) -> p c n", c=8)
        sqrot = Rot([abf(512) for _ in range(2)], "sq")
        rstd = af32(512)
        tmprot = Rot([af32(512) for _ in range(2)], "tmp")
        kn_rot = Rot([af32(512) for _ in range(2)], "kn")
        knb_rot = Rot([abf(512) for _ in range(2)], "knb")
        t1_rot = Rot([af32(512) for _ in range(2)], "t1")
        cs_b = [af32(1024).rearrange("p (a n) -> p a n", a=2) for _ in range(2)]
        vst = af32(1024).rearrange("p (s n) -> p s n", s=4)
        kst = af32(1024).rearrange("p (a s n) -> p a s n", a=2, s=4)[:, :, :, :]
        kst = af32(1024).rearrange("p (a s n) -> p a s n", a=2, s=4)
        cst = af32(4 * 128 * 2).rearrange("p (a s n) -> p a s n", a=2, s=4)
        load_w(wkv, w_in0[:, 1024:1536], 8, "wkv")
        precast_all(("wq", "wo"))
        S.op("dve", lambda e: e.memset(VA[:, :, 64:128], 1.0), writes=["VAones"])
        S.op("dve", lambda e: e.memset(VS[:, :, 64:128], 1.0), writes=["VSones"])
        for a, (Kdst, Vdst) in enumerate(((KA, VA), (KS, VS))):
            S.op("sp", lambda e, a=a: e.dma_start(out=cst[:, 0], in_=kctx[a].rearrange("(s p) n -> p s n", p=128)), writes=["cstk"], dma="cst")
            S.op("sp", lambda e, a=a: e.dma_start(out=cst[:, 1], in_=vctx[a].rearrange("(s p) n -> p s n", p=128)), writes=["cstv"], dma="cst")
            pb, pk = bank()
            for s_ in range(4):
                S.op("pe", lambda e, s_=s_, pb=pb: e.transpose(pb[:, s_ * 128:(s_ + 1) * 128], cst[:, 0, s_, :], ident),
                     reads=["cstk", "ident"], writes=[pk])
            S.op("act", lambda e, pb=pb, Kdst=Kdst: e.copy(out=Kdst[:, 0:512], in_=pb), reads=[pk], writes=["Kctx%d" % a])
            S.op("dve", lambda e, Vdst=Vdst: e.tensor_copy(out=Vdst[:, 0:4, 0:64], in_=cst[:, 1, :, 0:64]), reads=["cstv"], writes=["Vctx%d" % a])
            S.op("dve", lambda e, Vdst=Vdst: e.tensor_copy(out=Vdst[:, 0:4, 128:192], in_=cst[:, 1, :, 64:128]), reads=["cstv"], writes=["Vctx%da" % a])
        xin_v = xin.rearrange("(t s p) d -> t p s d", s=4, p=128)
        for t in range(NT):
            ci = tile_cond(t)
            xi, xik = xin_b[t % 2], "xin%d" % (t % 2)
            xt_, xtk = xt_b[t % 2], ["xt%d_%d" % (t % 2, c) for c in range(8)]
            S.op("sp", lambda e, t=t, xi=xi: e.dma_start(out=xi, in_=xin_v[t]), writes=[xik], dma=xik)
            if t < 8:
                cs_, csk = cs_b[t % 2], "cs%d" % (t % 2)
                S.op("sp", lambda e, t=t, cs_=cs_: e.dma_start(out=cs_, in_=c_cs0[:, :, t * 512:(t + 1) * 512].rearrange("a p n -> p a n")),
                     writes=[csk], dma=csk)
            for c in range(8):
                pb, pk = bank()
                for s_ in range(4):
                    S.op("pe", lambda e, c=c, s_=s_, pb=pb, xi=xi: e.transpose(pb[:, s_ * 128:(s_ + 1) * 128], xi[:, s_, c * 128:(c + 1) * 128], ident),
                         reads=[xik, "ident"], writes=[pk])
                eng = "act" if c % 2 == 0 else "dve"
                if eng == "act":
                    S.op("act", lambda e, c=c, pb=pb, xt_=xt_: e.copy(out=xt_[:, c, :], in_=pb), reads=[pk], writes=[xtk[c]])
                else:
                    S.op("dve", lambda e, c=c, pb=pb, xt_=xt_: e.tensor_copy(out=xt_[:, c, :], in_=pb), reads=[pk], writes=[xtk[c]])
            S.op("pool", lambda e, t=t, xt_=xt_: e.dma_start(out=xT[0][t], in_=xt_), reads=xtk, writes=[("xT0", t)], dma="st_xt%d" % (t % 2))
            xch = [xt_[:, c, :] for c in range(8)]
            rms_rstd(xch, xtk, 8, 1.0 / D, sqrot, rstd, "rstd")
            hk = ["hT%d" % c for c in range(8)]
            modulate(xch, xtk, rstd, "rstd", 0, 0, ci, tmprot, [hT[:, c, :] for c in range(8)], hk)
            for a, Kdst in enumerate((KA, KS)):
                pb, pk = bank()
                for k in range(8):
                    S.op("pe", lambda e, k=k, a=a, pb=pb: e.matmul(pb, lhsT=wkv[:, k, a * 128:(a + 1) * 128], rhs=hT[:, k, :], start=(k == 0), stop=(k == 7)),
                         reads=[hk[k], "wkv"], writes=[pk])
                kn, knk = kn_rot.next()
                if a == 0:
                    S.op("act", lambda e, pb=pb, kn=kn: e.copy(out=kn, in_=pb), reads=[pk], writes=[knk])
                    rs, rsk = t1_rot.next()
                    rms_rstd([kn], [knk], 1, 1.0 / 64, sqrot, rs, rsk, onesm=bones_bf)
                    S.op("dve", lambda e, kn=kn, rs=rs: e.scalar_tensor_tensor(out=kn, in0=kn, scalar=qkg_sb[:, 1:2], in1=rs, op0=ALU.mult, op1=ALU.mult),
                         reads=[knk, rsk, "qkg"], writes=[knk])
                else:
                    S.op("act", lambda e, pb=pb, kn=kn: e.copy(out=kn, in_=pb), reads=[pk], writes=[knk])
                col0 = 512 + t * 512
                if t < 8:
                    knb, knbk = knb_rot.next()
                    S.op("act", lambda e, kn=kn, knb=knb: e.copy(out=knb, in_=kn), reads=[knk], writes=[knbk])
                    pb2, pk2 = bank()
                    S.op("pe", lambda e, pb2=pb2, knb=knb: e.matmul(pb2, lhsT=rot0_bf, rhs=knb, start=True, stop=True), reads=[knbk, "mats"], writes=[pk2])
                    t1, t1k = t1_rot.next()
                    S.op("dve", lambda e, t1=t1, pb2=pb2, cs_=cs_: e.tensor_tensor(out=t1, in0=pb2, in1=cs_[:, 1, :], op=ALU.mult), reads=[pk2, csk], writes=[t1k])
                    S.op("pool", lambda e, kn=kn, cs_=cs_: e.tensor_tensor(out=kn, in0=kn, in1=cs_[:, 0, :], op=ALU.mult), reads=[knk, csk], writes=[knk])
                    S.op("dve", lambda e, kn=kn, t1=t1, Kdst=Kdst, col0=col0: e.tensor_tensor(out=Kdst[:, col0:col0 + 512], in0=kn, in1=t1, op=ALU.add),
                         reads=[knk, t1k], writes=[("K%d" % a, t)])
                else:
                    S.op("act", lambda e, kn=kn, Kdst=Kdst, col0=col0: e.copy(out=Kdst[:, col0:col0 + 512], in_=kn), reads=[knk], writes=[("K%d" % a, t)])
                    pb2, pk2 = bank()
                    for s_ in range(4):
                        S.op("pe", lambda e, s_=s_, pb2=pb2, kn=kn: e.transpose(pb2[:, s_ * 128:(s_ + 1) * 128], kn[:, s_ * 128:(s_ + 1) * 128], ident),
                             reads=[knk, "ident"], writes=[pk2])
                    S.op("dve", lambda e, a=a, pb2=pb2: e.tensor_copy(out=kst[:, a], in_=pb2.rearrange("p (s n) -> p s n", s=4)), reads=[pk2], writes=["kst%d" % a])
                    S.op("pool", lambda e, a=a: e.dma_start(out=nk[a].rearrange("(s p) n -> p s n", p=128), in_=kst[:, a]), reads=["kst%d" % a], writes=[("nk", a)], dma="st_k%d" % a)
            for s_ in range(4):
                pb, pk = bank()
                for k in range(8):
                    S.op("pe", lambda e, k=k, s_=s_, pb=pb: e.matmul(pb[:, 0:256], lhsT=hT[:, k, s_ * 128:(s_ + 1) * 128], rhs=wkv[:, k, 256:512], start=(k == 0), stop=(k == 7)),
                         reads=[hk[k], "wkv"], writes=[pk])
                kt = 4 + t * 4 + s_
                for a, Vdst in enumerate((VA, VS)):
                    for g in range(2):
                        if (a + g) % 2 == 0:
                            S.op("act", lambda e, a=a, g=g, Vdst=Vdst, pb=pb, kt=kt: e.copy(out=Vdst[:, kt, g * 128:g * 128 + 64], in_=pb[:, a * 128 + g * 64:a * 128 + g * 64 + 64]),
                                 reads=[pk], writes=[("V%d_%d" % (a, g), kt)])
                        else:
                            S.op("dve", lambda e, a=a, g=g, Vdst=Vdst, pb=pb, kt=kt: e.tensor_copy(out=Vdst[:, kt, g * 128:g * 128 + 64], in_=pb[:, a * 128 + g * 64:a * 128 + g * 64 + 64]),
                                 reads=[pk], writes=[("V%d_%d" % (a, g), kt)])
                if t == 8:
                    S.op("dve", lambda e, s_=s_, pb=pb: e.tensor_copy(out=vst[:, s_, :], in_=pb[:, 0:256]), reads=[pk], writes=[("vst", s_)])
            if t == 8:
                for a in range(2):
                    S.op("pool", lambda e, a=a: e.dma_start(out=nv[a].rearrange("(s p) n -> p s n", p=128), in_=vst[:, :, a * 128:(a + 1) * 128]),
                         reads=[("vst", s_) for s_ in range(4)], writes=[("nv", a)], dma="st_v%d" % a)
        S.barrier()


    def phase_A2():
        Arena.top = res["A1_TOP"]
        KA, KS, VA, VS = res["KA"], res["KS"], res["VA"], res["VS"]
        wq = abf(8 * 1024).rearrange("p (k n) -> p k n", k=8)
        wo = abf(8 * 1024).rearrange("p (k n) -> p k n", k=8)
        masks = abf(2 * 512).rearrange("p (a n) -> p a n", a=2)
        xF = af32(4096).rearrange("p (c n) -> p c n", c=8)
        xB = af32(4096).rearrange("p (c n) -> p c n", c=8)
        hT = abf(4096).rearrange("p (c n) -> p c n", c=8)
        QT_b = [abf(4096).rearrange("p (c n) -> p c n", c=8) for _ in range(2)]
        OT_b = [abf(4096).rearrange("p (c n) -> p c n", c=8) for _ in range(2)]
        horot = Rot([abf(512) for _ in range(2)], "ho")
        PTrot = Rot([abf(1024) for _ in range(3)], "PT")
        sqrot = Rot([abf(512) for _ in range(2)], "sq")
        rstdF = af32(512)
        rstdB = af32(512)
        tmprot = Rot([af32(512) for _ in range(2)], "tmp")
        kn_rot = Rot([af32(512) for _ in range(2)], "kn")
        knb_rot = Rot([abf(512) for _ in range(2)], "knb")
        t1_rot = Rot([af32(512) for _ in range(2)], "t1")
        recrot = Rot([af32(512) for _ in range(2)], "rec")
        rec2rot = Rot([af32(512) for _ in range(2)], "rec2")
        osbrot = Rot([af32(512) for _ in range(2)], "osb")
        cs_ = af32(1024).rearrange("p (a n) -> p a n", a=2)
        esk2 = af32(4)
        S.op("dve", lambda e: e.tensor_copy(out=esk2[0:64, :], in_=esink[0:64, 0:4]), reads=["esink"], writes=["esk2"])
        S.op("dve", lambda e: e.tensor_copy(out=esk2[64:128, :], in_=esink[64:128, 4:8]), reads=["esink"], writes=["esk2"])
        psG = Rot(PS[4:6], "PS", ["PS4", "PS5"])
        psP = Rot([0, 1], "pair")
        cur["rot"] = psG
        load_w(wq, None, 8, "wq", pc="wq")
        load_w(wo, None, 8, "wo", pc="wo")
        S.op("pool", lambda e: e.dma_start(out=masks, in_=c_mask.rearrange("a p n -> p a n")), writes=["masks"], dma="c1")
        precast_all()
        xFk = ["xF_%d" % c for c in range(8)]
        xBk = ["xB_%d" % c for c in range(8)]
        hk = ["hT%d" % c for c in range(8)]

        def stats_gen(xch, xk, rstd_, rkey):
            pb, pk = bank()
            prev = []
            for c0 in range(0, 8, 2):
                curl = []
                for c in (c0, c0 + 1):
                    sq, sk = sqrot.next()
                    S.op("pool", lambda e, sq=sq, c=c: e.tensor_tensor(out=sq, in0=xch[c], in1=xch[c], op=ALU.mult), reads=[xk[c]], writes=[sk])
                    curl.append((c, sq, sk))
                yield
                for c, sq, sk in curl:
                    S.op("pe", lambda e, sq=sq, c=c: e.matmul(pb, lhsT=ones_bf, rhs=sq, start=(c == 0), stop=(c == 7)), reads=[sk, "mats"], writes=[pk])
            yield
            S.op("act", lambda e: e.activation(out=rstd_, in_=pb, func=AF.Ln, scale=1.0 / D, bias=EPS), reads=[pk], writes=[rkey])
            S.op("act", lambda e: e.activation(out=rstd_, in_=rstd_, func=AF.Exp, scale=-0.5), reads=[rkey], writes=[rkey])
            yield

        def front(t):
            ci = tile_cond(t)
            QT = QT_b[t % 2]
            S.op("sp", lambda e: e.dma_start(out=xF, in_=xT[0][t]), writes=xFk, dma="ld_xF")
            if t < 8:
                S.op("sp", lambda e: e.dma_start(out=cs_, in_=c_cs0[:, :, t * 512:(t + 1) * 512].rearrange("a p n -> p a n")), writes=["cs"], dma="ld_cs")
            yield
            xch = [xF[:, c, :] for c in range(8)]
            yield from stats_gen(xch, xFk, rstdF, "rstdF")
            for c_ in range(8):
                modulate(xch, xFk, rstdF, "rstdF", 0, 0, ci, tmprot, [hT[:, c, :] for c in range(8)], hk, chunks=[c_], add_eng="dve")
                if c_ % 2 == 1:
                    yield
            for qc in range(8):
                pb, pk = bank()
                for k in range(8):
                    S.op("pe", lambda e, k=k, qc=qc, pb=pb: e.matmul(pb, lhsT=wq[:, k, qc * 128:(qc + 1) * 128], rhs=hT[:, k, :], start=(k == 0), stop=(k == 7)),
                         reads=[hk[k], "wq"], writes=[pk])
                qn, qnk = kn_rot.next()
                S.op("dve", lambda e, pb=pb, qn=qn: e.tensor_copy(out=qn, in_=pb), reads=[pk], writes=[qnk])
                if qc < 4:
                    sq, sk = sqrot.next()
                    S.op("pool", lambda e, sq=sq, qn=qn: e.tensor_tensor(out=sq, in0=qn, in1=qn, op=ALU.mult), reads=[qnk], writes=[sk])
                    yield
                    pbs, pks = bank()
                    S.op("pe", lambda e, sq=sq, pbs=pbs: e.matmul(pbs, lhsT=bones_bf, rhs=sq, start=True, stop=True), reads=[sk, "mats"], writes=[pks])
                    yield
                    rs, rsk = t1_rot.next()
                    S.op("act", lambda e, pbs=pbs, rs=rs: e.activation(out=rs, in_=pbs, func=AF.Ln, scale=1.0 / 64, bias=EPS), reads=[pks], writes=[rsk])
                    S.op("act", lambda e, rs=rs: e.activation(out=rs, in_=rs, func=AF.Exp, scale=-0.5), reads=[rsk], writes=[rsk])
                    S.op("dve", lambda e, qn=qn, rs=rs: e.scalar_tensor_tensor(out=qn, in0=qn, scalar=qkg_sb[:, 0:1], in1=rs, op0=ALU.mult, op1=ALU.mult),
                         reads=[qnk, rsk, "qkg"], writes=[qnk])
                if t < 8:
                    knb, knbk = knb_rot.next()
                    S.op("dve", lambda e, qn=qn, knb=knb: e.tensor_copy(out=knb, in_=qn), reads=[qnk], writes=[knbk])
                    yield
                    pb2, pk2 = bank()
                    S.op("pe", lambda e, pb2=pb2, knb=knb: e.matmul(pb2, lhsT=rot0_bf, rhs=knb, start=True, stop=True), reads=[knbk, "mats"], writes=[pk2])
                    S.op("pool", lambda e, qn=qn: e.tensor_tensor(out=qn, in0=qn, in1=cs_[:, 0, :], op=ALU.mult), reads=[qnk, "cs"], writes=[qnk])
                    yield
                    t1, t1k = t1_rot.next()
                    S.op("dve", lambda e, t1=t1, pb2=pb2: e.tensor_tensor(out=t1, in0=pb2, in1=cs_[:, 1, :], op=ALU.mult), reads=[pk2, "cs"], writes=[t1k])
                    S.op("pool", lambda e, qn=qn, t1=t1, qc=qc: e.tensor_tensor(out=QT[:, qc, :], in0=qn, in1=t1, op=ALU.add),
                         reads=[qnk, t1k], writes=[("QT", t % 2, qc)])
                else:
                    S.op("pool", lambda e, qn=qn, qc=qc: e.tensor_copy(out=QT[:, qc, :], in_=qn), reads=[qnk], writes=[("QT", t % 2, qc)])
                yield

        def back(t):
            ci = tile_cond(t)
            OT = OT_b[t % 2]
            S.op("sp", lambda e: e.dma_start(out=xB, in_=xT[0][t]), writes=xBk, dma="ld_xB")
            yield
            otk = [("OT", t % 2, typ, g, qb) for typ in range(2) for g in range(2) for qb in range(4)]
            for m in range(8):
                pb, pk = bank()
                for ch in range(8):
                    S.op("pe", lambda e, ch=ch, m=m, pb=pb: e.matmul(pb, lhsT=wo[:, ch, m * 128:(m + 1) * 128], rhs=OT[:, ch, :], start=(ch == 0), stop=(ch == 7)),
                         reads=otk + ["wo"], writes=[pk])
                S.op("dve", lambda e, m=m, pb=pb: e.scalar_tensor_tensor(out=xB[:, m, :], in0=pb, scalar=modv[:, 0, 2, m, ci:ci + 1], in1=xB[:, m, :],
                                                                       op0=ALU.mult, op1=ALU.add),
                     reads=[pk, xBk[m], "modv"], writes=[xBk[m]])
                yield
            S.op("sp", lambda e: e.dma_start(out=xT[1][t], in_=xB), reads=xBk, writes=[("xT1", t)], dma="st_xB")
            xch = [xB[:, c, :] for c in range(8)]
            yield from stats_gen(xch, xBk, rstdB, "rstdB")
            bufs = [horot.next() for _ in range(8)]

            def after(c):
                S.op("sp", lambda e, c=c: e.dma_start(out=hTs[0][t][:, c, :], in_=bufs[c][0]), reads=[bufs[c][1]], writes=[("hTs0", t, c)], dma="st2_" + bufs[c][1])
            for c_ in range(8):
                modulate(xch, xBk, rstdB, "rstdB", 0, 3, ci, tmprot, [b_[0] for b_ in bufs], [b_[1] for b_ in bufs], after=after, chunks=[c_], add_eng="dve")
                if c_ % 2 == 1:
                    yield

        def attn(t, filler):
            QT = QT_b[t % 2]
            OT = OT_b[t % 2]
            cnt = [0]

            def fill():
                cnt[0] += 1
                if cnt[0] % 2 == 0:
                    next(filler, None)
            for qb in range(4):
                for typ in range(2):
                    Ksrc, Vsrc = (KA, VA) if typ == 0 else (KS, VS)
                    kname = "K%d" % typ
                    base = 4 * typ
                    kl = []
                    if t < 8:
                        for j in range(4):
                            kl.append((j * 128, j, None, ["Kctx%d" % typ, "Vctx%d" % typ, "Vctx%da" % typ]))
                        if typ == 0:
                            for j in range(32):
                                kl.append((512 + j * 128, 4 + j, None, [(kname, j // 4)] + [("V%d_%d" % (typ, g_), 4 + j) for g_ in range(2)]))
                        else:
                            qbg = t * 4 + qb
                            for dj, mi in ((-1, 0), (0, None), (1, 1)):
                                j = qbg + dj
                                if 0 <= j < 32:
                                    kl.append((512 + j * 128, 4 + j, mi, [(kname, j // 4)] + [("V%d_%d" % (typ, g_), 4 + j) for g_ in range(2)]))
                    else:
                        sq_ = qb // 2
                        for j in range(2):
                            jj = 32 + sq_ * 2 + j
                            kl.append((512 + jj * 128, 4 + jj, None, [(kname, 8)] + [("V%d_%d" % (typ, g_), 4 + jj) for g_ in range(2)]))

                    def attn_pair(qb=qb, typ=typ, Ksrc=Ksrc, Vsrc=Vsrc, kname=kname, base=base, kl=kl):
                        nk_ = len(kl)
                        obs = [(PS[6], "PS6"), (PS[7], "PS7")]
                        pts = [None] * nk_

                        def isk(d_):
                            return (isinstance(d_, tuple) and d_[0] == kname) or (isinstance(d_, str) and d_.startswith("Kctx"))

                        def emit_s(j):
                            kcol, vt, mi, deps = kl[j]
                            pi = psP.next()[0]
                            keys = ["PS%d" % (2 * pi), "PS%d" % (2 * pi + 1)]
                            for g in range(2):
                                rows = slice(64 * g, 64 * g + 64)
                                S.op("pe", lambda e, g=g, rows=rows, kcol=kcol, pi=pi: e.matmul(PS[2 * pi + g].rearrange("p (h n) -> p h n", h=4), lhsT=Ksrc[rows, kcol:kcol + 128],
                                                                                              rhs=QT[rows, base:base + 4, qb * 128:(qb + 1) * 128], start=True, stop=True),
                                     reads=[d_ for d_ in deps if isk(d_)] + [("QT", t % 2, base + h_) for h_ in range(4)], writes=[keys[g]])
                            pt, ptk = PTrot.next()
                            S.op("act", lambda e, pi=pi, pt=pt: e.activation(out=pt, in_=PP[pi], func=AF.Exp, scale=0.125), reads=keys, writes=[ptk])
                            if mi is not None:
                                S.op("dve", lambda e, pt=pt, mi=mi: e.tensor_tensor(out=pt.rearrange("p (g n) -> p g n", g=2), in0=pt.rearrange("p (g n) -> p g n", g=2),
                                                                                 in1=masks[:, mi, :].unsqueeze(1).to_broadcast([128, 2, 512]), op=ALU.mult),
                                     reads=[ptk, "masks"], writes=[ptk])
                            pts[j] = (pt, ptk)

                        def emit_pv(j):
                            kcol, vt, mi, deps = kl[j]
                            pt, ptk = pts[j]
                            vdeps = [d_ for d_ in deps if not isk(d_)]
                            for g in range(2):
                                S.op("pe", lambda e, g=g, pt=pt, vt=vt, j=j: e.matmul(obs[g][0], lhsT=Vsrc[:, vt, 64 * g:64 * g + 128], rhs=pt[:, g * 512:(g + 1) * 512],
                                                                                   start=(j == 0), stop=(j == nk_ - 1)),
                                     reads=[ptk] + vdeps + ["V%sones" % ("A" if typ == 0 else "S")], writes=[obs[g][1]])
                        LA = 2
                        for j in range(min(LA, nk_)):
                            emit_s(j)
                        for j in range(nk_):
                            if j + LA < nk_:
                                emit_s(j + LA)
                            emit_pv(j)
                            fill()
                        rec, reck = recrot.next()
                        osb, osbk = osbrot.next()
                        for g in range(2):
                            rows = slice(64 * g, 64 * g + 64)
                            drows = slice(64 * (1 - g), 64 * (1 - g) + 64)
                            ob, obk = obs[g]
                            S.op("dve", lambda e, ob=ob, rows=rows, drows=drows: e.tensor_copy(out=rec[rows, :], in_=ob[drows, :]), reads=[obk], writes=[(reck, g)])
                            S.op("dve", lambda e, ob=ob, rows=rows: e.tensor_copy(out=osb[rows, :], in_=ob[rows, :]), reads=[obk], writes=[(osbk, g)])
                        if typ == 1:
                            S.op("dve", lambda e: e.tensor_tensor(out=rec.rearrange("p (h n) -> p h n", h=4), in0=rec.rearrange("p (h n) -> p h n", h=4),
                                                                  in1=esk2.unsqueeze(2).to_broadcast([128, 4, 128]), op=ALU.add),
                                 reads=[(reck, 0), (reck, 1), "esk2"], writes=[(reck, 0), (reck, 1)])
                        rec2, rec2k = rec2rot.next()
                        S.op("dve", lambda e: e.reciprocal(out=rec2, in_=rec), reads=[(reck, 0), (reck, 1)], writes=[rec2k])
                        S.op("pool", lambda e: e.tensor_tensor(out=OT[:, base:base + 4, qb * 128:(qb + 1) * 128],
                                                               in0=osb.rearrange("p (h n) -> p h n", h=4),
                                                               in1=rec2.rearrange("p (h n) -> p h n", h=4), op=ALU.mult),
                             reads=[(osbk, 0), (osbk, 1), rec2k], writes=[("OT", t % 2, typ, 0, qb), ("OT", t % 2, typ, 1, qb)])
                    attn_pair()

        def chain(*gens):
            for g_ in gens:
                if g_ is not None:
                    yield from g_

        for _ in front(0):
            pass
        for t in range(NT):
            filler = chain(back(t - 1) if t >= 1 else None, front(t + 1) if t + 1 < NT else None)
            attn(t, filler)
            for _ in filler:
                pass
        for _ in back(NT - 1):
            pass
        cur["rot"] = psrot
        S.barrier()

    def phase_mlp(l):
        Arena.top = PERSIST_TOP
        w1s = abf(8 * 4096).rearrange("p (k n) -> p k n", k=8)
        w2s = abf(32 * 1024).rearrange("p (k n) -> p k n", k=32)
        hT = abf(4096).rearrange("p (c n) -> p c n", c=8)
        aT = abf(32 * 512).rearrange("p (c n) -> p c n", c=32)
        xt_ = af32(4096).rearrange("p (c n) -> p c n", c=8)
        rrot = Rot([abf(512) for _ in range(2)], "rr")
        sqrot = Rot([abf(512) for _ in range(2)], "sq")
        rstd = af32(512)
        if l == 0:
            tmprot = Rot([af32(512) for _ in range(2)], "tmp")
            horot = Rot([abf(512) for _ in range(2)], "ho")
        else:
            ytrot = Rot([af32(1024) for _ in range(2)], "yt")
        load_w(w1s, None, 8, "w1s", col_groups=[(i * 1024, (i + 1) * 1024) for i in range(4)], pc="w1_%d" % l)
        load_w(w2s, None, 32, "w2s", per_k=True, pc="w2_%d" % l)
        hsrc = hTs[0] if l == 0 else hTs[2]
        xsrc = xT[1] if l == 0 else xT[3]
        xtk = ["xt_%d" % c for c in range(8)]
        hk = ["hT%d" % c for c in range(8)]

        def L1(t):
            S.op("sp", lambda e: e.dma_start(out=hT, in_=hsrc[t]), writes=hk, dma="ld_h")
            for f in range(32):
                pb, pk = bank()
                for k in range(8):
                    S.op("pe", lambda e, k=k, f=f, pb=pb: e.matmul(pb, lhsT=w1s[:, k, f * 128:(f + 1) * 128], rhs=hT[:, k, :], start=(k == 0), stop=(k == 7)),
                         reads=[hk[k], ("w1s", f // 8)], writes=[pk])
                r, rk_ = rrot.next()
                S.op("act", lambda e, pb=pb, r=r: e.activation(out=r, in_=pb, func=AF.Relu), reads=[pk], writes=[rk_])
                S.op("dve" if f % 2 == 0 else "pool", lambda e, r=r, f=f: e.tensor_tensor(out=aT[:, f, :], in0=r, in1=r, op=ALU.mult), reads=[rk_], writes=[("aT", f)])

        def L2(t):
            ci = tile_cond(t)
            S.op("sp", lambda e: e.dma_start(out=xt_, in_=xsrc[t]), writes=xtk, dma="ld_x")
            for m in range(8):
                pb, pk = bank()
                for f in range(32):
                    S.op("pe", lambda e, f=f, m=m, pb=pb: e.matmul(pb, lhsT=w2s[:, f, m * 128:(m + 1) * 128], rhs=aT[:, f, :], start=(f == 0), stop=(f == 31)),
                         reads=[("aT", f), ("w2s", f // 8)], writes=[pk])
                S.op("dve", lambda e, m=m, pb=pb: e.scalar_tensor_tensor(out=xt_[:, m, :], in0=pb, scalar=modv[:, l, 5, m, ci:ci + 1], in1=xt_[:, m, :],
                                                                       op0=ALU.mult, op1=ALU.add),
                     reads=[pk, xtk[m], "modv"], writes=[xtk[m]])

        def epi(t):
            ci = tile_cond(t)
            xch = [xt_[:, c, :] for c in range(8)]
            if l == 0:
                S.op("pool", lambda e: e.dma_start(out=xT[2][t], in_=xt_), reads=xtk, writes=[("xT2", t)], dma="st_x")
                rms_rstd(xch, xtk, 8, 1.0 / D, sqrot, rstd, "rstd")
                bufs = [horot.next() for _ in range(8)]

                def after(c):
                    S.op("pool", lambda e, c=c: e.dma_start(out=hTs[1][t][:, c, :], in_=bufs[c][0]), reads=[bufs[c][1]], writes=[("hTs1", t, c)], dma="st_" + bufs[c][1])
                modulate(xch, xtk, rstd, "rstd", 1, 0, ci, tmprot, [b_[0] for b_ in bufs], [b_[1] for b_ in bufs], after=after)
            else:
                rms_rstd(xch, xtk, 8, 1.0 / D, sqrot, rstd, "rstd")
                for c in range(8):
                    S.op("dve", lambda e, c=c: e.scalar_tensor_tensor(out=xt_[:, c, :], in0=xt_[:, c, :], scalar=gains_sb[:, 4, c:c + 1], in1=rstd, op0=ALU.mult, op1=ALU.mult),
                         reads=[xtk[c], "rstd", "gains"], writes=[xtk[c]])
                for s_ in range(4):
                    yt, ytk = ytrot.next()
                    for hf in range(2):
                        pb, pk = bank()
                        for cc in range(4):
                            c = 4 * hf + cc
                            S.op("pe", lambda e, c=c, cc=cc, pb=pb, s_=s_: e.transpose(pb[:, cc * 128:(cc + 1) * 128], xt_[:, c, s_ * 128:(s_ + 1) * 128], ident),
                                 reads=[xtk[c], "ident"], writes=[pk])
                        if hf == 0:
                            S.op("act", lambda e, pb=pb, yt=yt: e.copy(out=yt[:, 0:512], in_=pb), reads=[pk], writes=[ytk + "a"])
                        else:
                            S.op("dve", lambda e, pb=pb, yt=yt: e.tensor_copy(out=yt[:, 512:1024], in_=pb), reads=[pk], writes=[ytk + "b"])
                    r0 = t * 512 + s_ * 128
                    S.op("pool", lambda e, yt=yt, r0=r0: e.dma_start(out=y_out[r0:r0 + 128, :], in_=yt), reads=[ytk + "a", ytk + "b"],
                         writes=[("y", t, s_)], dma="st_" + ytk)

        L1(0)
        L2(0)
        for t in range(1, NT):
            L1(t)
            epi(t - 1)
            L2(t)
        epi(NT - 1)
        S.barrier()


    def phase_B1():
        Arena.top = PERSIST_TOP
        wi = abf(8 * 6144).rearrange("p (k n) -> p k n", k=8)
        hT_b = [abf(4096).rearrange("p (c n) -> p c n", c=8) for _ in range(2)]
        cs_b = [af32(2048).rearrange("p (a r n) -> p a r n", a=2, r=2)]
        kn_rot = Rot([af32(512) for _ in range(4)], "kn")
        knb_rot = Rot([abf(512) for _ in range(3)], "knb")
        t1_rot = Rot([af32(512) for _ in range(2)], "t1")
        qo_rot = Rot([abf(512) for _ in range(4)], "qo")
        go_rot = Rot([abf(512) for _ in range(4)], "go")
        ktok_b = [abf(4096).rearrange("p (s n) -> p s n", s=4)]
        vb_rot = Rot([abf(2048) for _ in range(2)], "vb")
        wi_groups = [(0, 1024), (1024, 2048), (2048, 3072), (3072, 4096), (4096, 5120), (5120, 6144)]
        load_w(wi, None, 8, "wi", col_groups=wi_groups, pc="w_in1")

        def b1_tile(t):
            hT, hk = hT_b[t % 2], ["hT%d_%d" % (t % 2, c) for c in range(8)]
            S.op("sp", lambda e: e.dma_start(out=hT, in_=hTs[1][t]), writes=hk, dma="ld_h%d" % (t % 2))
            cs_ = cs_b[0]
            if t < 8:
                for a_ in range(2):
                    S.op("sp", lambda e, a_=a_: e.dma_start(out=cs_[:, a_], in_=c_cs1[a_][:, :, t * 512:(t + 1) * 512]), writes=["cs"], dma="ld_cs")
            ktok = ktok_b[0]
            chunks = [(typ, qc) for typ in range(2) for qc in range(8)]
            st = {}

            def stage_proj(i):
                typ, qc = chunks[i]
                pb, pk = bank()
                c0 = typ * 1024 + qc * 128
                for k in range(8):
                    S.op("pe", lambda e, k=k, pb=pb, c0=c0: e.matmul(pb, lhsT=wi[:, k, c0:c0 + 128], rhs=hT[:, k, :], start=(k == 0), stop=(k == 7)),
                         reads=[hk[k], ("wi", c0 // 1024)], writes=[pk])
                qn, qnk = kn_rot.next()
                sc = 1.0 if typ == 0 else 1.0 / 16.0
                S.op("act", lambda e, pb=pb, qn=qn, sc=sc: e.activation(out=qn, in_=pb, func=AF.Identity, scale=sc), reads=[pk], writes=[qnk])
                st[i] = dict(qn=qn, qnk=qnk)
                if t < 8:
                    knb, knbk = knb_rot.next()
                    S.op("dve", lambda e, qn=qn, knb=knb: e.tensor_copy(out=knb, in_=qn), reads=[qnk], writes=[knbk])
                    st[i].update(knb=knb, knbk=knbk)

            def stage_rope(i):
                typ, qc = chunks[i]
                dc = qc % 2
                qn, qnk = st[i]["qn"], st[i]["qnk"]
                qo, qok = qo_rot.next()
                if t < 8:
                    knb, knbk = st[i]["knb"], st[i]["knbk"]
                    pb2, pk2 = bank()
                    S.op("pe", lambda e, pb2=pb2, knb=knb: e.matmul(pb2, lhsT=rot1_bf, rhs=knb, start=True, stop=True), reads=[knbk, "mats"], writes=[pk2])
                    t1, t1k = t1_rot.next()
                    S.op("dve", lambda e, t1=t1, pb2=pb2, dc=dc: e.tensor_tensor(out=t1, in0=pb2, in1=cs_[:, 1, dc, :], op=ALU.mult), reads=[pk2, "cs"], writes=[t1k])
                    S.op("pool", lambda e, qn=qn, dc=dc: e.tensor_tensor(out=qn, in0=qn, in1=cs_[:, 0, dc, :], op=ALU.mult), reads=[qnk, "cs"], writes=[qnk])
                    if typ == 0:
                        S.op("dve", lambda e, qn=qn, t1=t1, qo=qo: e.tensor_tensor(out=qo, in0=qn, in1=t1, op=ALU.add), reads=[qnk, t1k], writes=[qok])
                    else:
                        S.op("dve", lambda e, qn=qn, t1=t1: e.tensor_tensor(out=qn, in0=qn, in1=t1, op=ALU.add), reads=[qnk, t1k], writes=[qnk])
                        S.op("act", lambda e, qn=qn, qo=qo: e.copy(out=qo, in_=qn), reads=[qnk], writes=[qok])
                else:
                    S.op("act", lambda e, qn=qn, qo=qo: e.copy(out=qo, in_=qn), reads=[qnk], writes=[qok])
                dst = (qTs if typ == 0 else kTs)
                S.op("act", lambda e, qo=qo, dst=dst, qc=qc: e.dma_start(out=dst[t][qc], in_=qo), reads=[qok], writes=[("qk", typ, t, qc)], dma="st_" + qok)

            def stage_tr(i):
                typ, qc = chunks[i]
                if typ != 1:
                    return
                qn, qnk = st[i]["qn"], st[i]["qnk"]
                pb3, pk3 = bank()
                for s_ in range(4):
                    S.op("pe", lambda e, s_=s_, pb3=pb3, qn=qn: e.transpose(pb3[:, s_ * 128:(s_ + 1) * 128], qn[:, s_ * 128:(s_ + 1) * 128], ident),
                         reads=[qnk, "ident"], writes=[pk3])
                S.op("dve", lambda e, pb3=pb3, qc=qc: e.tensor_copy(out=ktok[:, :, qc * 128:(qc + 1) * 128], in_=pb3.rearrange("p (s n) -> p s n", s=4)),
                     reads=[pk3], writes=[("ktok", qc)])
            nch = len(chunks)
            for i in range(nch + 2):
                if i < nch:
                    stage_proj(i)
                if 0 <= i - 1 < nch:
                    stage_rope(i - 1)
                if 0 <= i - 2 < nch:
                    stage_tr(i - 2)
            S.op("act", lambda e: e.dma_start(out=kts[t], in_=ktok), reads=[("ktok", qc) for qc in range(8)], writes=[("kts", t)], dma="st_ktok")
            for s_ in range(4):
                vb, vbk = vb_rot.next()
                for vg in range(4):
                    pb, pk = bank()
                    for k in range(8):
                        S.op("pe", lambda e, k=k, pb=pb, s_=s_, vg=vg: e.matmul(pb, lhsT=hT[:, k, s_ * 128:(s_ + 1) * 128], rhs=wi[:, k, 2048 + vg * 512:2048 + (vg + 1) * 512],
                                                                            start=(k == 0), stop=(k == 7)),
                             reads=[hk[k], ("wi", 2 + vg // 2)], writes=[pk])
                    if vg % 2 == 0:
                        S.op("act", lambda e, pb=pb, vb=vb, vg=vg: e.copy(out=vb[:, vg * 512:(vg + 1) * 512], in_=pb), reads=[pk], writes=[(vbk, vg)])
                    else:
                        S.op("dve", lambda e, pb=pb, vb=vb, vg=vg: e.tensor_copy(out=vb[:, vg * 512:(vg + 1) * 512], in_=pb), reads=[pk], writes=[(vbk, vg)])
                S.op("act", lambda e, vb=vb, s_=s_: e.dma_start(out=vs_[t][s_], in_=vb), reads=[(vbk, vg) for vg in range(4)], writes=[("vs", t, s_)], dma="st_" + vbk)
            for gc in range(16):
                pb, pk = bank()
                c0 = 4096 + gc * 128
                for k in range(8):
                    S.op("pe", lambda e, k=k, pb=pb, c0=c0: e.matmul(pb, lhsT=wi[:, k, c0:c0 + 128], rhs=hT[:, k, :], start=(k == 0), stop=(k == 7)),
                         reads=[hk[k], ("wi", c0 // 1024)], writes=[pk])
                go, gok = go_rot.next()
                S.op("act", lambda e, pb=pb, go=go: e.activation(out=go, in_=pb, func=AF.Silu), reads=[pk], writes=[gok])
                S.op("act", lambda e, go=go, gc=gc: e.dma_start(out=gTs[t][gc], in_=go), reads=[gok], writes=[("gTs", t, gc)], dma="st_" + gok)
        for t in range(NT):
            b1_tile(t)
        S.barrier()

    def ret_tables():
        rt = af32(770)
        lg = af32(8)
        c128 = af32(1)
        decT = af32(8 * 128).rearrange("p (a n) -> p a n", a=8)
        qdec = af32(8 * 128).rearrange("p (a n) -> p a n", a=8)
        kdec = af32(8)
        cdec = af32(8)
        S.op("sp", lambda e: e.dma_start(out=rt, in_=c_ret), writes=["rt"], dma="c0")
        S.op("sp", lambda e: e.dma_start(out=lg, in_=decr), writes=["lg"], dma="c0")
        S.op("dve", lambda e: e.memset(c128, 128.0), writes=["c128"])
        S.op("act", lambda e: e.activation(out=lg, in_=lg, func=AF.Exp, scale=-1.0), reads=["lg"], writes=["lg"])
        S.op("dve", lambda e: e.tensor_scalar(out=lg, in0=lg, scalar1=1.0, scalar2=None, op0=ALU.add), reads=["lg"], writes=["lg"])
        S.op("act", lambda e: e.activation(out=lg, in_=lg, func=AF.Ln), reads=["lg"], writes=["lg"])
        S.op("dve", lambda e: e.tensor_scalar(out=lg, in0=lg, scalar1=-1.0, scalar2=None, op0=ALU.mult), reads=["lg"], writes=["lg"])
        for d in range(2):
            for h in range(4):
                a = 4 * d + h
                S.op("act", lambda e, a=a, d=d: e.activation(out=decT[:, a, :], in_=rt[:, d * 128:(d + 1) * 128], func=AF.Exp, scale=lg[:, a:a + 1]), reads=["rt", "lg"], writes=[("decT", a)])
                S.op("dve", lambda e, a=a, d=d: e.tensor_tensor(out=decT[:, a, :], in0=decT[:, a, :], in1=rt[:, 512 + d * 128:512 + (d + 1) * 128], op=ALU.mult),
                     reads=[("decT", a), "rt"], writes=[("decT", a)])
                S.op("act", lambda e, a=a, d=d: e.activation(out=qdec[:, a, :], in_=rt[:, 256 + d * 128:256 + (d + 1) * 128], func=AF.Exp, scale=lg[:, a:a + 1]), reads=["rt", "lg"], writes=[("qdec", a)])
                S.op("act", lambda e, a=a, d=d: e.activation(out=kdec[:, a:a + 1], in_=rt[:, 768 + d:769 + d], func=AF.Exp, scale=lg[:, a:a + 1]), reads=["rt", "lg"], writes=[("kdec", a)])
                S.op("act", lambda e, a=a: e.activation(out=cdec[:, a:a + 1], in_=c128, func=AF.Exp, scale=lg[:, a:a + 1]), reads=["c128", "lg"], writes=[("cdec", a)])
        return decT, qdec, kdec, cdec

    def phase_scan(d):
        Arena.top = PERSIST_TOP
        decT, qdec, kdec, cdec = ret_tables()
        S32 = af32(4096).rearrange("p (h c e) -> p h c e", h=4, c=2)
        Sbf = [abf(4096).rearrange("p (h c e) -> p h c e", h=4, c=2) for _ in range(2)]
        qT_b = [abf(4096).rearrange("p (c n) -> p c n", c=8) for _ in range(2)]
        kT_b = [abf(4096).rearrange("p (c n) -> p c n", c=8) for _ in range(2)]
        kt_b = [abf(4096).rearrange("p (s n) -> p s n", s=4) for _ in range(2)]
        v_b = [abf(8192).rearrange("p (s n) -> p s n", s=4) for _ in range(2)]
        attm_rot = Rot([abf(128) for _ in range(4)], "attm")
        qs_rot = Rot([abf(256).rearrange("p (c n) -> p c n", c=2) for _ in range(4)], "qs")
        kf_rot = Rot([abf(256) for _ in range(4)], "kf")
        if d == 0:
            of_rot = Rot([af32(2048).rearrange("p (c n) -> p c n", c=16) for _ in range(2)], "ofst")
        else:
            gT = abf(16 * 512).rearrange("p (c n) -> p c n", c=16)
            of_rot = Rot([af32(2048).rearrange("p (c n) -> p c n", c=16) for _ in range(2)], "ofld")
            osum_b = [af32(2048).rearrange("p (c n) -> p c n", c=16) for _ in range(2)]
            pending = []
            ocnt = [0]
            sq4 = [abf(512).rearrange("p (c n) -> p c n", c=4) for _ in range(4)]
            rs_rot = Rot([af32(128) for _ in range(4)], "rsh")
            tmp_rot = Rot([af32(512).rearrange("p (c n) -> p c n", c=4) for _ in range(4)], "gtmp")
            u_rot = Rot([abf(2048).rearrange("p (c n) -> p c n", c=16) for _ in range(2)], "ust")
        sidx = [0]

        def scan_tile(t, n):
            b = n % 2
            qT, kT, kt, v = qT_b[b], kT_b[b], kt_b[b], v_b[b]
            S.op("sp", lambda e: e.dma_start(out=qT, in_=qTs[t].rearrange("c p n -> p c n")), writes=["qT%d" % b], dma="ld_q%d" % b)
            S.op("sp", lambda e: e.dma_start(out=kT, in_=kTs[t].rearrange("c p n -> p c n")), writes=["kT%d" % b], dma="ld_k%d" % b)
            S.op("sp", lambda e: e.dma_start(out=kt, in_=kts[t]), writes=["kt%d" % b], dma="ld_kt%d" % b)
            S.op("sp", lambda e: e.dma_start(out=v, in_=vs_[t].rearrange("s p n -> p s n")), writes=["v%d" % b], dma="ld_v%d" % b)
            if d == 1:
                while pending:
                    pending.pop(0)()
                S.op("sp", lambda e: e.dma_start(out=gT, in_=gTs[t].rearrange("c p n -> p c n")), writes=["gT"], dma="ld_g")
                S.op("pool", lambda e: e.tensor_tensor(out=gT, in0=gT, in1=gng_sb.unsqueeze(2).to_broadcast([128, 16, 512]), op=ALU.mult), reads=["gT", "gng"], writes=["gT"])
            if t < 8:
                seqs = [([0, 1, 2, 3] if d == 0 else [3, 2, 1, 0], None)]
            else:
                seqs = [([0, 1] if d == 0 else [1, 0], 0), ([2, 3] if d == 0 else [3, 2], 1)]
            def chunk(order, pseq, ci_, s_):
                if True:
                    first = (t == (0 if d == 0 else 7) and ci_ == 0) if t < 8 else (ci_ == 0)
                    has_state = True if t < 8 else (ci_ > 0)
                    last_sample = (t == (7 if d == 0 else 0)) and ci_ == len(order) - 1 and t < 8
                    if t < 8 and first:
                        S.op("sp", lambda e: e.dma_start(out=S32, in_=s0[d].rearrange("h (c p) e -> p h c e", p=128)), writes=[("S32", h, c) for h in range(4) for c in range(2)], dma="ld_s0")
                        nb = Sbf[sidx[0] % 2]
                        for h in range(4):
                            S.op("act", lambda e, h=h, nb=nb: e.copy(out=nb[:, h], in_=S32[:, h]), reads=[("S32", h, 0), ("S32", h, 1)], writes=[("Sbf", sidx[0] % 2, h)])
                    curS = Sbf[sidx[0] % 2]
                    curk = sidx[0] % 2
                    nxtS = Sbf[(sidx[0] + 1) % 2]
                    nxtk = (sidx[0] + 1) % 2
                    sidx[0] += 1
                    cols = slice(s_ * 128, (s_ + 1) * 128)
                    per_h = []
                    for h in range(4):
                        a = 4 * d + h
                        pb, pk = bank()
                        for dc in range(2):
                            S.op("pe", lambda e, pb=pb, h=h, dc=dc: e.matmul(pb[:, 0:128], lhsT=kT[:, 2 * h + dc, cols], rhs=qT[:, 2 * h + dc, cols], start=(dc == 0), stop=(dc == 1)),
                                 reads=["kT%d" % b, "qT%d" % b], writes=[pk])
                        am, amk = attm_rot.next()
                        S.op("dve", lambda e, pb=pb, am=am, a=a: e.tensor_tensor(out=am, in0=pb[:, 0:128], in1=decT[:, a, :], op=ALU.mult), reads=[pk, ("decT", a)], writes=[amk])
                        qs, qsk = qs_rot.next()
                        if has_state:
                            S.op("pool", lambda e, qs=qs, h=h, a=a: e.tensor_tensor(out=qs, in0=qT[:, 2 * h:2 * h + 2, cols], in1=qdec[:, a, :].unsqueeze(1).to_broadcast([128, 2, 128]), op=ALU.mult),
                                 reads=["qT%d" % b, ("qdec", a)], writes=[qsk])
                        kf, kfk = kf_rot.next()
                        if not last_sample:
                            S.op("act", lambda e, kf=kf, h=h, a=a: e.activation(out=kf, in_=kt[:, s_, h * 256:(h + 1) * 256], func=AF.Identity, scale=kdec[:, a:a + 1]),
                                 reads=["kt%d" % b, ("kdec", a)], writes=[kfk])
                        per_h.append((am, amk, qs, qsk, kf, kfk))
                    if d == 0:
                        ost, ostk = of_rot.next()
                    else:
                        ob_i = ocnt[0] % 2
                        ocnt[0] += 1
                        osum = osum_b[ob_i]
                        ofl, oflk = of_rot.next()
                        S.op("sp", lambda e, ofl=ofl: e.dma_start(out=ofl, in_=ofs[t][s_]), writes=[oflk], dma="ld_" + oflk)
                    for h in range(4):
                        am, amk, qs, qsk, kf, kfk = per_h[h]
                        po, pok = bank()
                        for ec in range(4):
                            S.op("pe", lambda e, po=po, ec=ec, h=h, am=am: e.matmul(po[:, ec * 128:(ec + 1) * 128], lhsT=v[:, s_, h * 512 + ec * 128:h * 512 + (ec + 1) * 128], rhs=am,
                                                                                start=True, stop=(not has_state)),
                                 reads=["v%d" % b, amk], writes=[pok])
                            if has_state:
                                for dc in range(2):
                                    S.op("pe", lambda e, po=po, ec=ec, h=h, dc=dc, qs=qs: e.matmul(po[:, ec * 128:(ec + 1) * 128], lhsT=curS[:, h, dc, ec * 128:(ec + 1) * 128], rhs=qs[:, dc, :],
                                                                                              start=False, stop=(dc == 1)),
                                         reads=[("Sbf", curk, h), qsk], writes=[pok])
                        pov = po.rearrange("p (c n) -> p c n", c=4)
                        if d == 0:
                            S.op("act", lambda e, pov=pov, ost=ost, h=h: e.copy(out=ost[:, 4 * h:4 * h + 4, :], in_=pov), reads=[pok], writes=[(ostk, h)])
                        else:
                            S.op("dve", lambda e, pov=pov, ofl=ofl, h=h: e.tensor_tensor(out=osum[:, 4 * h:4 * h + 4, :], in0=pov, in1=ofl[:, 4 * h:4 * h + 4, :], op=ALU.add),
                                 reads=[pok, oflk], writes=[("osum", ob_i, h)])
                    if d == 0:
                        S.op("act", lambda e, ost=ost: e.dma_start(out=ofs[t][s_], in_=ost), reads=[(ostk, h) for h in range(4)], writes=[("ofs", t, s_)], dma="st_" + ostk)
                    else:
                        while pending:
                            pending.pop(0)()
                    if not last_sample:
                        for h in range(4):
                            a = 4 * d + h
                            am, amk, qs, qsk, kf, kfk = per_h[h]
                            for dc in range(2):
                                pb, pk = bank()
                                S.op("pe", lambda e, pb=pb, kf=kf, h=h, dc=dc: e.matmul(pb, lhsT=kf[:, dc * 128:(dc + 1) * 128], rhs=v[:, s_, h * 512:(h + 1) * 512], start=True, stop=True),
                                     reads=[kfk, "v%d" % b], writes=[pk])
                                if has_state:
                                    S.op("dve", lambda e, pb=pb, h=h, dc=dc, a=a: e.scalar_tensor_tensor(out=S32[:, h, dc, :], in0=S32[:, h, dc, :], scalar=cdec[:, a:a + 1], in1=pb, op0=ALU.mult, op1=ALU.add),
                                         reads=[pk, ("S32", h, dc), ("cdec", a)], writes=[("S32", h, dc)])
                                else:
                                    S.op("dve", lambda e, pb=pb, h=h, dc=dc: e.tensor_copy(out=S32[:, h, dc, :], in_=pb), reads=[pk], writes=[("S32", h, dc)])
                            S.op("act", lambda e, h=h, nxtS=nxtS: e.copy(out=nxtS[:, h], in_=S32[:, h]), reads=[("S32", h, 0), ("S32", h, 1)], writes=[("Sbf", nxtk, h)])
                    if t == 8 and ci_ == len(order) - 1:
                        S.op("pool", lambda e, pseq=pseq: e.dma_start(out=ns[d][pseq].rearrange("h (c p) e -> p h c e", p=128), in_=S32),
                             reads=[("S32", h, c) for h in range(4) for c in range(2)], writes=[("ns", d, pseq)], dma="st_ns")
                    def gn_emit():
                        ust, ustk = u_rot.next()
                        sqs = []
                        for h in range(4):
                            sqh, sqk = sq4[h], "gsq%d" % h
                            S.op("act", lambda e, h=h, sqh=sqh: e.activation(out=sqh, in_=osum[:, 4 * h:4 * h + 4, :], func=AF.Square), reads=[("osum", ob_i, h)], writes=[sqk])
                            sqs.append((sqh, sqk))
                        pbs = []
                        for h in range(4):
                            sqh, sqk = sqs[h]
                            pb, pk = bank()
                            for ec in range(4):
                                S.op("pe", lambda e, pb=pb, ec=ec, sqh=sqh: e.matmul(pb[:, 0:128], lhsT=ones_bf, rhs=sqh[:, ec, :], start=(ec == 0), stop=(ec == 3)), reads=[sqk, "mats"], writes=[pk])
                            pbs.append((pb, pk))
                        rss = []
                        for h in range(4):
                            pb, pk = pbs[h]
                            rs, rsk = rs_rot.next()
                            S.op("act", lambda e, pb=pb, rs=rs: e.activation(out=rs, in_=pb[:, 0:128], func=AF.Ln, scale=1.0 / 512, bias=EPS), reads=[pk], writes=[rsk])
                            S.op("act", lambda e, rs=rs: e.activation(out=rs, in_=rs, func=AF.Exp, scale=-0.5), reads=[rsk], writes=[rsk])
                            rss.append((rs, rsk))
                        for h in range(4):
                            rs, rsk = rss[h]
                            tp, tpk = tmp_rot.next()
                            S.op("dve", lambda e, tp=tp, rs=rs, h=h: e.tensor_tensor(out=tp, in0=osum[:, 4 * h:4 * h + 4, :], in1=rs.unsqueeze(1).to_broadcast([128, 4, 128]), op=ALU.mult),
                                 reads=[("osum", ob_i, h), rsk], writes=[tpk])
                            S.op("dve", lambda e, tp=tp, ust=ust, h=h: e.tensor_tensor(out=ust[:, 4 * h:4 * h + 4, :], in0=tp, in1=gT[:, 4 * h:4 * h + 4, cols], op=ALU.mult),
                                 reads=[tpk, "gT"], writes=[(ustk, h)])
                        S.op("act", lambda e, ust=ust: e.dma_start(out=uTs[t][s_], in_=ust), reads=[(ustk, h) for h in range(4)], writes=[("uTs", t, s_)], dma="st_" + ustk)
                    if d == 1:
                        pending.append(gn_emit)
            for order, pseq in seqs:
                for ci_, s_ in enumerate(order):
                    chunk(order, pseq, ci_, s_)

        order_t = list(range(8)) if d == 0 else list(range(7, -1, -1))
        for n, t in enumerate(order_t + [8]):
            scan_tile(t, n)
        if d == 1:
            while pending:
                pending.pop(0)()
        S.barrier()

    def phase_B2c():
        Arena.top = PERSIST_TOP
        wo = abf(16 * 1024).rearrange("p (k n) -> p k n", k=16)
        uT_b = [abf(16 * 512).rearrange("p (c n) -> p c n", c=16) for _ in range(2)]
        xt_b = [af32(4096).rearrange("p (c n) -> p c n", c=8) for _ in range(2)]
        hout = abf(4096).rearrange("p (c n) -> p c n", c=8)
        sqrot = Rot([abf(512) for _ in range(2)], "sq")
        rstd = af32(512)
        tmprot = Rot([af32(512) for _ in range(2)], "tmp")
        load_w(wo, None, 16, "wo1", pc="w_out1")

        def c_tile(t):
            ci = tile_cond(t)
            b = t % 2
            uT, xt_ = uT_b[b], xt_b[b]
            xtk = ["xt%d_%d" % (b, c) for c in range(8)]
            for s_ in range(4):
                S.op("sp", lambda e, s_=s_: e.dma_start(out=uT[:, :, s_ * 128:(s_ + 1) * 128], in_=uTs[t][s_]), writes=[("uT", b, s_)], dma="ld_u%d" % b)
            S.op("sp", lambda e: e.dma_start(out=xt_, in_=xT[2][t]), writes=xtk, dma="ld_xt%d" % b)
            for m in range(8):
                pb, pk = bank()
                for ch in range(16):
                    S.op("pe", lambda e, ch=ch, m=m, pb=pb: e.matmul(pb, lhsT=wo[:, ch, m * 128:(m + 1) * 128], rhs=uT[:, ch, :], start=(ch == 0), stop=(ch == 15)),
                         reads=[("uT", b, s_) for s_ in range(4)] + ["wo1"], writes=[pk])
                S.op("dve", lambda e, m=m, pb=pb: e.scalar_tensor_tensor(out=xt_[:, m, :], in0=pb, scalar=modv[:, 1, 2, m, ci:ci + 1], in1=xt_[:, m, :], op0=ALU.mult, op1=ALU.add),
                     reads=[pk, xtk[m], "modv"], writes=[xtk[m]])
            S.op("pool", lambda e: e.dma_start(out=xT[3][t], in_=xt_), reads=xtk, writes=[("xT3", t)], dma="st_xt%d" % b)
            xch = [xt_[:, c, :] for c in range(8)]
            rms_rstd(xch, xtk, 8, 1.0 / D, sqrot, rstd, "rstd")
            hok = ["ho%d" % c for c in range(8)]
            modulate(xch, xtk, rstd, "rstd", 1, 3, ci, tmprot, [hout[:, c, :] for c in range(8)], hok)
            S.op("pool", lambda e: e.dma_start(out=hTs[2][t], in_=hout), reads=hok, writes=[("hTs2", t)], dma="st_ho")
        for t in range(NT):
            c_tile(t)
        S.barrier()

    allp = [("mod", phase_mod), ("A1", phase_A1), ("A2", phase_A2), ("A3", lambda: phase_mlp(0)), ("B1", phase_B1), ("B2f", lambda: phase_scan(0)),
            ("B2b", lambda: phase_scan(1)), ("B2c", phase_B2c), ("B3", lambda: phase_mlp(1))]
    for nm, fnp in allp:
        if phases is None or nm in phases:
            fnp()
    S.emit()
    return nc, es


def _prep_inputs(inp, b, consts):
    f = lambda a: np.ascontiguousarray(np.asarray(a, np.float32))
    m = {}
    m["xin"] = f(np.concatenate([inp["x_sample"][b], inp["x_prompt"][2 * b].reshape(256, D), inp["x_prompt"][2 * b + 1].reshape(256, D)], 0))
    m["cond"] = f(np.stack([_fm(inp["c"][b], 8), _fm(inp["c_ctx"], 8)], -1))
    m["kctx"] = f(np.stack([inp["cache_l0_attn_k"][b].reshape(512, 128), inp["cache_l0_swa_k"][b].reshape(512, 128)]))
    m["vctx"] = f(np.stack([inp["cache_l0_attn_v"][b].reshape(512, 128), inp["cache_l0_swa_v"][b].reshape(512, 128)]))
    m["s0"] = f(np.stack([inp["state_l1_ret_fwd"][b], inp["state_l1_ret_bwd"][b]]))
    m["adaw0"] = f(inp["l0_ada_w"])
    m["adaw1"] = f(inp["l1_ada_w"])
    m["adab"] = f(np.stack([_fm(inp["l0_ada_b"], 48), _fm(inp["l1_ada_b"], 48)], 1))
    m["gains"] = f(np.stack([_fm(inp[k], 8) for k in ("l0_norm_mix", "l0_norm_mlp", "l1_norm_mix", "l1_norm_mlp", "final_norm")], 1))
    m["qkg"] = f(np.stack([np.tile(inp["l0_q_norm"], 2), np.tile(inp["l0_k_norm"], 2)], -1))
    m["sinkr"] = f(np.tile(np.asarray(inp["l0_sink"])[None, :], (128, 1)))
    m["decr"] = f(np.tile(np.concatenate([inp["l1_ret_decay_fwd"], inp["l1_ret_decay_bwd"]])[None, :], (128, 1)))
    m["gng"] = f(_fm(np.asarray(inp["l1_ret_gn"]).reshape(-1), 16))
    w = np.asarray(inp["l0_w_in"], np.float32)
    cols = []
    for base in (0, 768):
        for c in range(4):
            cols += list(range(base + c * 64, base + (c + 1) * 64)) + list(range(base + (4 + c) * 64, base + (5 + c) * 64))
    cols += list(range(512, 640)) + list(range(1280, 1408)) + list(range(640, 768)) + list(range(1408, 1536))
    m["w_in0"] = f(w[:, cols])
    rows = []
    for base in (0, 512):
        for c in range(4):
            rows += list(range(base + c * 64, base + (c + 1) * 64)) + list(range(base + (4 + c) * 64, base + (5 + c) * 64))
    m["w_out0"] = f(np.asarray(inp["l0_w_out"], np.float32)[rows, :])
    m["w1_0"] = f(inp["l0_mlp_w1"])
    m["w2_0"] = f(inp["l0_mlp_w2"])
    m["w1_1"] = f(inp["l1_mlp_w1"])
    m["w2_1"] = f(inp["l1_mlp_w2"])
    m["w_in1"] = f(inp["l1_w_in"])
    m["w_out1"] = f(inp["l1_w_out"])
    m.update(consts)
    return m


_CACHE = {}


def kernel(**inputs):
    inp = {k: np.asarray(v) for k, v in inputs.items()}
    consts = _consts()
    if "nc" not in _CACHE:
        _CACHE["nc"] = build()
    nc, _es = _CACHE["nc"]
    in_maps = [_prep_inputs(inp, b, consts) for b in range(8)]
    r = run_bass_kernel_spmd(nc, in_maps, core_ids=list(range(8)))
    outs = r.results
    y_prompt = np.zeros((16, 256, D), np.float32)
    y_sample = np.zeros((8, 4096, D), np.float32)
    nk = [np.zeros((16, 256, 2, 64), np.float32) for _ in range(2)]
    nv = [np.zeros((16, 256, 2, 64), np.float32) for _ in range(2)]
    ns = [np.zeros((16, 4, 256, 512), np.float32) for _ in range(2)]
    for b in range(8):
        o = outs[b]
        y_sample[b] = o["y_out"][:4096]
        y_prompt[2 * b] = o["y_out"][4096:4352]
        y_prompt[2 * b + 1] = o["y_out"][4352:4608]
        for a in range(2):
            nk[a][2 * b:2 * b + 2] = o["nk"][a].reshape(2, 256, 2, 64)
            nv[a][2 * b:2 * b + 2] = o["nv"][a].reshape(2, 256, 2, 64)
            ns[a][2 * b:2 * b + 2] = o["ns"][a]
    return (y_prompt, y_sample, nk[0], nv[0], nk[1], nv[1], ns[0], ns[1])
```

```python
import numpy as np
from contextlib import ExitStack
import concourse.bass as bass
import concourse.mybir as mybir
from concourse.bass_utils import run_bass_kernel_spmd

F32 = mybir.dt.float32
BF16 = mybir.dt.bfloat16
AF = mybir.ActivationFunctionType
ALU = mybir.AluOpType

D = 1024
NT = 9
TT = 512
EPS = 1e-6
SBW = 53000


class Op:
    __slots__ = ("eng", "fn", "deps", "dma_key", "signal", "sigval", "idx", "line")


class Sched:
    ENGS = ("pe", "act", "dve", "pool", "sp")

    def __init__(self, nc, es):
        self.nc = nc
        self.es = es
        self.ops = {e: [] for e in self.ENGS}
        self.lastw = {}
        self.readers = {}
        self.sem = {e: es.enter_context(nc.semaphore("d_" + e)) for e in self.ENGS}
        self.dsem = {}
        self.dcount = {}
        self.all_ops = []
        self.barrier_ops = None

    def op(self, eng, fn, reads=(), writes=(), dma=None):
        o = Op()
        import sys as _sys
        o.line = _sys._getframe(1).f_lineno
        o.eng = eng
        o.fn = fn
        o.dma_key = dma
        o.signal = False
        o.sigval = None
        writes = list(writes) + [r for r in reads if isinstance(r, str) and r.startswith("PS")]
        reads = [r for r in reads if not (isinstance(r, str) and r.startswith("PS"))]
        deps = set()
        for r in reads:
            w = self.lastw.get(r)
            if w is not None:
                deps.add(w)
        for w_ in writes:
            w = self.lastw.get(w_)
            if w is not None:
                deps.add(w)
            for rd in self.readers.get(w_, ()):
                deps.add(rd)
        if self.barrier_ops:
            deps.update(self.barrier_ops)
        deps.discard(o)
        o.deps = deps
        o.idx = len(self.ops[eng])
        self.ops[eng].append(o)
        self.all_ops.append(o)
        for r in reads:
            self.readers.setdefault(r, []).append(o)
        for w_ in writes:
            self.lastw[w_] = o
            self.readers[w_] = []
        if dma is not None and dma not in self.dsem:
            self.dsem[dma] = self.es.enter_context(self.nc.semaphore("q_" + dma))
            self.dcount[dma] = 0
        return o

    def barrier(self):
        print("BARRIER", {e: len(self.ops[e]) for e in self.ENGS}, "ARENA", getattr(self, "arena_top", None))
        last = set()
        for e in self.ENGS:
            if self.ops[e]:
                last.add(self.ops[e][-1])
        lastdma = {}
        for o in self.all_ops:
            if o.dma_key is not None:
                lastdma[o.dma_key] = o
        last.update(lastdma.values())
        self.barrier_ops = last
        self.lastw = {}
        self.readers = {}

    def emit(self):
        import os
        lim = int(os.environ.get("SCHED_LIMIT", "0"))
        if lim:
            keep = set(id(o) for o in self.all_ops[:lim])
            self.all_ops = self.all_ops[:lim]
            for e in self.ENGS:
                self.ops[e] = [o for o in self.ops[e] if id(o) in keep]
        print("SCHED ops:", len(self.all_ops), {e: len(self.ops[e]) for e in self.ENGS})
        if os.environ.get("SCHED_DUMP"):
            a, b = [int(x) for x in os.environ["SCHED_DUMP"].split(":")]
            for i, o in enumerate(self.all_ops[a:b]):
                print("OP", a + i, o.eng, o.line, o.dma_key)
        for o in self.all_ops:
            for d in o.deps:
                if d.eng == "pe" and o.eng == "pe" and d.dma_key is None:
                    continue
                d.signal = True
        cnt = {e: 0 for e in self.ENGS}
        for e in self.ENGS:
            for o in self.ops[e]:
                if o.dma_key is not None:
                    self.dcount[o.dma_key] += 16
                    o.sigval = (self.dsem[o.dma_key], self.dcount[o.dma_key])
                elif o.signal:
                    cnt[e] += 1
                    o.sigval = (self.sem[e], cnt[e])
        engobj = {"pe": None, "act": None, "dve": None, "pool": None, "sp": None}
        block = self.es.enter_context(self.nc.Block())

        def run(ename):
            def body(eng):
                waited = {}
                for o in self.ops[ename]:
                    need = {}
                    for d in o.deps:
                        if d.eng == "pe" and ename == "pe" and d.dma_key is None:
                            continue
                        s, v = d.sigval
                        k = id(s)
                        if waited.get(k, 0) >= v:
                            continue
                        if k not in need or need[k][1] < v:
                            need[k] = (s, v)
                    for k, (s, v) in need.items():
                        eng.wait_ge(s, v)
                        waited[k] = v
                    ins = o.fn(eng)
                    if o.dma_key is not None:
                        ins.then_inc(o.sigval[0], 16)
                    elif o.signal:
                        ins.then_inc(o.sigval[0], 1)
                if ename == "sp":
                    for key, s in self.dsem.items():
                        if self.dcount[key] > 0:
                            eng.wait_ge(s, self.dcount[key])
            return body

        block.tensor(run("pe"))
        block.scalar(run("act"))
        block.vector(run("dve"))
        block.gpsimd(run("pool"))
        block.sync(run("sp"))


def _consts():
    c = {}
    c["c_ident"] = np.eye(128, dtype=np.float32)
    mats = np.zeros((4, 128, 128), np.float32)
    mats[0] = 1.0
    mats[1, :64, :64] = 1.0
    mats[1, 64:, 64:] = 1.0
    for hb in (0, 64):
        for off in (0, 32):
            for i in range(16):
                mats[2, hb + off + 16 + i, hb + off + i] = -1.0
                mats[2, hb + off + i, hb + off + 16 + i] = 1.0
    for i in range(64):
        mats[3, 64 + i, i] = -1.0
        mats[3, i, 64 + i] = 1.0
    c["c_mats"] = mats
    t = np.arange(4096)
    row = (t // 64).astype(np.float32)
    col = (t % 64).astype(np.float32)
    inv0 = (10000.0 ** (-np.arange(16, dtype=np.float32) / 16)).astype(np.float32)
    ang = np.zeros((64, 4096), np.float32)
    for d in range(64):
        pos = row if d < 32 else col
        ang[d] = pos * inv0[d % 16]
    ang = np.concatenate([ang, ang], 0)
    c["c_cs0"] = np.stack([np.cos(ang), np.sin(ang)]).astype(np.float32)
    inv1 = (10000.0 ** (-np.arange(64, dtype=np.float32) / 64)).astype(np.float32)
    ang1 = np.zeros((128, 2, 4096), np.float32)
    for p in range(128):
        ang1[p, 0] = row * inv1[p % 64]
        ang1[p, 1] = col * inv1[p % 64]
    c["c_cs1"] = np.stack([np.cos(ang1), np.sin(ang1)]).astype(np.float32)
    kp = np.arange(128)[:, None]
    qf = np.arange(128)[None, :]
    m1 = (qf <= kp).astype(np.float32)
    m2 = (kp <= qf).astype(np.float32)
    c["c_mask"] = np.stack([np.tile(m1, (1, 4)), np.tile(m2, (1, 4))]).astype(np.float32)
    j = np.arange(128, dtype=np.float32)[:, None]
    i = np.arange(128, dtype=np.float32)[None, :]
    ret = np.zeros((128, 6 * 128 + 2), np.float32)
    ret[:, 0:128] = np.maximum(i - j, 0)
    ret[:, 128:256] = np.maximum(j - i, 0)
    ret[:, 256:384] = (i + 1) + 0 * j
    ret[:, 384:512] = (128 - i) + 0 * j
    ret[:, 512:640] = (i >= j)
    ret[:, 640:768] = (j >= i)
    ret[:, 768] = 127 - np.arange(128)
    ret[:, 769] = np.arange(128)
    c["c_ret"] = ret
    return c


def _fm(v, k):
    return np.ascontiguousarray(np.asarray(v, np.float32).reshape(k, 128).T)


def build(debug=False, phases=None):
    nc = bass.Bass("TRN2", target_bir_lowering=False)
    es = ExitStack()

    def din(name, shape, dt=F32):
        return nc.dram_tensor(name, list(shape), dt, kind="ExternalInput").ap()

    def dout(name, shape, dt=F32):
        return nc.dram_tensor(name, list(shape), dt, kind="ExternalOutput").ap()

    def dscr(name, shape, dt=F32):
        kind = "ExternalOutput" if debug else "Internal"
        return nc.dram_tensor(name, list(shape), dt, kind=kind).ap()

    xin = din("xin", [NT * TT, D])
    cond = din("cond", [128, 8, 2])
    kctx = din("kctx", [2, 512, 128])
    vctx = din("vctx", [2, 512, 128])
    s0 = din("s0", [2, 4, 256, 512])
    adaw = [din("adaw0", [D, 6 * D]), din("adaw1", [D, 6 * D])]
    adab = din("adab", [128, 2, 48])
    gains = din("gains", [128, 5, 8])
    qkg = din("qkg", [128, 2])
    sinkr = din("sinkr", [128, 8])
    decr = din("decr", [128, 8])
    gng = din("gng", [128, 16])
    w_in0 = din("w_in0", [D, 1536])
    w_out0 = din("w_out0", [D, D])
    w1 = [din("w1_0", [D, 4 * D]), din("w1_1", [D, 4 * D])]
    w2 = [din("w2_0", [4 * D, D]), din("w2_1", [4 * D, D])]
    w_in1 = din("w_in1", [D, 6 * D])
    w_out1 = din("w_out1", [2 * D, D])
    c_ident = din("c_ident", [128, 128])
    c_mats = din("c_mats", [4, 128, 128])
    c_cs0 = din("c_cs0", [2, 128, 4096])
    c_cs1 = din("c_cs1", [2, 128, 2, 4096])
    c_mask = din("c_mask", [2, 128, 512])
    c_ret = din("c_ret", [128, 770])

    y_out = dout("y_out", [NT * TT, D])
    nk = dout("nk", [2, 512, 128])
    nv = dout("nv", [2, 512, 128])
    ns = dout("ns", [2, 2, 4, 256, 512])

    xT = [dscr("xT%d" % i, [NT, 128, 8, TT]) for i in range(4)]
    hTs = [dscr("hT%d" % i, [NT, 128, 8, TT], BF16) for i in range(3)]
    qTs = dscr("qTs", [NT, 8, 128, TT], BF16)
    kTs = dscr("kTs", [NT, 8, 128, TT], BF16)
    kts = dscr("kts", [NT, 128, 4, 1024], BF16)
    vs_ = dscr("vs", [NT, 4, 128, 2048], BF16)
    gTs = dscr("gTs", [NT, 16, 128, TT], BF16)
    ofs = dscr("ofs", [NT, 4, 128, 16, 128])
    uTs = dscr("uTs", [NT, 4, 128, 16, 128], BF16)

    wbf = {"w1_0": dscr("w1_0_bf", [D, 4 * D], BF16), "w2_0": dscr("w2_0_bf", [4 * D, D], BF16), "w_in1": dscr("w_in1_bf", [D, 6 * D], BF16),
           "w_out1": dscr("w_out1_bf", [2 * D, D], BF16), "w1_1": dscr("w1_1_bf", [D, 4 * D], BF16), "w2_1": dscr("w2_1_bf", [4 * D, D], BF16),
           "wq": dscr("wq_bf", [D, D], BF16), "wo": dscr("wo_bf", [D, D], BF16)}
    wsrc = {"w1_0": w1[0], "w2_0": w2[0], "w_in1": w_in1, "w_out1": w_out1, "w1_1": w1[1], "w2_1": w2[1], "wq": w_in0[:, 0:1024], "wo": w_out0}

    big = es.enter_context(nc.sbuf_tensor("big", [128, SBW], F32))
    PP = [es.enter_context(nc.psum_tensor("pp%d" % i, [128, 1024], F32))[:] for i in range(4)]
    PS = [PP[i // 2][:, (i % 2) * 512:(i % 2) * 512 + 512] for i in range(8)]
    S = Sched(nc, es)

    class Arena:
        top = 0

    def af32(n):
        a = big[:, Arena.top:Arena.top + n]
        Arena.top += n
        S.arena_top = Arena.top
        assert Arena.top <= SBW, Arena.top
        return a

    def abf(n):
        w = (n + 1) // 2
        a = big[:, Arena.top:Arena.top + w].bitcast(BF16)
        Arena.top += w
        S.arena_top = Arena.top
        assert Arena.top <= SBW, Arena.top
        return a

    uid = [0]

    def rk(prefix="r"):
        uid[0] += 1
        return "%s%d" % (prefix, uid[0])

    class Rot:
        def __init__(self, aps, name, keys=None):
            self.aps = aps
            self.keys = keys if keys is not None else [name + str(i) for i in range(len(aps))]
            self.i = 0

        def next(self):
            k = self.i % len(self.aps)
            self.i += 1
            return self.aps[k], self.keys[k]

    psrot = Rot([p for p in PS], "PS")
    cur = {"rot": psrot}

    def bank():
        return cur["rot"].next()

    ident = af32(128)
    mats_bf = abf(4 * 128).rearrange("p (m n) -> p m n", m=4)
    ones_bf, bones_bf, rot0_bf, rot1_bf = (mats_bf[:, i, :] for i in range(4))
    modv = af32(2 * 6 * 16).rearrange("p (l s k c) -> p l s k c", l=2, s=6, k=8)
    gains_sb = af32(40).rearrange("p (a k) -> p a k", a=5)
    qkg_sb = af32(2)
    esink = af32(8)
    gng_sb = af32(16)
    PERSIST_TOP = Arena.top

    S.op("sp", lambda e: e.dma_start(out=ident, in_=c_ident), writes=["ident"], dma="c0")
    S.op("pool", lambda e: e.dma_start(out=mats_bf, in_=c_mats.rearrange("m p n -> p m n")), writes=["mats"], dma="c1")
    S.op("sp", lambda e: e.dma_start(out=gains_sb, in_=gains), writes=["gains"], dma="c0")
    S.op("sp", lambda e: e.dma_start(out=qkg_sb, in_=qkg), writes=["qkg"], dma="c0")
    S.op("sp", lambda e: e.dma_start(out=esink, in_=sinkr), writes=["esink"], dma="c0")
    S.op("sp", lambda e: e.dma_start(out=gng_sb, in_=gng), writes=["gng"], dma="c0")
    S.op("act", lambda e: e.activation(out=esink, in_=esink, func=AF.Exp), reads=["esink"], writes=["esink"])

    def phase_mod():
        Arena.top = PERSIST_TOP
        cnd = af32(16).rearrange("p (k c) -> p k c", k=8)
        scn = af32(16).rearrange("p (k c) -> p k c", k=8)
        tmp = af32(16).rearrange("p (k c) -> p k c", k=8)
        adab_sb = af32(96).rearrange("p (l j) -> p l j", l=2)
        acc = af32(2 * 96).rearrange("p (l j c) -> p l j c", l=2, j=48)
        wb = [af32(4096).rearrange("p (k n) -> p k n", k=8) for _ in range(2)]
        modrow = af32(6144)
        S.op("sp", lambda e: e.dma_start(out=cnd, in_=cond), writes=["cnd"], dma="c0")
        S.op("sp", lambda e: e.dma_start(out=adab_sb, in_=adab), writes=["adab"], dma="c0")
        S.op("act", lambda e: e.activation(out=tmp, in_=cnd, func=AF.Exp, scale=-1.0), reads=["cnd"], writes=["mtmp"])
        S.op("dve", lambda e: e.tensor_scalar(out=tmp, in0=tmp, scalar1=1.0, scalar2=None, op0=ALU.add), reads=["mtmp"], writes=["mtmp"])
        S.op("dve", lambda e: e.reciprocal(out=tmp, in_=tmp), reads=["mtmp"], writes=["mtmp"])
        S.op("dve", lambda e: e.tensor_tensor(out=scn, in0=cnd, in1=tmp, op=ALU.mult), reads=["mtmp", "cnd"], writes=["scn"])
        n = 0
        for l in range(2):
            wv = adaw[l].rearrange("(k p) n -> p k n", p=128)
            for cg in range(12):
                w_ap, wkey = wb[n % 2], "adw%d" % (n % 2)
                q_ = ("sp", "act")[n % 2]
                n += 1
                S.op(q_, lambda e, w_ap=w_ap, cg=cg, wv=wv: e.dma_start(out=w_ap, in_=wv[:, :, cg * 512:(cg + 1) * 512]), writes=[wkey], dma=wkey)
                pb, pk = bank()
                for k in range(8):
                    S.op("pe", lambda e, pb=pb, w_ap=w_ap, k=k: e.matmul(pb[0:2, :], lhsT=scn[:, k, :], rhs=w_ap[:, k, :], start=(k == 0), stop=(k == 7)),
                         reads=[wkey, "scn"], writes=[pk])
                S.op("dve", lambda e, pb=pb, cg=cg: e.tensor_copy(out=modrow[0:2, cg * 512:(cg + 1) * 512], in_=pb[0:2, :]), reads=[pk], writes=[("modrow", cg)])
            pb, pk = bank()
            for j in range(48):
                S.op("pe", lambda e, pb=pb, j=j: e.transpose(pb[:, 2 * j:2 * j + 2], modrow[0:2, j * 128:(j + 1) * 128], ident[0:2, 0:2]),
                     reads=[("modrow", j // 4), "ident"], writes=[pk])
            S.op("dve", lambda e, pb=pb, l=l: e.tensor_copy(out=acc[:, l], in_=pb[:, 0:96].rearrange("p (j c) -> p j c", j=48)), reads=[pk], writes=["acc%d" % l])
            S.op("dve", lambda e, l=l: e.tensor_tensor(out=acc[:, l], in0=acc[:, l],
                                                       in1=adab_sb[:, l, :].unsqueeze(2).to_broadcast([128, 48, 2]), op=ALU.add),
                 reads=["acc%d" % l, "adab"], writes=["acc%d" % l])
            a6 = acc[:, l].rearrange("p (s k) c -> p s k c", s=6)
            for s_i in (0, 2, 3, 5):
                S.op("dve", lambda e, l=l, s_i=s_i, a6=a6: e.tensor_copy(out=modv[:, l, s_i], in_=a6[:, s_i]),
                     reads=["acc%d" % l], writes=["modv"])
            for s_i, g_i in ((1, 2 * l), (4, 2 * l + 1)):
                S.op("dve", lambda e, l=l, s_i=s_i, a6=a6: e.tensor_scalar(out=modv[:, l, s_i], in0=a6[:, s_i], scalar1=1.0, scalar2=None, op0=ALU.add),
                     reads=["acc%d" % l], writes=["modv"])
                S.op("dve", lambda e, l=l, s_i=s_i, g_i=g_i: e.tensor_tensor(out=modv[:, l, s_i], in0=modv[:, l, s_i],
                                                                            in1=gains_sb[:, g_i, :].unsqueeze(2).to_broadcast([128, 8, 2]), op=ALU.mult),
                     reads=["modv", "gains"], writes=["modv"])
        S.barrier()

    def precast_all(names=("w1_0", "w2_0", "w_in1", "w_out1", "w1_1", "w2_1")):
        for name in names:
            src = wsrc[name]
            for k in range(src.shape[0] // 128):
                S.op("pool", lambda e, name=name, src=src, k=k: e.dma_start(out=wbf[name][k * 128:(k + 1) * 128, :], in_=src[k * 128:(k + 1) * 128, :]),
                     writes=[("pc", name, k)], dma="pc_" + name)

    def load_w(dst, src_rows, k_chunks, key, col_groups=None, per_k=False, pc=None):
        if pc is not None:
            eng_, src_rows, rd = "sp", wbf[pc], (lambda k: [("pc", pc, k)])
        else:
            eng_, rd = "pool", (lambda k: [])
        return _load_w(dst, src_rows, k_chunks, key, col_groups, per_k, eng_, rd)

    hwq = [0]

    def _load_w(dst, src_rows, k_chunks, key, col_groups, per_k, eng_, rd):
        if eng_ == "sp":
            srcv = src_rows.rearrange("(k p) n -> p k n", p=128)
            if col_groups is not None:
                for gi_, (c0, c1) in enumerate(col_groups):
                    q_ = ("sp", "act")[hwq[0] % 2]
                    hwq[0] += 1
                    S.op(q_, lambda e, c0=c0, c1=c1: e.dma_start(out=dst[:, :, c0:c1], in_=srcv[:, :, c0:c1]),
                         reads=[r_ for k in range(k_chunks) for r_ in rd(k)], writes=[(key, gi_)], dma="%s_g%d%s" % (key, gi_, q_))
            else:
                for g0 in range(0, k_chunks, 8):
                    q_ = ("sp", "act")[hwq[0] % 2]
                    hwq[0] += 1
                    S.op(q_, lambda e, g0=g0: e.dma_start(out=dst[:, g0:g0 + 8, :], in_=srcv[:, g0:g0 + 8, :]),
                         reads=[r_ for k in range(g0, g0 + 8) for r_ in rd(k)], writes=[(key, g0 // 8) if per_k else key], dma="%s_k%d%s" % (key, g0 // 8, q_))
            return
        if col_groups is not None:
            for gi_, (c0, c1) in enumerate(col_groups):
                for k in range(k_chunks):
                    S.op(eng_, lambda e, k=k, c0=c0, c1=c1: e.dma_start(out=dst[:, k, c0:c1], in_=src_rows[k * 128:(k + 1) * 128, c0:c1]),
                         reads=rd(k), writes=[(key, gi_)], dma="%s_g%d" % (key, gi_))
            return
        for k in range(k_chunks):
            S.op(eng_, lambda e, k=k: e.dma_start(out=dst[:, k, :], in_=src_rows[k * 128:(k + 1) * 128, :]),
                 reads=rd(k), writes=[(key, k // 8) if per_k else key], dma=("%s_k%d" % (key, k // 8)) if per_k else key)

    def rms_rstd(xchunks, xkeys, nchunk, inv_n, sqrot, rstd, rstd_key, onesm=None, sq_eng="act"):
        onesm = ones_bf if onesm is None else onesm
        pb, pk = bank()
        N = xchunks[0].shape[-1]
        for c in range(nchunk):
            sq, sk = sqrot.next()
            if sq_eng == "act":
                S.op("act", lambda e, sq=sq, c=c: e.activation(out=sq[:, 0:N], in_=xchunks[c], func=AF.Square), reads=[xkeys[c]], writes=[sk])
            else:
                S.op(sq_eng, lambda e, sq=sq, c=c: e.tensor_tensor(out=sq[:, 0:N], in0=xchunks[c], in1=xchunks[c], op=ALU.mult), reads=[xkeys[c]], writes=[sk])
            S.op("pe", lambda e, sq=sq, c=c, pb=pb: e.matmul(pb[:, 0:N], lhsT=onesm, rhs=sq[:, 0:N], start=(c == 0), stop=(c == nchunk - 1)),
                 reads=[sk, "mats"], writes=[pk])
        S.op("act", lambda e, pb=pb: e.activation(out=rstd[:, 0:N], in_=pb[:, 0:N], func=AF.Ln, scale=inv_n, bias=EPS), reads=[pk], writes=[rstd_key])
        S.op("act", lambda e: e.activation(out=rstd[:, 0:N], in_=rstd[:, 0:N], func=AF.Exp, scale=-0.5), reads=[rstd_key], writes=[rstd_key])

    def modulate(xchunks, xkeys, rstd, rstd_key, l, gi, ci, tmprot, outs, outkeys, after=None, chunks=None, add_eng="act"):
        for c in (range(8) if chunks is None else chunks):
            tp, tk = tmprot.next()
            if add_eng == "dve":
                S.op("dve", lambda e, c=c, tp=tp: e.tensor_tensor(out=tp, in0=xchunks[c], in1=rstd, op=ALU.mult), reads=[xkeys[c], rstd_key], writes=[tk])
                S.op("dve", lambda e, c=c, tp=tp: e.tensor_scalar(out=outs[c], in0=tp, scalar1=modv[:, l, gi + 1, c, ci:ci + 1], scalar2=modv[:, l, gi, c, ci:ci + 1],
                                                                op0=ALU.mult, op1=ALU.add),
                     reads=[tk, "modv"], writes=[outkeys[c]])
            else:
                S.op("dve", lambda e, c=c, tp=tp: e.scalar_tensor_tensor(out=tp, in0=xchunks[c], scalar=modv[:, l, gi + 1, c, ci:ci + 1], in1=rstd,
                                                                        op0=ALU.mult, op1=ALU.mult),
                     reads=[xkeys[c], rstd_key, "modv"], writes=[tk])
                S.op("act", lambda e, c=c, tp=tp: e.activation(out=outs[c], in_=tp, func=AF.Identity, bias=modv[:, l, gi, c, ci:ci + 1], scale=1.0),
                     reads=[tk, "modv"], writes=[outkeys[c]])
            if after is not None:
                after(c)

    def tile_cond(t):
        return 0 if t < 8 else 1

    res = {}

    def phase_A1():
        Arena.top = PERSIST_TOP
        KA = abf(5120)
        KS = abf(5120)
        VA = abf(40 * 192).rearrange("p (t c) -> p t c", t=40)
        VS = abf(40 * 192).rearrange("p (t c) -> p t c", t=40)
        res.update(KA=KA, KS=KS, VA=VA, VS=VS)
        res["A1_TOP"] = Arena.top
        wkv = abf(8 * 512).rearrange("p (k n) -> p k n", k=8)
        xin_b = [af32(4096).rearrange("p (s d) -> p s d", s=4) for _ in range(2)]
        xt_b = [af32(4096).rearrange("p (c n) -> p c n", c=8) for _ in range(2)]
        hT = abf(4096).rearrange("p (c n) -> p c n", c=8)
        sqrot = Rot([abf(512) for _ in range(2)], "sq")
        rstd = af32(512)
        tmprot = Rot([af32(512) for _ in range(2)], "tmp")
        kn_rot = Rot([af32(512) for _ in range(2)], "kn")
        knb_rot = Rot([abf(512) for _ in range(2)], "knb")
        t1_rot = Rot([af32(512) for _ in range(2)], "t1")
        cs_b = [af32(1024).rearrange("p (a n) -> p a n", a=2) for _ in range(2)]
        vst = af32(1024).rearrange("p (s n) -> p s n", s=4)
        kst = af32(1024).rearrange("p (a s n) -> p a s n", a=2, s=4)[:, :, :, :]
        kst = af32(1024).rearrange("p (a s n) -> p a s n", a=2, s=4)
        cst = af32(4 * 128 * 2).rearrange("p (a s n) -> p a s n", a=2, s=4)
        load_w(wkv, w_in0[:, 1024:1536], 8, "wkv")
        precast_all(("wq", "wo"))
        S.op("dve", lambda e: e.memset(VA[:, :, 64:128], 1.0), writes=["VAones"])
        S.op("dve", lambda e: e.memset(VS[:, :, 64:128], 1.0), writes=["VSones"])
        for a, (Kdst, Vdst) in enumerate(((KA, VA), (KS, VS))):
            S.op("sp", lambda e, a=a: e.dma_start(out=cst[:, 0], in_=kctx[a].rearrange("(s p) n -> p s n", p=128)), writes=["cstk"], dma="cst")
            S.op("sp", lambda e, a=a: e.dma_start(out=cst[:, 1], in_=vctx[a].rearrange("(s p) n -> p s n", p=128)), writes=["cstv"], dma="cst")
            pb, pk = bank()
            for s_ in range(4):
                S.op("pe", lambda e, s_=s_, pb=pb: e.transpose(pb[:, s_ * 128:(s_ + 1) * 128], cst[:, 0, s_, :], ident),
                     reads=["cstk", "ident"], writes=[pk])
            S.op("act", lambda e, pb=pb, Kdst=Kdst: e.copy(out=Kdst[:, 0:512], in_=pb), reads=[pk], writes=["Kctx%d" % a])
            S.op("dve", lambda e, Vdst=Vdst: e.tensor_copy(out=Vdst[:, 0:4, 0:64], in_=cst[:, 1, :, 0:64]), reads=["cstv"], writes=["Vctx%d" % a])
            S.op("dve", lambda e, Vdst=Vdst: e.tensor_copy(out=Vdst[:, 0:4, 128:192], in_=cst[:, 1, :, 64:128]), reads=["cstv"], writes=["Vctx%da" % a])
        xin_v = xin.rearrange("(t s p) d -> t p s d", s=4, p=128)
        for t in range(NT):
            ci = tile_cond(t)
            xi, xik = xin_b[t % 2], "xin%d" % (t % 2)
            xt_, xtk = xt_b[t % 2], ["xt%d_%d" % (t % 2, c) for c in range(8)]
            S.op("sp", lambda e, t=t, xi=xi: e.dma_start(out=xi, in_=xin_v[t]), writes=[xik], dma=xik)
            if t < 8:
                cs_, csk = cs_b[t % 2], "cs%d" % (t % 2)
                S.op("sp", lambda e, t=t, cs_=cs_: e.dma_start(out=cs_, in_=c_cs0[:, :, t * 512:(t + 1) * 512].rearrange("a p n -> p a n")),
                     writes=[csk], dma=csk)
            for c in range(8):
                pb, pk = bank()
                for s_ in range(4):
                    S.op("pe", lambda e, c=c, s_=s_, pb=pb, xi=xi: e.transpose(pb[:, s_ * 128:(s_ + 1) * 128], xi[:, s_, c * 128:(c + 1) * 128], ident),
                         reads=[xik, "ident"], writes=[pk])
                eng = "act" if c % 2 == 0 else "dve"
                if eng == "act":
                    S.op("act", lambda e, c=c, pb=pb, xt_=xt_: e.copy(out=xt_[:, c, :], in_=pb), reads=[pk], writes=[xtk[c]])
                else:
                    S.op("dve", lambda e, c=c, pb=pb, xt_=xt_: e.tensor_copy(out=xt_[:, c, :], in_=pb), reads=[pk], writes=[xtk[c]])
            S.op("pool", lambda e, t=t, xt_=xt_: e.dma_start(out=xT[0][t], in_=xt_), reads=xtk, writes=[("xT0", t)], dma="st_xt%d" % (t % 2))
            xch = [xt_[:, c, :] for c in range(8)]
            rms_rstd(xch, xtk, 8, 1.0 / D, sqrot, rstd, "rstd")
            hk = ["hT%d" % c for c in range(8)]
            modulate(xch, xtk, rstd, "rstd", 0, 0, ci, tmprot, [hT[:, c, :] for c in range(8)], hk)
            for a, Kdst in enumerate((KA, KS)):
                pb, pk = bank()
                for k in range(8):
                    S.op("pe", lambda e, k=k, a=a, pb=pb: e.matmul(pb, lhsT=wkv[:, k, a * 128:(a + 1) * 128], rhs=hT[:, k, :], start=(k == 0), stop=(k == 7)),
                         reads=[hk[k], "wkv"], writes=[pk])
                kn, knk = kn_rot.next()
                if a == 0:
                    S.op("act", lambda e, pb=pb, kn=kn: e.copy(out=kn, in_=pb), reads=[pk], writes=[knk])
                    rs, rsk = t1_rot.next()
                    rms_rstd([kn], [knk], 1, 1.0 / 64, sqrot, rs, rsk, onesm=bones_bf)
                    S.op("dve", lambda e, kn=kn, rs=rs: e.scalar_tensor_tensor(out=kn, in0=kn, scalar=qkg_sb[:, 1:2], in1=rs, op0=ALU.mult, op1=ALU.mult),
                         reads=[knk, rsk, "qkg"], writes=[knk])
                else:
                    S.op("act", lambda e, pb=pb, kn=kn: e.copy(out=kn, in_=pb), reads=[pk], writes=[knk])
                col0 = 512 + t * 512
                if t < 8:
                    knb, knbk = knb_rot.next()
                    S.op("act", lambda e, kn=kn, knb=knb: e.copy(out=knb, in_=kn), reads=[knk], writes=[knbk])
                    pb2, pk2 = bank()
                    S.op("pe", lambda e, pb2=pb2, knb=knb: e.matmul(pb2, lhsT=rot0_bf, rhs=knb, start=True, stop=True), reads=[knbk, "mats"], writes=[pk2])
                    t1, t1k = t1_rot.next()
                    S.op("dve", lambda e, t1=t1, pb2=pb2, cs_=cs_: e.tensor_tensor(out=t1, in0=pb2, in1=cs_[:, 1, :], op=ALU.mult), reads=[pk2, csk], writes=[t1k])
                    S.op("pool", lambda e, kn=kn, cs_=cs_: e.tensor_tensor(out=kn, in0=kn, in1=cs_[:, 0, :], op=ALU.mult), reads=[knk, csk], writes=[knk])
                    S.op("dve", lambda e, kn=kn, t1=t1, Kdst=Kdst, col0=col0: e.tensor_tensor(out=Kdst[:, col0:col0 + 512], in0=kn, in1=t1, op=ALU.add),
                         reads=[knk, t1k], writes=[("K%d" % a, t)])
                else:
                    S.op("act", lambda e, kn=kn, Kdst=Kdst, col0=col0: e.copy(out=Kdst[:, col0:col0 + 512], in_=kn), reads=[knk], writes=[("K%d" % a, t)])
                    pb2, pk2 = bank()
                    for s_ in range(4):
                        S.op("pe", lambda e, s_=s_, pb2=pb2, kn=kn: e.transpose(pb2[:, s_ * 128:(s_ + 1) * 128], kn[:, s_ * 128:(s_ + 1) * 128], ident),
                             reads=[knk, "ident"], writes=[pk2])
                    S.op("dve", lambda e, a=a, pb2=pb2: e.tensor_copy(out=kst[:, a], in_=pb2.rearrange("p (s n) -> p s n", s=4)), reads=[pk2], writes=["kst%d" % a])
                    S.op("pool", lambda e, a=a: e.dma_start(out=nk[a].rearrange("(s p) n -> p s n", p=128), in_=kst[:, a]), reads=["kst%d" % a], writes=[("nk", a)], dma="st_k%d" % a)
            for s_ in range(4):
                pb, pk = bank()
                for k in range(8):
                    S.op("pe", lambda e, k=k, s_=s_, pb=pb: e.matmul(pb[:, 0:256], lhsT=hT[:, k, s_ * 128:(s_ + 1) * 128], rhs=wkv[:, k, 256:512], start=(k == 0), stop=(k == 7)),
                         reads=[hk[k], "wkv"], writes=[pk])
                kt = 4 + t * 4 + s_
                for a, Vdst in enumerate((VA, VS)):
                    for g in range(2):
                        if (a + g) % 2 == 0:
                            S.op("act", lambda e, a=a, g=g, Vdst=Vdst, pb=pb, kt=kt: e.copy(out=Vdst[:, kt, g * 128:g * 128 + 64], in_=pb[:, a * 128 + g * 64:a * 128 + g * 64 + 64]),
                                 reads=[pk], writes=[("V%d_%d" % (a, g), kt)])
                        else:
                            S.op("dve", lambda e, a=a, g=g, Vdst=Vdst, pb=pb, kt=kt: e.tensor_copy(out=Vdst[:, kt, g * 128:g * 128 + 64], in_=pb[:, a * 128 + g * 64:a * 128 + g * 64 + 64]),
                                 reads=[pk], writes=[("V%d_%d" % (a, g), kt)])
                if t == 8:
                    S.op("dve", lambda e, s_=s_, pb=pb: e.tensor_copy(out=vst[:, s_, :], in_=pb[:, 0:256]), reads=[pk], writes=[("vst", s_)])
            if t == 8:
                for a in range(2):
                    S.op("pool", lambda e, a=a: e.dma_start(out=nv[a].rearrange("(s p) n -> p s n", p=128), in_=vst[:, :, a * 128:(a + 1) * 128]),
                         reads=[("vst", s_) for s_ in range(4)], writes=[("nv", a)], dma="st_v%d" % a)
        S.barrier()


    def phase_A2():
        Arena.top = res["A1_TOP"]
        KA, KS, VA, VS = res["KA"], res["KS"], res["VA"], res["VS"]
        wq = abf(8 * 1024).rearrange("p (k n) -> p k n", k=8)
        wo = abf(8 * 1024).rearrange("p (k n) -> p k n", k=8)
        masks = abf(2 * 512).rearrange("p (a n) -> p a n", a=2)
        xF = af32(4096).rearrange("p (c n) -> p c n", c=8)
        xB = af32(4096).rearrange("p (c n) -> p c n", c=8)
        hT = abf(4096).rearrange("p (c n) -> p c n", c=8)
        QT_b = [abf(4096).rearrange("p (c n) -> p c n", c=8) for _ in range(2)]
        OT_b = [abf(4096).rearrange("p (c n) -> p c n", c=8) for _ in range(2)]
        horot = Rot([abf(512) for _ in range(2)], "ho")
        PTrot = Rot([abf(1024) for _ in range(3)], "PT")
        sqrot = Rot([abf(512) for _ in range(2)], "sq")
        rstdF = af32(512)
        rstdB = af32(512)
        tmprot = Rot([af32(512) for _ in range(2)], "tmp")
        kn_rot = Rot([af32(512) for _ in range(2)], "kn")
        knb_rot = Rot([abf(512) for _ in range(2)], "knb")
        t1_rot = Rot([af32(512) for _ in range(2)], "t1")
        recrot = Rot([af32(512) for _ in range(2)], "rec")
        rec2rot = Rot([af32(512) for _ in range(2)], "rec2")
        osbrot = Rot([af32(512) for _ in range(2)], "osb")
        cs_ = af32(1024).rearrange("p (a n) -> p a n", a=2)
        esk2 = af32(4)
        S.op("dve", lambda e: e.tensor_copy(out=esk2[0:64, :], in_=esink[0:64, 0:4]), reads=["esink"], writes=["esk2"])
        S.op("dve", lambda e: e.tensor_copy(out=esk2[64:128, :], in_=esink[64:128, 4:8]), reads=["esink"], writes=["esk2"])
        psG = Rot(PS[4:6], "PS", ["PS4", "PS5"])
        psP = Rot([0, 1], "pair")
        cur["rot"] = psG
        load_w(wq, None, 8, "wq", pc="wq")
        load_w(wo, None, 8, "wo", pc="wo")
        S.op("pool", lambda e: e.dma_start(out=masks, in_=c_mask.rearrange("a p n -> p a n")), writes=["masks"], dma="c1")
        precast_all()
        xFk = ["xF_%d" % c for c in range(8)]
        xBk = ["xB_%d" % c for c in range(8)]
        hk = ["hT%d" % c for c in range(8)]

        def stats_gen(xch, xk, rstd_, rkey):
            pb, pk = bank()
            prev = []
            for c0 in range(0, 8, 2):
                curl = []
                for c in (c0, c0 + 1):
                    sq, sk = sqrot.next()
                    S.op("pool", lambda e, sq=sq, c=c: e.tensor_tensor(out=sq, in0=xch[c], in1=xch[c], op=ALU.mult), reads=[xk[c]], writes=[sk])
                    curl.append((c, sq, sk))
                yield
                for c, sq, sk in curl:
                    S.op("pe", lambda e, sq=sq, c=c: e.matmul(pb, lhsT=ones_bf, rhs=sq, start=(c == 0), stop=(c == 7)), reads=[sk, "mats"], writes=[pk])
            yield
            S.op("act", lambda e: e.activation(out=rstd_, in_=pb, func=AF.Ln, scale=1.0 / D, bias=EPS), reads=[pk], writes=[rkey])
            S.op("act", lambda e: e.activation(out=rstd_, in_=rstd_, func=AF.Exp, scale=-0.5), reads=[rkey], writes=[rkey])
            yield

        def front(t):
            ci = tile_cond(t)
            QT = QT_b[t % 2]
            S.op("sp", lambda e: e.dma_start(out=xF, in_=xT[0][t]), writes=xFk, dma="ld_xF")
            if t < 8:
                S.op("sp", lambda e: e.dma_start(out=cs_, in_=c_cs0[:, :, t * 512:(t + 1) * 512].rearrange("a p n -> p a n")), writes=["cs"], dma="ld_cs")
            yield
            xch = [xF[:, c, :] for c in range(8)]
            yield from stats_gen(xch, xFk, rstdF, "rstdF")
            for c_ in range(8):
                modulate(xch, xFk, rstdF, "rstdF", 0, 0, ci, tmprot, [hT[:, c, :] for c in range(8)], hk, chunks=[c_], add_eng="dve")
                if c_ % 2 == 1:
                    yield
            for qc in range(8):
                pb, pk = bank()
                for k in range(8):
                    S.op("pe", lambda e, k=k, qc=qc, pb=pb: e.matmul(pb, lhsT=wq[:, k, qc * 128:(qc + 1) * 128], rhs=hT[:, k, :], start=(k == 0), stop=(k == 7)),
                         reads=[hk[k], "wq"], writes=[pk])
                    if k == 3:
                        yield
                qn, qnk = kn_rot.next()
                S.op("dve", lambda e, pb=pb, qn=qn: e.tensor_copy(out=qn, in_=pb), reads=[pk], writes=[qnk])
                if qc < 4:
                    sq, sk = sqrot.next()
                    S.op("pool", lambda e, sq=sq, qn=qn: e.tensor_tensor(out=sq, in0=qn, in1=qn, op=ALU.mult), reads=[qnk], writes=[sk])
                    yield
                    pbs, pks = bank()
                    S.op("pe", lambda e, sq=sq, pbs=pbs: e.matmul(pbs, lhsT=bones_bf, rhs=sq, start=True, stop=True), reads=[sk, "mats"], writes=[pks])
                    yield
                    rs, rsk = t1_rot.next()
                    S.op("act", lambda e, pbs=pbs, rs=rs: e.activation(out=rs, in_=pbs, func=AF.Ln, scale=1.0 / 64, bias=EPS), reads=[pks], writes=[rsk])
                    S.op("act", lambda e, rs=rs: e.activation(out=rs, in_=rs, func=AF.Exp, scale=-0.5), reads=[rsk], writes=[rsk])
                    S.op("dve", lambda e, qn=qn, rs=rs: e.scalar_tensor_tensor(out=qn, in0=qn, scalar=qkg_sb[:, 0:1], in1=rs, op0=ALU.mult, op1=ALU.mult),
                         reads=[qnk, rsk, "qkg"], writes=[qnk])
                if t < 8:
                    knb, knbk = knb_rot.next()
                    S.op("dve", lambda e, qn=qn, knb=knb: e.tensor_copy(out=knb, in_=qn), reads=[qnk], writes=[knbk])
                    yield
                    pb2, pk2 = bank()
                    S.op("pe", lambda e, pb2=pb2, knb=knb: e.matmul(pb2, lhsT=rot0_bf, rhs=knb, start=True, stop=True), reads=[knbk, "mats"], writes=[pk2])
                    S.op("pool", lambda e, qn=qn: e.tensor_tensor(out=qn, in0=qn, in1=cs_[:, 0, :], op=ALU.mult), reads=[qnk, "cs"], writes=[qnk])
                    yield
                    t1, t1k = t1_rot.next()
                    S.op("dve", lambda e, t1=t1, pb2=pb2: e.tensor_tensor(out=t1, in0=pb2, in1=cs_[:, 1, :], op=ALU.mult), reads=[pk2, "cs"], writes=[t1k])
                    S.op("pool", lambda e, qn=qn, t1=t1, qc=qc: e.tensor_tensor(out=QT[:, qc, :], in0=qn, in1=t1, op=ALU.add),
                         reads=[qnk, t1k], writes=[("QT", t % 2, qc)])
                else:
                    S.op("pool", lambda e, qn=qn, qc=qc: e.tensor_copy(out=QT[:, qc, :], in_=qn), reads=[qnk], writes=[("QT", t % 2, qc)])
                yield

        def back(t):
            ci = tile_cond(t)
            OT = OT_b[t % 2]
            S.op("sp", lambda e: e.dma_start(out=xB, in_=xT[0][t]), writes=xBk, dma="ld_xB")
            yield
            otk = [("OT", t % 2, typ, g, qb) for typ in range(2) for g in range(2) for qb in range(4)]
            for m in range(8):
                pb, pk = bank()
                for ch in range(8):
                    S.op("pe", lambda e, ch=ch, m=m, pb=pb: e.matmul(pb, lhsT=wo[:, ch, m * 128:(m + 1) * 128], rhs=OT[:, ch, :], start=(ch == 0), stop=(ch == 7)),
                         reads=otk + ["wo"], writes=[pk])
                    if ch == 3:
                        yield
                S.op("dve", lambda e, m=m, pb=pb: e.scalar_tensor_tensor(out=xB[:, m, :], in0=pb, scalar=modv[:, 0, 2, m, ci:ci + 1], in1=xB[:, m, :],
                                                                       op0=ALU.mult, op1=ALU.add),
                     reads=[pk, xBk[m], "modv"], writes=[xBk[m]])
                yield
            S.op("sp", lambda e: e.dma_start(out=xT[1][t], in_=xB), reads=xBk, writes=[("xT1", t)], dma="st_xB")
            xch = [xB[:, c, :] for c in range(8)]
            yield from stats_gen(xch, xBk, rstdB, "rstdB")
            bufs = [horot.next() for _ in range(8)]

            def after(c):
                S.op("sp", lambda e, c=c: e.dma_start(out=hTs[0][t][:, c, :], in_=bufs[c][0]), reads=[bufs[c][1]], writes=[("hTs0", t, c)], dma="st2_" + bufs[c][1])
            for c_ in range(8):
                modulate(xch, xBk, rstdB, "rstdB", 0, 3, ci, tmprot, [b_[0] for b_ in bufs], [b_[1] for b_ in bufs], after=after, chunks=[c_], add_eng="dve")
                if c_ % 2 == 1:
                    yield

        def attn(t, filler):
            QT = QT_b[t % 2]
            OT = OT_b[t % 2]
            cnt = [0]

            def fill():
                cnt[0] += 1
                if cnt[0] % 2 == 0:
                    next(filler, None)
            for qb in range(4):
                for typ in range(2):
                    Ksrc, Vsrc = (KA, VA) if typ == 0 else (KS, VS)
                    kname = "K%d" % typ
                    base = 4 * typ
                    kl = []
                    if t < 8:
                        for j in range(4):
                            kl.append((j * 128, j, None, ["Kctx%d" % typ, "Vctx%d" % typ, "Vctx%da" % typ]))
                        if typ == 0:
                            for j in range(32):
                                kl.append((512 + j * 128, 4 + j, None, [(kname, j // 4)] + [("V%d_%d" % (typ, g_), 4 + j) for g_ in range(2)]))
                        else:
                            qbg = t * 4 + qb
                            for dj, mi in ((-1, 0), (0, None), (1, 1)):
                                j = qbg + dj
                                if 0 <= j < 32:
                                    kl.append((512 + j * 128, 4 + j, mi, [(kname, j // 4)] + [("V%d_%d" % (typ, g_), 4 + j) for g_ in range(2)]))
                    else:
                        sq_ = qb // 2
                        for j in range(2):
                            jj = 32 + sq_ * 2 + j
                            kl.append((512 + jj * 128, 4 + jj, None, [(kname, 8)] + [("V%d_%d" % (typ, g_), 4 + jj) for g_ in range(2)]))

                    def attn_pair(qb=qb, typ=typ, Ksrc=Ksrc, Vsrc=Vsrc, kname=kname, base=base, kl=kl):
                        nk_ = len(kl)
                        obs = [(PS[6], "PS6"), (PS[7], "PS7")]
                        pts = [None] * nk_

                        def isk(d_):
                            return (isinstance(d_, tuple) and d_[0] == kname) or (isinstance(d_, str) and d_.startswith("Kctx"))

                        def emit_s(j):
                            kcol, vt, mi, deps = kl[j]
                            pi = psP.next()[0]
                            keys = ["PS%d" % (2 * pi), "PS%d" % (2 * pi + 1)]
                            for g in range(2):
                                rows = slice(64 * g, 64 * g + 64)
                                S.op("pe", lambda e, g=g, rows=rows, kcol=kcol, pi=pi: e.matmul(PS[2 * pi + g].rearrange("p (h n) -> p h n", h=4), lhsT=Ksrc[rows, kcol:kcol + 128],
                                                                                              rhs=QT[rows, base:base + 4, qb * 128:(qb + 1) * 128], start=True, stop=True),
                                     reads=[d_ for d_ in deps if isk(d_)] + [("QT", t % 2, base + h_) for h_ in range(4)], writes=[keys[g]])
                            pt, ptk = PTrot.next()
                            S.op("act", lambda e, pi=pi, pt=pt: e.activation(out=pt, in_=PP[pi], func=AF.Exp, scale=0.125), reads=keys, writes=[ptk])
                            if mi is not None:
                                S.op("dve", lambda e, pt=pt, mi=mi: e.tensor_tensor(out=pt.rearrange("p (g n) -> p g n", g=2), in0=pt.rearrange("p (g n) -> p g n", g=2),
                                                                                 in1=masks[:, mi, :].unsqueeze(1).to_broadcast([128, 2, 512]), op=ALU.mult),
                                     reads=[ptk, "masks"], writes=[ptk])
                            pts[j] = (pt, ptk)

                        def emit_pv(j):
                            kcol, vt, mi, deps = kl[j]
                            pt, ptk = pts[j]
                            vdeps = [d_ for d_ in deps if not isk(d_)]
                            for g in range(2):
                                S.op("pe", lambda e, g=g, pt=pt, vt=vt, j=j: e.matmul(obs[g][0], lhsT=Vsrc[:, vt, 64 * g:64 * g + 128], rhs=pt[:, g * 512:(g + 1) * 512],
                                                                                   start=(j == 0), stop=(j == nk_ - 1)),
                                     reads=[ptk] + vdeps + ["V%sones" % ("A" if typ == 0 else "S")], writes=[obs[g][1]])
                        LA = 2
                        for j in range(min(LA, nk_)):
                            emit_s(j)
                        for j in range(nk_):
                            if j + LA < nk_:
                                emit_s(j + LA)
                            emit_pv(j)
                            fill()
                        rec, reck = recrot.next()
                        osb, osbk = osbrot.next()
                        for g in range(2):
                            rows = slice(64 * g, 64 * g + 64)
                            drows = slice(64 * (1 - g), 64 * (1 - g) + 64)
                            ob, obk = obs[g]
                            S.op("dve", lambda e, ob=ob, rows=rows, drows=drows: e.tensor_copy(out=rec[rows, :], in_=ob[drows, :]), reads=[obk], writes=[(reck, g)])
                            S.op("dve", lambda e, ob=ob, rows=rows: e.tensor_copy(out=osb[rows, :], in_=ob[rows, :]), reads=[obk], writes=[(osbk, g)])
                        if typ == 1:
                            S.op("dve", lambda e: e.tensor_tensor(out=rec.rearrange("p (h n) -> p h n", h=4), in0=rec.rearrange("p (h n) -> p h n", h=4),
                                                                  in1=esk2.unsqueeze(2).to_broadcast([128, 4, 128]), op=ALU.add),
                                 reads=[(reck, 0), (reck, 1), "esk2"], writes=[(reck, 0), (reck, 1)])
                        rec2, rec2k = rec2rot.next()
                        S.op("dve", lambda e: e.reciprocal(out=rec2, in_=rec), reads=[(reck, 0), (reck, 1)], writes=[rec2k])
                        S.op("pool", lambda e: e.tensor_tensor(out=OT[:, base:base + 4, qb * 128:(qb + 1) * 128],
                                                               in0=osb.rearrange("p (h n) -> p h n", h=4),
                                                               in1=rec2.rearrange("p (h n) -> p h n", h=4), op=ALU.mult),
                             reads=[(osbk, 0), (osbk, 1), rec2k], writes=[("OT", t % 2, typ, 0, qb), ("OT", t % 2, typ, 1, qb)])
                    attn_pair()

        def chain(*gens):
            for g_ in gens:
                if g_ is not None:
                    yield from g_

        for _ in front(0):
            pass
        for t in range(NT):
            filler = chain(back(t - 1) if t >= 1 else None, front(t + 1) if t + 1 < NT else None)
            attn(t, filler)
            for _ in filler:
                pass
        for _ in back(NT - 1):
            pass
        cur["rot"] = psrot
        S.barrier()

    def phase_mlp(l):
        Arena.top = PERSIST_TOP
        w1s = abf(8 * 4096).rearrange("p (k n) -> p k n", k=8)
        w2s = abf(32 * 1024).rearrange("p (k n) -> p k n", k=32)
        hT = abf(4096).rearrange("p (c n) -> p c n", c=8)
        aT = abf(32 * 512).rearrange("p (c n) -> p c n", c=32)
        xt_ = af32(4096).rearrange("p (c n) -> p c n", c=8)
        rrot = Rot([abf(512) for _ in range(2)], "rr")
        sqrot = Rot([abf(512) for _ in range(2)], "sq")
        rstd = af32(512)
        if l == 0:
            tmprot = Rot([af32(512) for _ in range(2)], "tmp")
            horot = Rot([abf(512) for _ in range(2)], "ho")
        else:
            ytrot = Rot([af32(1024) for _ in range(2)], "yt")
        load_w(w1s, None, 8, "w1s", col_groups=[(i * 1024, (i + 1) * 1024) for i in range(4)], pc="w1_%d" % l)
        load_w(w2s, None, 32, "w2s", per_k=True, pc="w2_%d" % l)
        hsrc = hTs[0] if l == 0 else hTs[2]
        xsrc = xT[1] if l == 0 else xT[3]
        xtk = ["xt_%d" % c for c in range(8)]
        hk = ["hT%d" % c for c in range(8)]

        def L1(t):
            S.op("sp", lambda e: e.dma_start(out=hT, in_=hsrc[t]), writes=hk, dma="ld_h")
            for f in range(32):
                pb, pk = bank()
                for k in range(8):
                    S.op("pe", lambda e, k=k, f=f, pb=pb: e.matmul(pb, lhsT=w1s[:, k, f * 128:(f + 1) * 128], rhs=hT[:, k, :], start=(k == 0), stop=(k == 7)),
                         reads=[hk[k], ("w1s", f // 8)], writes=[pk])
                r, rk_ = rrot.next()
                S.op("act", lambda e, pb=pb, r=r: e.activation(out=r, in_=pb, func=AF.Relu), reads=[pk], writes=[rk_])
                S.op("dve" if f % 2 == 0 else "pool", lambda e, r=r, f=f: e.tensor_tensor(out=aT[:, f, :], in0=r, in1=r, op=ALU.mult), reads=[rk_], writes=[("aT", f)])

        def L2(t):
            ci = tile_cond(t)
            S.op("sp", lambda e: e.dma_start(out=xt_, in_=xsrc[t]), writes=xtk, dma="ld_x")
            for m in range(8):
                pb, pk = bank()
                for f in range(32):
                    S.op("pe", lambda e, f=f, m=m, pb=pb: e.matmul(pb, lhsT=w2s[:, f, m * 128:(m + 1) * 128], rhs=aT[:, f, :], start=(f == 0), stop=(f == 31)),
                         reads=[("aT", f), ("w2s", f // 8)], writes=[pk])
                S.op("dve", lambda e, m=m, pb=pb: e.scalar_tensor_tensor(out=xt_[:, m, :], in0=pb, scalar=modv[:, l, 5, m, ci:ci + 1], in1=xt_[:, m, :],
                                                                       op0=ALU.mult, op1=ALU.add),
                     reads=[pk, xtk[m], "modv"], writes=[xtk[m]])

        def epi(t):
            ci = tile_cond(t)
            xch = [xt_[:, c, :] for c in range(8)]
            if l == 0:
                S.op("pool", lambda e: e.dma_start(out=xT[2][t], in_=xt_), reads=xtk, writes=[("xT2", t)], dma="st_x")
                rms_rstd(xch, xtk, 8, 1.0 / D, sqrot, rstd, "rstd")
                bufs = [horot.next() for _ in range(8)]

                def after(c):
                    S.op("pool", lambda e, c=c: e.dma_start(out=hTs[1][t][:, c, :], in_=bufs[c][0]), reads=[bufs[c][1]], writes=[("hTs1", t, c)], dma="st_" + bufs[c][1])
                modulate(xch, xtk, rstd, "rstd", 1, 0, ci, tmprot, [b_[0] for b_ in bufs], [b_[1] for b_ in bufs], after=after)
            else:
                rms_rstd(xch, xtk, 8, 1.0 / D, sqrot, rstd, "rstd")
                for c in range(8):
                    S.op("dve", lambda e, c=c: e.scalar_tensor_tensor(out=xt_[:, c, :], in0=xt_[:, c, :], scalar=gains_sb[:, 4, c:c + 1], in1=rstd, op0=ALU.mult, op1=ALU.mult),
                         reads=[xtk[c], "rstd", "gains"], writes=[xtk[c]])
                for s_ in range(4):
                    yt, ytk = ytrot.next()
                    for hf in range(2):
                        pb, pk = bank()
                        for cc in range(4):
                            c = 4 * hf + cc
                            S.op("pe", lambda e, c=c, cc=cc, pb=pb, s_=s_: e.transpose(pb[:, cc * 128:(cc + 1) * 128], xt_[:, c, s_ * 128:(s_ + 1) * 128], ident),
                                 reads=[xtk[c], "ident"], writes=[pk])
                        if hf == 0:
                            S.op("act", lambda e, pb=pb, yt=yt: e.copy(out=yt[:, 0:512], in_=pb), reads=[pk], writes=[ytk + "a"])
                        else:
                            S.op("dve", lambda e, pb=pb, yt=yt: e.tensor_copy(out=yt[:, 512:1024], in_=pb), reads=[pk], writes=[ytk + "b"])
                    r0 = t * 512 + s_ * 128
                    S.op("pool", lambda e, yt=yt, r0=r0: e.dma_start(out=y_out[r0:r0 + 128, :], in_=yt), reads=[ytk + "a", ytk + "b"],
                         writes=[("y", t, s_)], dma="st_" + ytk)

        L1(0)
        L2(0)
        for t in range(1, NT):
            L1(t)
            epi(t - 1)
            L2(t)
        epi(NT - 1)
        S.barrier()


    def phase_B1():
        Arena.top = PERSIST_TOP
        wi = abf(8 * 6144).rearrange("p (k n) -> p k n", k=8)
        hT_b = [abf(4096).rearrange("p (c n) -> p c n", c=8) for _ in range(2)]
        cs_b = [af32(2048).rearrange("p (a r n) -> p a r n", a=2, r=2)]
        kn_rot = Rot([af32(512) for _ in range(4)], "kn")
        knb_rot = Rot([abf(512) for _ in range(3)], "knb")
        t1_rot = Rot([af32(512) for _ in range(2)], "t1")
        qo_rot = Rot([abf(512) for _ in range(4)], "qo")
        go_rot = Rot([abf(512) for _ in range(4)], "go")
        ktok_b = [abf(4096).rearrange("p (s n) -> p s n", s=4)]
        vb_rot = Rot([abf(2048) for _ in range(2)], "vb")
        wi_groups = [(0, 1024), (1024, 2048), (2048, 3072), (3072, 4096), (4096, 5120), (5120, 6144)]
        load_w(wi, None, 8, "wi", col_groups=wi_groups, pc="w_in1")

        def b1_tile(t):
            hT, hk = hT_b[t % 2], ["hT%d_%d" % (t % 2, c) for c in range(8)]
            S.op("sp", lambda e: e.dma_start(out=hT, in_=hTs[1][t]), writes=hk, dma="ld_h%d" % (t % 2))
            cs_ = cs_b[0]
            if t < 8:
                for a_ in range(2):
                    S.op("sp", lambda e, a_=a_: e.dma_start(out=cs_[:, a_], in_=c_cs1[a_][:, :, t * 512:(t + 1) * 512]), writes=["cs"], dma="ld_cs")
            ktok = ktok_b[0]
            chunks = [(typ, qc) for typ in range(2) for qc in range(8)]
            st = {}

            def stage_proj(i):
                typ, qc = chunks[i]
                pb, pk = bank()
                c0 = typ * 1024 + qc * 128
                for k in range(8):
                    S.op("pe", lambda e, k=k, pb=pb, c0=c0: e.matmul(pb, lhsT=wi[:, k, c0:c0 + 128], rhs=hT[:, k, :], start=(k == 0), stop=(k == 7)),
                         reads=[hk[k], ("wi", c0 // 1024)], writes=[pk])
                qn, qnk = kn_rot.next()
                sc = 1.0 if typ == 0 else 1.0 / 16.0
                S.op("act", lambda e, pb=pb, qn=qn, sc=sc: e.activation(out=qn, in_=pb, func=AF.Identity, scale=sc), reads=[pk], writes=[qnk])
                st[i] = dict(qn=qn, qnk=qnk)
                if t < 8:
                    knb, knbk = knb_rot.next()
                    S.op("dve", lambda e, qn=qn, knb=knb: e.tensor_copy(out=knb, in_=qn), reads=[qnk], writes=[knbk])
                    st[i].update(knb=knb, knbk=knbk)

            def stage_rope(i):
                typ, qc = chunks[i]
                dc = qc % 2
                qn, qnk = st[i]["qn"], st[i]["qnk"]
                qo, qok = qo_rot.next()
                if t < 8:
                    knb, knbk = st[i]["knb"], st[i]["knbk"]
                    pb2, pk2 = bank()
                    S.op("pe", lambda e, pb2=pb2, knb=knb: e.matmul(pb2, lhsT=rot1_bf, rhs=knb, start=True, stop=True), reads=[knbk, "mats"], writes=[pk2])
                    t1, t1k = t1_rot.next()
                    S.op("dve", lambda e, t1=t1, pb2=pb2, dc=dc: e.tensor_tensor(out=t1, in0=pb2, in1=cs_[:, 1, dc, :], op=ALU.mult), reads=[pk2, "cs"], writes=[t1k])
                    S.op("pool", lambda e, qn=qn, dc=dc: e.tensor_tensor(out=qn, in0=qn, in1=cs_[:, 0, dc, :], op=ALU.mult), reads=[qnk, "cs"], writes=[qnk])
                    if typ == 0:
                        S.op("dve", lambda e, qn=qn, t1=t1, qo=qo: e.tensor_tensor(out=qo, in0=qn, in1=t1, op=ALU.add), reads=[qnk, t1k], writes=[qok])
                    else:
                        S.op("dve", lambda e, qn=qn, t1=t1: e.tensor_tensor(out=qn, in0=qn, in1=t1, op=ALU.add), reads=[qnk, t1k], writes=[qnk])
                        S.op("act", lambda e, qn=qn, qo=qo: e.copy(out=qo, in_=qn), reads=[qnk], writes=[qok])
                else:
                    S.op("act", lambda e, qn=qn, qo=qo: e.copy(out=qo, in_=qn), reads=[qnk], writes=[qok])
                dst = (qTs if typ == 0 else kTs)
                S.op("act", lambda e, qo=qo, dst=dst, qc=qc: e.dma_start(out=dst[t][qc], in_=qo), reads=[qok], writes=[("qk", typ, t, qc)], dma="st_" + qok)

            def stage_tr(i):
                typ, qc = chunks[i]
                if typ != 1:
                    return
                qn, qnk = st[i]["qn"], st[i]["qnk"]
                pb3, pk3 = bank()
                for s_ in range(4):
                    S.op("pe", lambda e, s_=s_, pb3=pb3, qn=qn: e.transpose(pb3[:, s_ * 128:(s_ + 1) * 128], qn[:, s_ * 128:(s_ + 1) * 128], ident),
                         reads=[qnk, "ident"], writes=[pk3])
                S.op("dve", lambda e, pb3=pb3, qc=qc: e.tensor_copy(out=ktok[:, :, qc * 128:(qc + 1) * 128], in_=pb3.rearrange("p (s n) -> p s n", s=4)),
                     reads=[pk3], writes=[("ktok", qc)])
            nch = len(chunks)
            for i in range(nch + 2):
                if i < nch:
                    stage_proj(i)
                if 0 <= i - 1 < nch:
                    stage_rope(i - 1)
                if 0 <= i - 2 < nch:
                    stage_tr(i - 2)
            S.op("act", lambda e: e.dma_start(out=kts[t], in_=ktok), reads=[("ktok", qc) for qc in range(8)], writes=[("kts", t)], dma="st_ktok")
            for s_ in range(4):
                vb, vbk = vb_rot.next()
                for vg in range(4):
                    pb, pk = bank()
                    for k in range(8):
                        S.op("pe", lambda e, k=k, pb=pb, s_=s_, vg=vg: e.matmul(pb, lhsT=hT[:, k, s_ * 128:(s_ + 1) * 128], rhs=wi[:, k, 2048 + vg * 512:2048 + (vg + 1) * 512],
                                                                            start=(k == 0), stop=(k == 7)),
                             reads=[hk[k], ("wi", 2 + vg // 2)], writes=[pk])
                    if vg % 2 == 0:
                        S.op("act", lambda e, pb=pb, vb=vb, vg=vg: e.copy(out=vb[:, vg * 512:(vg + 1) * 512], in_=pb), reads=[pk], writes=[(vbk, vg)])
                    else:
                        S.op("dve", lambda e, pb=pb, vb=vb, vg=vg: e.tensor_copy(out=vb[:, vg * 512:(vg + 1) * 512], in_=pb), reads=[pk], writes=[(vbk, vg)])
                S.op("act", lambda e, vb=vb, s_=s_: e.dma_start(out=vs_[t][s_], in_=vb), reads=[(vbk, vg) for vg in range(4)], writes=[("vs", t, s_)], dma="st_" + vbk)
            for gc in range(16):
                pb, pk = bank()
                c0 = 4096 + gc * 128
                for k in range(8):
                    S.op("pe", lambda e, k=k, pb=pb, c0=c0: e.matmul(pb, lhsT=wi[:, k, c0:c0 + 128], rhs=hT[:, k, :], start=(k == 0), stop=(k == 7)),
                         reads=[hk[k], ("wi", c0 // 1024)], writes=[pk])
                go, gok = go_rot.next()
                S.op("act", lambda e, pb=pb, go=go: e.activation(out=go, in_=pb, func=AF.Silu), reads=[pk], writes=[gok])
                S.op("act", lambda e, go=go, gc=gc: e.dma_start(out=gTs[t][gc], in_=go), reads=[gok], writes=[("gTs", t, gc)], dma="st_" + gok)
        for t in range(NT):
            b1_tile(t)
        S.barrier()

    def ret_tables():
        rt = af32(770)
        lg = af32(8)
        c128 = af32(1)
        decT = af32(8 * 128).rearrange("p (a n) -> p a n", a=8)
        qdec = af32(8 * 128).rearrange("p (a n) -> p a n", a=8)
        kdec = af32(8)
        cdec = af32(8)
        S.op("sp", lambda e: e.dma_start(out=rt, in_=c_ret), writes=["rt"], dma="c0")
        S.op("sp", lambda e: e.dma_start(out=lg, in_=decr), writes=["lg"], dma="c0")
        S.op("dve", lambda e: e.memset(c128, 128.0), writes=["c128"])
        S.op("act", lambda e: e.activation(out=lg, in_=lg, func=AF.Exp, scale=-1.0), reads=["lg"], writes=["lg"])
        S.op("dve", lambda e: e.tensor_scalar(out=lg, in0=lg, scalar1=1.0, scalar2=None, op0=ALU.add), reads=["lg"], writes=["lg"])
        S.op("act", lambda e: e.activation(out=lg, in_=lg, func=AF.Ln), reads=["lg"], writes=["lg"])
        S.op("dve", lambda e: e.tensor_scalar(out=lg, in0=lg, scalar1=-1.0, scalar2=None, op0=ALU.mult), reads=["lg"], writes=["lg"])
        for d in range(2):
            for h in range(4):
                a = 4 * d + h
                S.op("act", lambda e, a=a, d=d: e.activation(out=decT[:, a, :], in_=rt[:, d * 128:(d + 1) * 128], func=AF.Exp, scale=lg[:, a:a + 1]), reads=["rt", "lg"], writes=[("decT", a)])
                S.op("dve", lambda e, a=a, d=d: e.tensor_tensor(out=decT[:, a, :], in0=decT[:, a, :], in1=rt[:, 512 + d * 128:512 + (d + 1) * 128], op=ALU.mult),
                     reads=[("decT", a), "rt"], writes=[("decT", a)])
                S.op("act", lambda e, a=a, d=d: e.activation(out=qdec[:, a, :], in_=rt[:, 256 + d * 128:256 + (d + 1) * 128], func=AF.Exp, scale=lg[:, a:a + 1]), reads=["rt", "lg"], writes=[("qdec", a)])
                S.op("act", lambda e, a=a, d=d: e.activation(out=kdec[:, a:a + 1], in_=rt[:, 768 + d:769 + d], func=AF.Exp, scale=lg[:, a:a + 1]), reads=["rt", "lg"], writes=[("kdec", a)])
                S.op("act", lambda e, a=a: e.activation(out=cdec[:, a:a + 1], in_=c128, func=AF.Exp, scale=lg[:, a:a + 1]), reads=["c128", "lg"], writes=[("cdec", a)])
        return decT, qdec, kdec, cdec

    def phase_scan(d):
        Arena.top = PERSIST_TOP
        decT, qdec, kdec, cdec = ret_tables()
        S32 = af32(4096).rearrange("p (h c e) -> p h c e", h=4, c=2)
        Sbf = [abf(4096).rearrange("p (h c e) -> p h c e", h=4, c=2) for _ in range(2)]
        qT_b = [abf(4096).rearrange("p (c n) -> p c n", c=8) for _ in range(2)]
        kT_b = [abf(4096).rearrange("p (c n) -> p c n", c=8) for _ in range(2)]
        kt_b = [abf(4096).rearrange("p (s n) -> p s n", s=4) for _ in range(2)]
        v_b = [abf(8192).rearrange("p (s n) -> p s n", s=4) for _ in range(2)]
        attm_rot = Rot([abf(128) for _ in range(4)], "attm")
        qs_rot = Rot([abf(256).rearrange("p (c n) -> p c n", c=2) for _ in range(4)], "qs")
        kf_rot = Rot([abf(256) for _ in range(4)], "kf")
        if d == 0:
            of_rot = Rot([af32(2048).rearrange("p (c n) -> p c n", c=16) for _ in range(2)], "ofst")
        else:
            gT = abf(16 * 512).rearrange("p (c n) -> p c n", c=16)
            of_rot = Rot([af32(2048).rearrange("p (c n) -> p c n", c=16) for _ in range(2)], "ofld")
            osum_b = [af32(2048).rearrange("p (c n) -> p c n", c=16) for _ in range(2)]
            pending = []
            ocnt = [0]
            sq4 = [abf(512).rearrange("p (c n) -> p c n", c=4) for _ in range(4)]
            rs_rot = Rot([af32(128) for _ in range(4)], "rsh")
            tmp_rot = Rot([af32(512).rearrange("p (c n) -> p c n", c=4) for _ in range(4)], "gtmp")
            u_rot = Rot([abf(2048).rearrange("p (c n) -> p c n", c=16) for _ in range(2)], "ust")
        sidx = [0]

        def scan_tile(t, n):
            b = n % 2
            qT, kT, kt, v = qT_b[b], kT_b[b], kt_b[b], v_b[b]
            S.op("sp", lambda e: e.dma_start(out=qT, in_=qTs[t].rearrange("c p n -> p c n")), writes=["qT%d" % b], dma="ld_q%d" % b)
            S.op("sp", lambda e: e.dma_start(out=kT, in_=kTs[t].rearrange("c p n -> p c n")), writes=["kT%d" % b], dma="ld_k%d" % b)
            S.op("sp", lambda e: e.dma_start(out=kt, in_=kts[t]), writes=["kt%d" % b], dma="ld_kt%d" % b)
            S.op("sp", lambda e: e.dma_start(out=v, in_=vs_[t].rearrange("s p n -> p s n")), writes=["v%d" % b], dma="ld_v%d" % b)
            if d == 1:
                while pending:
                    pending.pop(0)()
                S.op("sp", lambda e: e.dma_start(out=gT, in_=gTs[t].rearrange("c p n -> p c n")), writes=["gT"], dma="ld_g")
                S.op("pool", lambda e: e.tensor_tensor(out=gT, in0=gT, in1=gng_sb.unsqueeze(2).to_broadcast([128, 16, 512]), op=ALU.mult), reads=["gT", "gng"], writes=["gT"])
            if t < 8:
                seqs = [([0, 1, 2, 3] if d == 0 else [3, 2, 1, 0], None)]
            else:
                seqs = [([0, 1] if d == 0 else [1, 0], 0), ([2, 3] if d == 0 else [3, 2], 1)]
            def chunk(order, pseq, ci_, s_):
                if True:
                    first = (t == (0 if d == 0 else 7) and ci_ == 0) if t < 8 else (ci_ == 0)
                    has_state = True if t < 8 else (ci_ > 0)
                    last_sample = (t == (7 if d == 0 else 0)) and ci_ == len(order) - 1 and t < 8
                    if t < 8 and first:
                        S.op("sp", lambda e: e.dma_start(out=S32, in_=s0[d].rearrange("h (c p) e -> p h c e", p=128)), writes=[("S32", h, c) for h in range(4) for c in range(2)], dma="ld_s0")
                        nb = Sbf[sidx[0] % 2]
                        for h in range(4):
                            S.op("act", lambda e, h=h, nb=nb: e.copy(out=nb[:, h], in_=S32[:, h]), reads=[("S32", h, 0), ("S32", h, 1)], writes=[("Sbf", sidx[0] % 2, h)])
                    curS = Sbf[sidx[0] % 2]
                    curk = sidx[0] % 2
                    nxtS = Sbf[(sidx[0] + 1) % 2]
                    nxtk = (sidx[0] + 1) % 2
                    sidx[0] += 1
                    cols = slice(s_ * 128, (s_ + 1) * 128)
                    per_h = []
                    for h in range(4):
                        a = 4 * d + h
                        pb, pk = bank()
                        for dc in range(2):
                            S.op("pe", lambda e, pb=pb, h=h, dc=dc: e.matmul(pb[:, 0:128], lhsT=kT[:, 2 * h + dc, cols], rhs=qT[:, 2 * h + dc, cols], start=(dc == 0), stop=(dc == 1)),
                                 reads=["kT%d" % b, "qT%d" % b], writes=[pk])
                        am, amk = attm_rot.next()
                        S.op("dve", lambda e, pb=pb, am=am, a=a: e.tensor_tensor(out=am, in0=pb[:, 0:128], in1=decT[:, a, :], op=ALU.mult), reads=[pk, ("decT", a)], writes=[amk])
                        qs, qsk = qs_rot.next()
                        if has_state:
                            S.op("pool", lambda e, qs=qs, h=h, a=a: e.tensor_tensor(out=qs, in0=qT[:, 2 * h:2 * h + 2, cols], in1=qdec[:, a, :].unsqueeze(1).to_broadcast([128, 2, 128]), op=ALU.mult),
                                 reads=["qT%d" % b, ("qdec", a)], writes=[qsk])
                        kf, kfk = kf_rot.next()
                        if not last_sample:
                            S.op("act", lambda e, kf=kf, h=h, a=a: e.activation(out=kf, in_=kt[:, s_, h * 256:(h + 1) * 256], func=AF.Identity, scale=kdec[:, a:a + 1]),
                                 reads=["kt%d" % b, ("kdec", a)], writes=[kfk])
                        per_h.append((am, amk, qs, qsk, kf, kfk))
                    if d == 0:
                        ost, ostk = of_rot.next()
                    else:
                        ob_i = ocnt[0] % 2
                        ocnt[0] += 1
                        osum = osum_b[ob_i]
                        ofl, oflk = of_rot.next()
                        S.op("sp", lambda e, ofl=ofl: e.dma_start(out=ofl, in_=ofs[t][s_]), writes=[oflk], dma="ld_" + oflk)
                    for h in range(4):
                        am, amk, qs, qsk, kf, kfk = per_h[h]
                        po, pok = bank()
                        for ec in range(4):
                            S.op("pe", lambda e, po=po, ec=ec, h=h, am=am: e.matmul(po[:, ec * 128:(ec + 1) * 128], lhsT=v[:, s_, h * 512 + ec * 128:h * 512 + (ec + 1) * 128], rhs=am,
                                                                                start=True, stop=(not has_state)),
                                 reads=["v%d" % b, amk], writes=[pok])
                            if has_state:
                                for dc in range(2):
                                    S.op("pe", lambda e, po=po, ec=ec, h=h, dc=dc, qs=qs: e.matmul(po[:, ec * 128:(ec + 1) * 128], lhsT=curS[:, h, dc, ec * 128:(ec + 1) * 128], rhs=qs[:, dc, :],
                                                                                              start=False, stop=(dc == 1)),
                                         reads=[("Sbf", curk, h), qsk], writes=[pok])
                        pov = po.rearrange("p (c n) -> p c n", c=4)
                        if d == 0:
                            S.op("act", lambda e, pov=pov, ost=ost, h=h: e.copy(out=ost[:, 4 * h:4 * h + 4, :], in_=pov), reads=[pok], writes=[(ostk, h)])
                        else:
                            S.op("dve", lambda e, pov=pov, ofl=ofl, h=h: e.tensor_tensor(out=osum[:, 4 * h:4 * h + 4, :], in0=pov, in1=ofl[:, 4 * h:4 * h + 4, :], op=ALU.add),
                                 reads=[pok, oflk], writes=[("osum", ob_i, h)])
                    if d == 0:
                        S.op("act", lambda e, ost=ost: e.dma_start(out=ofs[t][s_], in_=ost), reads=[(ostk, h) for h in range(4)], writes=[("ofs", t, s_)], dma="st_" + ostk)
                    else:
                        while pending:
                            pending.pop(0)()
                    if not last_sample:
                        for h in range(4):
                            a = 4 * d + h
                            am, amk, qs, qsk, kf, kfk = per_h[h]
                            for dc in range(2):
                                pb, pk = bank()
                                S.op("pe", lambda e, pb=pb, kf=kf, h=h, dc=dc: e.matmul(pb, lhsT=kf[:, dc * 128:(dc + 1) * 128], rhs=v[:, s_, h * 512:(h + 1) * 512], start=True, stop=True),
                                     reads=[kfk, "v%d" % b], writes=[pk])
                                if has_state:
                                    S.op("dve", lambda e, pb=pb, h=h, dc=dc, a=a: e.scalar_tensor_tensor(out=S32[:, h, dc, :], in0=S32[:, h, dc, :], scalar=cdec[:, a:a + 1], in1=pb, op0=ALU.mult, op1=ALU.add),
                                         reads=[pk, ("S32", h, dc), ("cdec", a)], writes=[("S32", h, dc)])
                                else:
                                    S.op("dve", lambda e, pb=pb, h=h, dc=dc: e.tensor_copy(out=S32[:, h, dc, :], in_=pb), reads=[pk], writes=[("S32", h, dc)])
                            S.op("act", lambda e, h=h, nxtS=nxtS: e.copy(out=nxtS[:, h], in_=S32[:, h]), reads=[("S32", h, 0), ("S32", h, 1)], writes=[("Sbf", nxtk, h)])
                    if t == 8 and ci_ == len(order) - 1:
                        S.op("pool", lambda e, pseq=pseq: e.dma_start(out=ns[d][pseq].rearrange("h (c p) e -> p h c e", p=128), in_=S32),
                             reads=[("S32", h, c) for h in range(4) for c in range(2)], writes=[("ns", d, pseq)], dma="st_ns")
                    def gn_emit():
                        ust, ustk = u_rot.next()
                        sqs = []
                        for h in range(4):
                            sqh, sqk = sq4[h], "gsq%d" % h
                            S.op("act", lambda e, h=h, sqh=sqh: e.activation(out=sqh, in_=osum[:, 4 * h:4 * h + 4, :], func=AF.Square), reads=[("osum", ob_i, h)], writes=[sqk])
                            sqs.append((sqh, sqk))
                        pbs = []
                        for h in range(4):
                            sqh, sqk = sqs[h]
                            pb, pk = bank()
                            for ec in range(4):
                                S.op("pe", lambda e, pb=pb, ec=ec, sqh=sqh: e.matmul(pb[:, 0:128], lhsT=ones_bf, rhs=sqh[:, ec, :], start=(ec == 0), stop=(ec == 3)), reads=[sqk, "mats"], writes=[pk])
                            pbs.append((pb, pk))
                        rss = []
                        for h in range(4):
                            pb, pk = pbs[h]
                            rs, rsk = rs_rot.next()
                            S.op("act", lambda e, pb=pb, rs=rs: e.activation(out=rs, in_=pb[:, 0:128], func=AF.Ln, scale=1.0 / 512, bias=EPS), reads=[pk], writes=[rsk])
                            S.op("act", lambda e, rs=rs: e.activation(out=rs, in_=rs, func=AF.Exp, scale=-0.5), reads=[rsk], writes=[rsk])
                            rss.append((rs, rsk))
                        for h in range(4):
                            rs, rsk = rss[h]
                            tp, tpk = tmp_rot.next()
                            S.op("dve", lambda e, tp=tp, rs=rs, h=h: e.tensor_tensor(out=tp, in0=osum[:, 4 * h:4 * h + 4, :], in1=rs.unsqueeze(1).to_broadcast([128, 4, 128]), op=ALU.mult),
                                 reads=[("osum", ob_i, h), rsk], writes=[tpk])
                            S.op("dve", lambda e, tp=tp, ust=ust, h=h: e.tensor_tensor(out=ust[:, 4 * h:4 * h + 4, :], in0=tp, in1=gT[:, 4 * h:4 * h + 4, cols], op=ALU.mult),
                                 reads=[tpk, "gT"], writes=[(ustk, h)])
                        S.op("act", lambda e, ust=ust: e.dma_start(out=uTs[t][s_], in_=ust), reads=[(ustk, h) for h in range(4)], writes=[("uTs", t, s_)], dma="st_" + ustk)
                    if d == 1:
                        pending.append(gn_emit)
            for order, pseq in seqs:
                for ci_, s_ in enumerate(order):
                    chunk(order, pseq, ci_, s_)

        order_t = list(range(8)) if d == 0 else list(range(7, -1, -1))
        for n, t in enumerate(order_t + [8]):
            scan_tile(t, n)
        if d == 1:
            while pending:
                pending.pop(0)()
        S.barrier()

    def phase_B2c():
        Arena.top = PERSIST_TOP
        wo = abf(16 * 1024).rearrange("p (k n) -> p k n", k=16)
        uT_b = [abf(16 * 512).rearrange("p (c n) -> p c n", c=16) for _ in range(2)]
        xt_b = [af32(4096).rearrange("p (c n) -> p c n", c=8) for _ in range(2)]
        hout = abf(4096).rearrange("p (c n) -> p c n", c=8)
        sqrot = Rot([abf(512) for _ in range(2)], "sq")
        rstd = af32(512)
        tmprot = Rot([af32(512) for _ in range(2)], "tmp")
        load_w(wo, None, 16, "wo1", pc="w_out1")

        def c_tile(t):
            ci = tile_cond(t)
            b = t % 2
            uT, xt_ = uT_b[b], xt_b[b]
            xtk = ["xt%d_%d" % (b, c) for c in range(8)]
            for s_ in range(4):
                S.op("sp", lambda e, s_=s_: e.dma_start(out=uT[:, :, s_ * 128:(s_ + 1) * 128], in_=uTs[t][s_]), writes=[("uT", b, s_)], dma="ld_u%d" % b)
            S.op("sp", lambda e: e.dma_start(out=xt_, in_=xT[2][t]), writes=xtk, dma="ld_xt%d" % b)
            for m in range(8):
                pb, pk = bank()
                for ch in range(16):
                    S.op("pe", lambda e, ch=ch, m=m, pb=pb: e.matmul(pb, lhsT=wo[:, ch, m * 128:(m + 1) * 128], rhs=uT[:, ch, :], start=(ch == 0), stop=(ch == 15)),
                         reads=[("uT", b, s_) for s_ in range(4)] + ["wo1"], writes=[pk])
                S.op("dve", lambda e, m=m, pb=pb: e.scalar_tensor_tensor(out=xt_[:, m, :], in0=pb, scalar=modv[:, 1, 2, m, ci:ci + 1], in1=xt_[:, m, :], op0=ALU.mult, op1=ALU.add),
                     reads=[pk, xtk[m], "modv"], writes=[xtk[m]])
            S.op("pool", lambda e: e.dma_start(out=xT[3][t], in_=xt_), reads=xtk, writes=[("xT3", t)], dma="st_xt%d" % b)
            xch = [xt_[:, c, :] for c in range(8)]
            rms_rstd(xch, xtk, 8, 1.0 / D, sqrot, rstd, "rstd")
            hok = ["ho%d" % c for c in range(8)]
            modulate(xch, xtk, rstd, "rstd", 1, 3, ci, tmprot, [hout[:, c, :] for c in range(8)], hok)
            S.op("pool", lambda e: e.dma_start(out=hTs[2][t], in_=hout), reads=hok, writes=[("hTs2", t)], dma="st_ho")
        for t in range(NT):
            c_tile(t)
        S.barrier()

    allp = [("mod", phase_mod), ("A1", phase_A1), ("A2", phase_A2), ("A3", lambda: phase_mlp(0)), ("B1", phase_B1), ("B2f", lambda: phase_scan(0)),
            ("B2b", lambda: phase_scan(1)), ("B2c", phase_B2c), ("B3", lambda: phase_mlp(1))]
    for nm, fnp in allp:
        if phases is None or nm in phases:
            fnp()
    S.emit()
    return nc, es


def _prep_inputs(inp, b, consts):
    f = lambda a: np.ascontiguousarray(np.asarray(a, np.float32))
    m = {}
    m["xin"] = f(np.concatenate([inp["x_sample"][b], inp["x_prompt"][2 * b].reshape(256, D), inp["x_prompt"][2 * b + 1].reshape(256, D)], 0))
    m["cond"] = f(np.stack([_fm(inp["c"][b], 8), _fm(inp["c_ctx"], 8)], -1))
    m["kctx"] = f(np.stack([inp["cache_l0_attn_k"][b].reshape(512, 128), inp["cache_l0_swa_k"][b].reshape(512, 128)]))
    m["vctx"] = f(np.stack([inp["cache_l0_attn_v"][b].reshape(512, 128), inp["cache_l0_swa_v"][b].reshape(512, 128)]))
    m["s0"] = f(np.stack([inp["state_l1_ret_fwd"][b], inp["state_l1_ret_bwd"][b]]))
    m["adaw0"] = f(inp["l0_ada_w"])
    m["adaw1"] = f(inp["l1_ada_w"])
    m["adab"] = f(np.stack([_fm(inp["l0_ada_b"], 48), _fm(inp["l1_ada_b"], 48)], 1))
    m["gains"] = f(np.stack([_fm(inp[k], 8) for k in ("l0_norm_mix", "l0_norm_mlp", "l1_norm_mix", "l1_norm_mlp", "final_norm")], 1))
    m["qkg"] = f(np.stack([np.tile(inp["l0_q_norm"], 2), np.tile(inp["l0_k_norm"], 2)], -1))
    m["sinkr"] = f(np.tile(np.asarray(inp["l0_sink"])[None, :], (128, 1)))
    m["decr"] = f(np.tile(np.concatenate([inp["l1_ret_decay_fwd"], inp["l1_ret_decay_bwd"]])[None, :], (128, 1)))
    m["gng"] = f(_fm(np.asarray(inp["l1_ret_gn"]).reshape(-1), 16))
    w = np.asarray(inp["l0_w_in"], np.float32)
    cols = []
    for base in (0, 768):
        for c in range(4):
            cols += list(range(base + c * 64, base + (c + 1) * 64)) + list(range(base + (4 + c) * 64, base + (5 + c) * 64))
    cols += list(range(512, 640)) + list(range(1280, 1408)) + list(range(640, 768)) + list(range(1408, 1536))
    m["w_in0"] = f(w[:, cols])
    rows = []
    for base in (0, 512):
        for c in range(4):
            rows += list(range(base + c * 64, base + (c + 1) * 64)) + list(range(base + (4 + c) * 64, base + (5 + c) * 64))
    m["w_out0"] = f(np.asarray(inp["l0_w_out"], np.float32)[rows, :])
    m["w1_0"] = f(inp["l0_mlp_w1"])
    m["w2_0"] = f(inp["l0_mlp_w2"])
    m["w1_1"] = f(inp["l1_mlp_w1"])
    m["w2_1"] = f(inp["l1_mlp_w2"])
    m["w_in1"] = f(inp["l1_w_in"])
    m["w_out1"] = f(inp["l1_w_out"])
    m.update(consts)
    return m


_CACHE = {}


def kernel(**inputs):
    inp = {k: np.asarray(v) for k, v in inputs.items()}
    consts = _consts()
    if "nc" not in _CACHE:
        _CACHE["nc"] = build()
    nc, _es = _CACHE["nc"]
    in_maps = [_prep_inputs(inp, b, consts) for b in range(8)]
    r = run_bass_kernel_spmd(nc, in_maps, core_ids=list(range(8)))
    outs = r.results
    y_prompt = np.zeros((16, 256, D), np.float32)
    y_sample = np.zeros((8, 4096, D), np.float32)
    nk = [np.zeros((16, 256, 2, 64), np.float32) for _ in range(2)]
    nv = [np.zeros((16, 256, 2, 64), np.float32) for _ in range(2)]
    ns = [np.zeros((16, 4, 256, 512), np.float32) for _ in range(2)]
    for b in range(8):
        o = outs[b]
        y_sample[b] = o["y_out"][:4096]
        y_prompt[2 * b] = o["y_out"][4096:4352]
        y_prompt[2 * b + 1] = o["y_out"][4352:4608]
        for a in range(2):
            nk[a][2 * b:2 * b + 2] = o["nk"][a].reshape(2, 256, 2, 64)
            nv[a][2 * b:2 * b + 2] = o["nv"][a].reshape(2, 256, 2, 64)
            ns[a][2 * b:2 * b + 2] = o["ns"][a]
    return (y_prompt, y_sample, nk[0], nv[0], nk[1], nv[1], ns[0], ns[1])
```

```python
import numpy as np
from contextlib import ExitStack
import concourse.bass as bass
import concourse.mybir as mybir
from concourse.bass_utils import run_bass_kernel_spmd

F32 = mybir.dt.float32
BF16 = mybir.dt.bfloat16
AF = mybir.ActivationFunctionType
ALU = mybir.AluOpType

D = 1024
NT = 9
TT = 512
EPS = 1e-6
SBW = 53000


class Op:
    __slots__ = ("eng", "fn", "deps", "dma_key", "signal", "sigval", "idx", "line")


class Sched:
    ENGS = ("pe", "act", "dve", "pool", "sp")

    def __init__(self, nc, es):
        self.nc = nc
        self.es = es
        self.ops = {e: [] for e in self.ENGS}
        self.lastw = {}
        self.readers = {}
        self.sem = {e: es.enter_context(nc.semaphore("d_" + e)) for e in self.ENGS}
        self.dsem = {}
        self.dcount = {}
        self.all_ops = []
        self.barrier_ops = None

    def op(self, eng, fn, reads=(), writes=(), dma=None):
        o = Op()
        import sys as _sys
        o.line = _sys._getframe(1).f_lineno
        o.eng = eng
        o.fn = fn
        o.dma_key = dma
        o.signal = False
        o.sigval = None
        writes = list(writes) + [r for r in reads if isinstance(r, str) and r.startswith("PS")]
        reads = [r for r in reads if not (isinstance(r, str) and r.startswith("PS"))]
        deps = set()
        for r in reads:
            w = self.lastw.get(r)
            if w is not None:
                deps.add(w)
        for w_ in writes:
            w = self.lastw.get(w_)
            if w is not None:
                deps.add(w)
            for rd in self.readers.get(w_, ()):
                deps.add(rd)
        if self.barrier_ops:
            deps.update(self.barrier_ops)
        deps.discard(o)
        o.deps = deps
        o.idx = len(self.ops[eng])
        self.ops[eng].append(o)
        self.all_ops.append(o)
        for r in reads:
            self.readers.setdefault(r, []).append(o)
        for w_ in writes:
            self.lastw[w_] = o
            self.readers[w_] = []
        if dma is not None and dma not in self.dsem:
            self.dsem[dma] = self.es.enter_context(self.nc.semaphore("q_" + dma))
            self.dcount[dma] = 0
        return o

    def barrier(self):
        print("BARRIER", {e: len(self.ops[e]) for e in self.ENGS}, "ARENA", getattr(self, "arena_top", None))
        last = set()
        for e in self.ENGS:
            if self.ops[e]:
                last.add(self.ops[e][-1])
        lastdma = {}
        for o in self.all_ops:
            if o.dma_key is not None:
                lastdma[o.dma_key] = o
        last.update(lastdma.values())
        self.barrier_ops = last
        self.lastw = {}
        self.readers = {}

    def emit(self):
        import os
        lim = int(os.environ.get("SCHED_LIMIT", "0"))
        if lim:
            keep = set(id(o) for o in self.all_ops[:lim])
            self.all_ops = self.all_ops[:lim]
            for e in self.ENGS:
                self.ops[e] = [o for o in self.ops[e] if id(o) in keep]
        print("SCHED ops:", len(self.all_ops), {e: len(self.ops[e]) for e in self.ENGS})
        if os.environ.get("SCHED_DUMP"):
            a, b = [int(x) for x in os.environ["SCHED_DUMP"].split(":")]
            for i, o in enumerate(self.all_ops[a:b]):
                print("OP", a + i, o.eng, o.line, o.dma_key)
        for o in self.all_ops:
            for d in o.deps:
                if d.eng == "pe" and o.eng == "pe" and d.dma_key is None:
                    continue
                d.signal = True
        cnt = {e: 0 for e in self.ENGS}
        for e in self.ENGS:
            for o in self.ops[e]:
                if o.dma_key is not None:
                    self.dcount[o.dma_key] += 16
                    o.sigval = (self.dsem[o.dma_key], self.dcount[o.dma_key])
                elif o.signal:
                    cnt[e] += 1
                    o.sigval = (self.sem[e], cnt[e])
        engobj = {"pe": None, "act": None, "dve": None, "pool": None, "sp": None}
        block = self.es.enter_context(self.nc.Block())

        def run(ename):
            def body(eng):
                waited = {}
                for o in self.ops[ename]:
                    need = {}
                    for d in o.deps:
                        if d.eng == "pe" and ename == "pe" and d.dma_key is None:
                            continue
                        s, v = d.sigval
                        k = id(s)
                        if waited.get(k, 0) >= v:
                            continue
                        if k not in need or need[k][1] < v:
                            need[k] = (s, v)
                    for k, (s, v) in need.items():
                        eng.wait_ge(s, v)
                        waited[k] = v
                    ins = o.fn(eng)
                    if o.dma_key is not None:
                        ins.then_inc(o.sigval[0], 16)
                    elif o.signal:
                        ins.then_inc(o.sigval[0], 1)
                if ename == "sp":
                    for key, s in self.dsem.items():
                        if self.dcount[key] > 0:
                            eng.wait_ge(s, self.dcount[key])
            return body

        block.tensor(run("pe"))
        block.scalar(run("act"))
        block.vector(run("dve"))
        block.gpsimd(run("pool"))
        block.sync(run("sp"))


def _consts():
    c = {}
    c["c_ident"] = np.eye(128, dtype=np.float32)
    mats = np.zeros((4, 128, 128), np.float32)
    mats[0] = 1.0
    mats[1, :64, :64] = 1.0
    mats[1, 64:, 64:] = 1.0
    for hb in (0, 64):
        for off in (0, 32):
            for i in range(16):
                mats[2, hb + off + 16 + i, hb + off + i] = -1.0
                mats[2, hb + off + i, hb + off + 16 + i] = 1.0
    for i in range(64):
        mats[3, 64 + i, i] = -1.0
        mats[3, i, 64 + i] = 1.0
    c["c_mats"] = mats
    t = np.arange(4096)
    row = (t // 64).astype(np.float32)
    col = (t % 64).astype(np.float32)
    inv0 = (10000.0 ** (-np.arange(16, dtype=np.float32) / 16)).astype(np.float32)
    ang = np.zeros((64, 4096), np.float32)
    for d in range(64):
        pos = row if d < 32 else col
        ang[d] = pos * inv0[d % 16]
    ang = np.concatenate([ang, ang], 0)
    c["c_cs0"] = np.stack([np.cos(ang), np.sin(ang)]).astype(np.float32)
    inv1 = (10000.0 ** (-np.arange(64, dtype=np.float32) / 64)).astype(np.float32)
    ang1 = np.zeros((128, 2, 4096), np.float32)
    for p in range(128):
        ang1[p, 0] = row * inv1[p % 64]
        ang1[p, 1] = col * inv1[p % 64]
    c["c_cs1"] = np.stack([np.cos(ang1), np.sin(ang1)]).astype(np.float32)
    kp = np.arange(128)[:, None]
    qf = np.arange(128)[None, :]
    m1 = (qf <= kp).astype(np.float32)
    m2 = (kp <= qf).astype(np.float32)
    c["c_mask"] = np.stack([np.tile(m1, (1, 4)), np.tile(m2, (1, 4))]).astype(np.float32)
    j = np.arange(128, dtype=np.float32)[:, None]
    i = np.arange(128, dtype=np.float32)[None, :]
    ret = np.zeros((128, 6 * 128 + 2), np.float32)
    ret[:, 0:128] = np.maximum(i - j, 0)
    ret[:, 128:256] = np.maximum(j - i, 0)
    ret[:, 256:384] = (i + 1) + 0 * j
    ret[:, 384:512] = (128 - i) + 0 * j
    ret[:, 512:640] = (i >= j)
    ret[:, 640:768] = (j >= i)
    ret[:, 768] = 127 - np.arange(128)
    ret[:, 769] = np.arange(128)
    c["c_ret"] = ret
    return c


def _fm(v, k):
    return np.ascontiguousarray(np.asarray(v, np.float32).reshape(k, 128).T)


def build(debug=False, phases=None):
    nc = bass.Bass("TRN2", target_bir_lowering=False)
    es = ExitStack()

    def din(name, shape, dt=F32):
        return nc.dram_tensor(name, list(shape), dt, kind="ExternalInput").ap()

    def dout(name, shape, dt=F32):
        return nc.dram_tensor(name, list(shape), dt, kind="ExternalOutput").ap()

    def dscr(name, shape, dt=F32):
        kind = "ExternalOutput" if debug else "Internal"
        return nc.dram_tensor(name, list(shape), dt, kind=kind).ap()

    xin = din("xin", [NT * TT, D])
    cond = din("cond", [128, 8, 2])
    kctx = din("kctx", [2, 512, 128])
    vctx = din("vctx", [2, 512, 128])
    s0 = din("s0", [2, 4, 256, 512])
    adaw = [din("adaw0", [D, 6 * D]), din("adaw1", [D, 6 * D])]
    adab = din("adab", [128, 2, 48])
    gains = din("gains", [128, 5, 8])
    qkg = din("qkg", [128, 2])
    sinkr = din("sinkr", [128, 8])
    decr = din("decr", [128, 8])
    gng = din("gng", [128, 16])
    w_in0 = din("w_in0", [D, 1536])
    w_out0 = din("w_out0", [D, D])
    w1 = [din("w1_0", [D, 4 * D]), din("w1_1", [D, 4 * D])]
    w2 = [din("w2_0", [4 * D, D]), din("w2_1", [4 * D, D])]
    w_in1 = din("w_in1", [D, 6 * D])
    w_out1 = din("w_out1", [2 * D, D])
    c_ident = din("c_ident", [128, 128])
    c_mats = din("c_mats", [4, 128, 128])
    c_cs0 = din("c_cs0", [2, 128, 4096])
    c_cs1 = din("c_cs1", [2, 128, 2, 4096])
    c_mask = din("c_mask", [2, 128, 512])
    c_ret = din("c_ret", [128, 770])

    y_out = dout("y_out", [NT * TT, D])
    nk = dout("nk", [2, 512, 128])
    nv = dout("nv", [2, 512, 128])
    ns = dout("ns", [2, 2, 4, 256, 512])

    xT = [dscr("xT%d" % i, [NT, 128, 8, TT]) for i in range(4)]
    hTs = [dscr("hT%d" % i, [NT, 128, 8, TT], BF16) for i in range(3)]
    qTs = dscr("qTs", [NT, 8, 128, TT], BF16)
    kTs = dscr("kTs", [NT, 8, 128, TT], BF16)
    kts = dscr("kts", [NT, 128, 4, 1024], BF16)
    vs_ = dscr("vs", [NT, 4, 128, 2048], BF16)
    gTs = dscr("gTs", [NT, 16, 128, TT], BF16)
    ofs = dscr("ofs", [NT, 4, 128, 16, 128])
    uTs = dscr("uTs", [NT, 4, 128, 16, 128], BF16)

    wbf = {"w1_0": dscr("w1_0_bf", [D, 4 * D], BF16), "w2_0": dscr("w2_0_bf", [4 * D, D], BF16), "w_in1": dscr("w_in1_bf", [D, 6 * D], BF16),
           "w_out1": dscr("w_out1_bf", [2 * D, D], BF16), "w1_1": dscr("w1_1_bf", [D, 4 * D], BF16), "w2_1": dscr("w2_1_bf", [4 * D, D], BF16),
           "wq": dscr("wq_bf", [D, D], BF16), "wo": dscr("wo_bf", [D, D], BF16)}
    wsrc = {"w1_0": w1[0], "w2_0": w2[0], "w_in1": w_in1, "w_out1": w_out1, "w1_1": w1[1], "w2_1": w2[1], "wq": w_in0[:, 0:1024], "wo": w_out0}

    big = es.enter_context(nc.sbuf_tensor("big", [128, SBW], F32))
    PP = [es.enter_context(nc.psum_tensor("pp%d" % i, [128, 1024], F32))[:] for i in range(4)]
    PS = [PP[i // 2][:, (i % 2) * 512:(i % 2) * 512 + 512] for i in range(8)]
    S = Sched(nc, es)

    class Arena:
        top = 0

    def af32(n):
        a = big[:, Arena.top:Arena.top + n]
        Arena.top += n
        S.arena_top = Arena.top
        assert Arena.top <= SBW, Arena.top
        return a

    def abf(n):
        w = (n + 1) // 2
        a = big[:, Arena.top:Arena.top + w].bitcast(BF16)
        Arena.top += w
        S.arena_top = Arena.top
        assert Arena.top <= SBW, Arena.top
        return a

    uid = [0]

    def rk(prefix="r"):
        uid[0] += 1
        return "%s%d" % (prefix, uid[0])

    class Rot:
        def __init__(self, aps, name, keys=None):
            self.aps = aps
            self.keys = keys if keys is not None else [name + str(i) for i in range(len(aps))]
            self.i = 0

        def next(self):
            k = self.i % len(self.aps)
            self.i += 1
            return self.aps[k], self.keys[k]

    psrot = Rot([p for p in PS], "PS")
    cur = {"rot": psrot}

    def bank():
        return cur["rot"].next()

    ident = af32(128)
    mats_bf = abf(4 * 128).rearrange("p (m n) -> p m n", m=4)
    ones_bf, bones_bf, rot0_bf, rot1_bf = (mats_bf[:, i, :] for i in range(4))
    modv = af32(2 * 6 * 16).rearrange("p (l s k c) -> p l s k c", l=2, s=6, k=8)
    gains_sb = af32(40).rearrange("p (a k) -> p a k", a=5)
    qkg_sb = af32(2)
    esink = af32(8)
    gng_sb = af32(16)
    PERSIST_TOP = Arena.top

    S.op("sp", lambda e: e.dma_start(out=ident, in_=c_ident), writes=["ident"], dma="c0")
    S.op("pool", lambda e: e.dma_start(out=mats_bf, in_=c_mats.rearrange("m p n -> p m n")), writes=["mats"], dma="c1")
    S.op("sp", lambda e: e.dma_start(out=gains_sb, in_=gains), writes=["gains"], dma="c0")
    S.op("sp", lambda e: e.dma_start(out=qkg_sb, in_=qkg), writes=["qkg"], dma="c0")
    S.op("sp", lambda e: e.dma_start(out=esink, in_=sinkr), writes=["esink"], dma="c0")
    S.op("sp", lambda e: e.dma_start(out=gng_sb, in_=gng), writes=["gng"], dma="c0")
    S.op("act", lambda e: e.activation(out=esink, in_=esink, func=AF.Exp), reads=["esink"], writes=["esink"])

    def phase_mod():
        Arena.top = PERSIST_TOP
        cnd = af32(16).rearrange("p (k c) -> p k c", k=8)
        scn = af32(16).rearrange("p (k c) -> p k c", k=8)
        tmp = af32(16).rearrange("p (k c) -> p k c", k=8)
        adab_sb = af32(96).rearrange("p (l j) -> p l j", l=2)
        acc = af32(2 * 96).rearrange("p (l j c) -> p l j c", l=2, j=48)
        wb = [af32(4096).rearrange("p (k n) -> p k n", k=8) for _ in range(2)]
        modrow = af32(6144)
        S.op("sp", lambda e: e.dma_start(out=cnd, in_=cond), writes=["cnd"], dma="c0")
        S.op("sp", lambda e: e.dma_start(out=adab_sb, in_=adab), writes=["adab"], dma="c0")
        S.op("act", lambda e: e.activation(out=tmp, in_=cnd, func=AF.Exp, scale=-1.0), reads=["cnd"], writes=["mtmp"])
        S.op("dve", lambda e: e.tensor_scalar(out=tmp, in0=tmp, scalar1=1.0, scalar2=None, op0=ALU.add), reads=["mtmp"], writes=["mtmp"])
        S.op("dve", lambda e: e.reciprocal(out=tmp, in_=tmp), reads=["mtmp"], writes=["mtmp"])
        S.op("dve", lambda e: e.tensor_tensor(out=scn, in0=cnd, in1=tmp, op=ALU.mult), reads=["mtmp", "cnd"], writes=["scn"])
        n = 0
        for l in range(2):
            wv = adaw[l].rearrange("(k p) n -> p k n", p=128)
            for cg in range(12):
                w_ap, wkey = wb[n % 2], "adw%d" % (n % 2)
                q_ = ("sp", "act")[n % 2]
                n += 1
                S.op(q_, lambda e, w_ap=w_ap, cg=cg, wv=wv: e.dma_start(out=w_ap, in_=wv[:, :, cg * 512:(cg + 1) * 512]), writes=[wkey], dma=wkey)
                pb, pk = bank()
                for k in range(8):
                    S.op("pe", lambda e, pb=pb, w_ap=w_ap, k=k: e.matmul(pb[0:2, :], lhsT=scn[:, k, :], rhs=w_ap[:, k, :], start=(k == 0), stop=(k == 7)),
                         reads=[wkey, "scn"], writes=[pk])
                S.op("dve", lambda e, pb=pb, cg=cg: e.tensor_copy(out=modrow[0:2, cg * 512:(cg + 1) * 512], in_=pb[0:2, :]), reads=[pk], writes=[("modrow", cg)])
            pb, pk = bank()
            for j in range(48):
                S.op("pe", lambda e, pb=pb, j=j: e.transpose(pb[:, 2 * j:2 * j + 2], modrow[0:2, j * 128:(j + 1) * 128], ident[0:2, 0:2]),
                     reads=[("modrow", j // 4), "ident"], writes=[pk])
            S.op("dve", lambda e, pb=pb, l=l: e.tensor_copy(out=acc[:, l], in_=pb[:, 0:96].rearrange("p (j c) -> p j c", j=48)), reads=[pk], writes=["acc%d" % l])
            S.op("dve", lambda e, l=l: e.tensor_tensor(out=acc[:, l], in0=acc[:, l],
                                                       in1=adab_sb[:, l, :].unsqueeze(2).to_broadcast([128, 48, 2]), op=ALU.add),
                 reads=["acc%d" % l, "adab"], writes=["acc%d" % l])
            a6 = acc[:, l].rearrange("p (s k) c -> p s k c", s=6)
            for s_i in (0, 2, 3, 5):
                S.op("dve", lambda e, l=l, s_i=s_i, a6=a6: e.tensor_copy(out=modv[:, l, s_i], in_=a6[:, s_i]),
                     reads=["acc%d" % l], writes=["modv"])
            for s_i, g_i in ((1, 2 * l), (4, 2 * l + 1)):
                S.op("dve", lambda e, l=l, s_i=s_i, a6=a6: e.tensor_scalar(out=modv[:, l, s_i], in0=a6[:, s_i], scalar1=1.0, scalar2=None, op0=ALU.add),
                     reads=["acc%d" % l], writes=["modv"])
                S.op("dve", lambda e, l=l, s_i=s_i, g_i=g_i: e.tensor_tensor(out=modv[:, l, s_i], in0=modv[:, l, s_i],
                                                                            in1=gains_sb[:, g_i, :].unsqueeze(2).to_broadcast([128, 8, 2]), op=ALU.mult),
                     reads=["modv", "gains"], writes=["modv"])
        S.barrier()

    def precast_all(names=("w1_0", "w2_0", "w_in1", "w_out1", "w1_1", "w2_1")):
        for name in names:
            src = wsrc[name]
            for k in range(src.shape[0] // 128):
                S.op("pool", lambda e, name=name, src=src, k=k: e.dma_start(out=wbf[name][k * 128:(k + 1) * 128, :], in_=src[k * 128:(k + 1) * 128, :]),
                     writes=[("pc", name, k)], dma="pc_" + name)

    def load_w(dst, src_rows, k_chunks, key, col_groups=None, per_k=False, pc=None):
        if pc is not None:
            eng_, src_rows, rd = "sp", wbf[pc], (lambda k: [("pc", pc, k)])
        else:
            eng_, rd = "pool", (lambda k: [])
        return _load_w(dst, src_rows, k_chunks, key, col_groups, per_k, eng_, rd)

    hwq = [0]

    def _load_w(dst, src_rows, k_chunks, key, col_groups, per_k, eng_, rd):
        if eng_ == "sp":
            srcv = src_rows.rearrange("(k p) n -> p k n", p=128)
            if col_groups is not None:
                for gi_, (c0, c1) in enumerate(col_groups):
                    q_ = ("sp", "act")[hwq[0] % 2]
                    hwq[0] += 1
                    S.op(q_, lambda e, c0=c0, c1=c1: e.dma_start(out=dst[:, :, c0:c1], in_=srcv[:, :, c0:c1]),
                         reads=[r_ for k in range(k_chunks) for r_ in rd(k)], writes=[(key, gi_)], dma="%s_g%d%s" % (key, gi_, q_))
            else:
                for g0 in range(0, k_chunks, 8):
                    q_ = ("sp", "act")[hwq[0] % 2]
                    hwq[0] += 1
                    S.op(q_, lambda e, g0=g0: e.dma_start(out=dst[:, g0:g0 + 8, :], in_=srcv[:, g0:g0 + 8, :]),
                         reads=[r_ for k in range(g0, g0 + 8) for r_ in rd(k)], writes=[(key, g0 // 8) if per_k else key], dma="%s_k%d%s" % (key, g0 // 8, q_))
            return
        if col_groups is not None:
            for gi_, (c0, c1) in enumerate(col_groups):
                for k in range(k_chunks):
                    S.op(eng_, lambda e, k=k, c0=c0, c1=c1: e.dma_start(out=dst[:, k, c0:c1], in_=src_rows[k * 128:(k + 1) * 128, c0:c1]),
                         reads=rd(k), writes=[(key, gi_)], dma="%s_g%d" % (key, gi_))
            return
        for k in range(k_chunks):
            S.op(eng_, lambda e, k=k: e.dma_start(out=dst[:, k, :], in_=src_rows[k * 128:(k + 1) * 128, :]),
                 reads=rd(k), writes=[(key, k // 8) if per_k else key], dma=("%s_k%d" % (key, k // 8)) if per_k else key)

    def rms_rstd(xchunks, xkeys, nchunk, inv_n, sqrot, rstd, rstd_key, onesm=None, sq_eng="act"):
        onesm = ones_bf if onesm is None else onesm
        pb, pk = bank()
        N = xchunks[0].shape[-1]
        for c in range(nchunk):
            sq, sk = sqrot.next()
            if sq_eng == "act":
                S.op("act", lambda e, sq=sq, c=c: e.activation(out=sq[:, 0:N], in_=xchunks[c], func=AF.Square), reads=[xkeys[c]], writes=[sk])
            else:
                S.op(sq_eng, lambda e, sq=sq, c=c: e.tensor_tensor(out=sq[:, 0:N], in0=xchunks[c], in1=xchunks[c], op=ALU.mult), reads=[xkeys[c]], writes=[sk])
            S.op("pe", lambda e, sq=sq, c=c, pb=pb: e.matmul(pb[:, 0:N], lhsT=onesm, rhs=sq[:, 0:N], start=(c == 0), stop=(c == nchunk - 1)),
                 reads=[sk, "mats"], writes=[pk])
        S.op("act", lambda e, pb=pb: e.activation(out=rstd[:, 0:N], in_=pb[:, 0:N], func=AF.Ln, scale=inv_n, bias=EPS), reads=[pk], writes=[rstd_key])
        S.op("act", lambda e: e.activation(out=rstd[:, 0:N], in_=rstd[:, 0:N], func=AF.Exp, scale=-0.5), reads=[rstd_key], writes=[rstd_key])

    def modulate(xchunks, xkeys, rstd, rstd_key, l, gi, ci, tmprot, outs, outkeys, after=None, chunks=None, add_eng="act"):
        for c in (range(8) if chunks is None else chunks):
            tp, tk = tmprot.next()
            if add_eng == "dve":
                S.op("dve", lambda e, c=c, tp=tp: e.tensor_tensor(out=tp, in0=xchunks[c], in1=rstd, op=ALU.mult), reads=[xkeys[c], rstd_key], writes=[tk])
                S.op("dve", lambda e, c=c, tp=tp: e.tensor_scalar(out=outs[c], in0=tp, scalar1=modv[:, l, gi + 1, c, ci:ci + 1], scalar2=modv[:, l, gi, c, ci:ci + 1],
                                                                op0=ALU.mult, op1=ALU.add),
                     reads=[tk, "modv"], writes=[outkeys[c]])
            else:
                S.op("dve", lambda e, c=c, tp=tp: e.scalar_tensor_tensor(out=tp, in0=xchunks[c], scalar=modv[:, l, gi + 1, c, ci:ci + 1], in1=rstd,
                                                                        op0=ALU.mult, op1=ALU.mult),
                     reads=[xkeys[c], rstd_key, "modv"], writes=[tk])
                S.op("act", lambda e, c=c, tp=tp: e.activation(out=outs[c], in_=tp, func=AF.Identity, bias=modv[:, l, gi, c, ci:ci + 1], scale=1.0),
                     reads=[tk, "modv"], writes=[outkeys[c]])
            if after is not None:
                after(c)

    def tile_cond(t):
        return 0 if t < 8 else 1

    res = {}

    def phase_A1():
        Arena.top = PERSIST_TOP
        KA = abf(5120)
        KS = abf(5120)
        VA = abf(40 * 192).rearrange("p (t c) -> p t c", t=40)
        VS = abf(40 * 192).rearrange("p (t c) -> p t c", t=40)
        res.update(KA=KA, KS=KS, VA=VA, VS=VS)
        res["A1_TOP"] = Arena.top
        wkv = abf(8 * 512).rearrange("p (k n) -> p k n", k=8)
        xin_b = [af32(4096).rearrange("p (s d) -> p s d", s=4) for _ in range(2)]
        xt_b = [af32(4096).rearrange("p (c n) -> p c n", c=8) for _ in range(2)]
        hT = abf(4096).rearrange("p (c n) -> p c n", c=8)
        sqrot = Rot([abf(512) for _ in range(2)], "sq")
        rstd = af32(512)
        tmprot = Rot([af32(512) for _ in range(2)], "tmp")
        kn_rot = Rot([af32(512) for _ in range(4)], "kn")
        knb_rot = Rot([abf(512) for _ in range(2)], "knb")
        t1_rot = Rot([af32(512) for _ in range(3)], "t1")
        cs_b = [af32(1024).rearrange("p (a n) -> p a n", a=2) for _ in range(2)]
        vst = af32(1024).rearrange("p (s n) -> p s n", s=4)
        kst = af32(1024).rearrange("p (a s n) -> p a s n", a=2, s=4)[:, :, :, :]
        kst = af32(1024).rearrange("p (a s n) -> p a s n", a=2, s=4)
        cst = af32(4 * 128 * 2).rearrange("p (a s n) -> p a s n", a=2, s=4)
        load_w(wkv, w_in0[:, 1024:1536], 8, "wkv")
        precast_all(("wq", "wo"))
        S.op("dve", lambda e: e.memset(VA[:, :, 64:128], 1.0), writes=["VAones"])
        S.op("dve", lambda e: e.memset(VS[:, :, 64:128], 1.0), writes=["VSones"])
        for a, (Kdst, Vdst) in enumerate(((KA, VA), (KS, VS))):
            S.op("sp", lambda e, a=a: e.dma_start(out=cst[:, 0], in_=kctx[a].rearrange("(s p) n -> p s n", p=128)), writes=["cstk"], dma="cst")
            S.op("sp", lambda e, a=a: e.dma_start(out=cst[:, 1], in_=vctx[a].rearrange("(s p) n -> p s n", p=128)), writes=["cstv"], dma="cst")
            pb, pk = bank()
            for s_ in range(4):
                S.op("pe", lambda e, s_=s_, pb=pb: e.transpose(pb[:, s_ * 128:(s_ + 1) * 128], cst[:, 0, s_, :], ident),
                     reads=["cstk", "ident"], writes=[pk])
            S.op("act", lambda e, pb=pb, Kdst=Kdst: e.copy(out=Kdst[:, 0:512], in_=pb), reads=[pk], writes=["Kctx%d" % a])
            S.op("dve", lambda e, Vdst=Vdst: e.tensor_copy(out=Vdst[:, 0:4, 0:64], in_=cst[:, 1, :, 0:64]), reads=["cstv"], writes=["Vctx%d" % a])
            S.op("dve", lambda e, Vdst=Vdst: e.tensor_copy(out=Vdst[:, 0:4, 128:192], in_=cst[:, 1, :, 64:128]), reads=["cstv"], writes=["Vctx%da" % a])
        xin_v = xin.rearrange("(t s p) d -> t p s d", s=4, p=128)
        hk = ["hT%d" % c for c in range(8)]
        ts = {}

        def stageA(t):
            ci = tile_cond(t)
            xi, xik = xin_b[t % 2], "xin%d" % (t % 2)
            xt_, xtk = xt_b[t % 2], ["xt%d_%d" % (t % 2, c) for c in range(8)]
            S.op("sp", lambda e: e.dma_start(out=xi, in_=xin_v[t]), writes=[xik], dma=xik)
            if t < 8:
                cs_, csk = cs_b[t % 2], "cs%d" % (t % 2)
                S.op("sp", lambda e: e.dma_start(out=cs_, in_=c_cs0[:, :, t * 512:(t + 1) * 512].rearrange("a p n -> p a n")), writes=[csk], dma=csk)
                ts[t] = (cs_, csk)
            for c in range(8):
                pb, pk = bank()
                for s_ in range(4):
                    S.op("pe", lambda e, c=c, s_=s_, pb=pb: e.transpose(pb[:, s_ * 128:(s_ + 1) * 128], xi[:, s_, c * 128:(c + 1) * 128], ident),
                         reads=[xik, "ident"], writes=[pk])
                if c % 2 == 0:
                    S.op("act", lambda e, c=c, pb=pb: e.copy(out=xt_[:, c, :], in_=pb), reads=[pk], writes=[xtk[c]])
                else:
                    S.op("dve", lambda e, c=c, pb=pb: e.tensor_copy(out=xt_[:, c, :], in_=pb), reads=[pk], writes=[xtk[c]])
            S.op("pool", lambda e: e.dma_start(out=xT[0][t], in_=xt_), reads=xtk, writes=[("xT0", t)], dma="st_xt%d" % (t % 2))
            xch = [xt_[:, c, :] for c in range(8)]
            rms_rstd(xch, xtk, 8, 1.0 / D, sqrot, rstd, "rstd")
            modulate(xch, xtk, rstd, "rstd", 0, 0, ci, tmprot, [hT[:, c, :] for c in range(8)], hk)

        kst_ = {}

        def stageB(t):
            kns = []
            for a in range(2):
                pb, pk = bank()
                for k in range(8):
                    S.op("pe", lambda e, k=k, a=a, pb=pb: e.matmul(pb, lhsT=wkv[:, k, a * 128:(a + 1) * 128], rhs=hT[:, k, :], start=(k == 0), stop=(k == 7)),
                         reads=[hk[k], "wkv"], writes=[pk])
                kn, knk = kn_rot.next()
                S.op("act", lambda e, pb=pb, kn=kn: e.copy(out=kn, in_=pb), reads=[pk], writes=[knk])
                kns.append((kn, knk))
            kst_[t] = kns
            for s_ in range(4):
                pb, pk = bank()
                for k in range(8):
                    S.op("pe", lambda e, k=k, s_=s_, pb=pb: e.matmul(pb[:, 0:256], lhsT=hT[:, k, s_ * 128:(s_ + 1) * 128], rhs=wkv[:, k, 256:512], start=(k == 0), stop=(k == 7)),
                         reads=[hk[k], "wkv"], writes=[pk])
                kt = 4 + t * 4 + s_
                for a, Vdst in enumerate((VA, VS)):
                    for g in range(2):
                        if (a + g) % 2 == 0:
                            S.op("act", lambda e, a=a, g=g, Vdst=Vdst, pb=pb, kt=kt: e.copy(out=Vdst[:, kt, g * 128:g * 128 + 64], in_=pb[:, a * 128 + g * 64:a * 128 + g * 64 + 64]),
                                 reads=[pk], writes=[("V%d_%d" % (a, g), kt)])
                        else:
                            S.op("dve", lambda e, a=a, g=g, Vdst=Vdst, pb=pb, kt=kt: e.tensor_copy(out=Vdst[:, kt, g * 128:g * 128 + 64], in_=pb[:, a * 128 + g * 64:a * 128 + g * 64 + 64]),
                                 reads=[pk], writes=[("V%d_%d" % (a, g), kt)])
                if t == 8:
                    S.op("dve", lambda e, s_=s_, pb=pb: e.tensor_copy(out=vst[:, s_, :], in_=pb[:, 0:256]), reads=[pk], writes=[("vst", s_)])
            if t == 8:
                for a in range(2):
                    S.op("pool", lambda e, a=a: e.dma_start(out=nv[a].rearrange("(s p) n -> p s n", p=128), in_=vst[:, :, a * 128:(a + 1) * 128]),
                         reads=[("vst", s_) for s_ in range(4)], writes=[("nv", a)], dma="st_v%d" % a)

        def stageC(t):
            for a, Kdst in enumerate((KA, KS)):
                kn, knk = kst_[t][a]
                if a == 0:
                    rs, rsk = t1_rot.next()
                    rms_rstd([kn], [knk], 1, 1.0 / 64, sqrot, rs, rsk, onesm=bones_bf)
                    S.op("dve", lambda e, kn=kn, rs=rs: e.scalar_tensor_tensor(out=kn, in0=kn, scalar=qkg_sb[:, 1:2], in1=rs, op0=ALU.mult, op1=ALU.mult),
                         reads=[knk, rsk, "qkg"], writes=[knk])
                col0 = 512 + t * 512
                if t < 8:
                    cs_, csk = ts[t]
                    knb, knbk = knb_rot.next()
                    S.op("act", lambda e, kn=kn, knb=knb: e.copy(out=knb, in_=kn), reads=[knk], writes=[knbk])
                    pb2, pk2 = bank()
                    S.op("pe", lambda e, pb2=pb2, knb=knb: e.matmul(pb2, lhsT=rot0_bf, rhs=knb, start=True, stop=True), reads=[knbk, "mats"], writes=[pk2])
                    t1, t1k = t1_rot.next()
                    S.op("dve", lambda e, t1=t1, pb2=pb2, cs_=cs_: e.tensor_tensor(out=t1, in0=pb2, in1=cs_[:, 1, :], op=ALU.mult), reads=[pk2, csk], writes=[t1k])
                    S.op("pool", lambda e, kn=kn, cs_=cs_: e.tensor_tensor(out=kn, in0=kn, in1=cs_[:, 0, :], op=ALU.mult), reads=[knk, csk], writes=[knk])
                    S.op("dve", lambda e, kn=kn, t1=t1, Kdst=Kdst, col0=col0: e.tensor_tensor(out=Kdst[:, col0:col0 + 512], in0=kn, in1=t1, op=ALU.add),
                         reads=[knk, t1k], writes=[("K%d" % a, t)])
                else:
                    S.op("act", lambda e, kn=kn, Kdst=Kdst, col0=col0: e.copy(out=Kdst[:, col0:col0 + 512], in_=kn), reads=[knk], writes=[("K%d" % a, t)])
                    pb2, pk2 = bank()
                    for s_ in range(4):
                        S.op("pe", lambda e, s_=s_, pb2=pb2, kn=kn: e.transpose(pb2[:, s_ * 128:(s_ + 1) * 128], kn[:, s_ * 128:(s_ + 1) * 128], ident),
                             reads=[knk, "ident"], writes=[pk2])
                    S.op("dve", lambda e, a=a, pb2=pb2: e.tensor_copy(out=kst[:, a], in_=pb2.rearrange("p (s n) -> p s n", s=4)), reads=[pk2], writes=["kst%d" % a])
                    S.op("pool", lambda e, a=a: e.dma_start(out=nk[a].rearrange("(s p) n -> p s n", p=128), in_=kst[:, a]), reads=["kst%d" % a], writes=[("nk", a)], dma="st_k%d" % a)

        stageA(0)
        stageB(0)
        for t in range(NT - 1):
            stageA(t + 1)
            stageC(t)
            stageB(t + 1)
        stageC(NT - 1)
        S.barrier()


    def phase_A2():
        Arena.top = res["A1_TOP"]
        KA, KS, VA, VS = res["KA"], res["KS"], res["VA"], res["VS"]
        wq = abf(8 * 1024).rearrange("p (k n) -> p k n", k=8)
        wo = abf(8 * 1024).rearrange("p (k n) -> p k n", k=8)
        masks = abf(2 * 512).rearrange("p (a n) -> p a n", a=2)
        xF = af32(4096).rearrange("p (c n) -> p c n", c=8)
        xB = af32(4096).rearrange("p (c n) -> p c n", c=8)
        hT = abf(4096).rearrange("p (c n) -> p c n", c=8)
        QT_b = [abf(4096).rearrange("p (c n) -> p c n", c=8) for _ in range(2)]
        OT_b = [abf(4096).rearrange("p (c n) -> p c n", c=8) for _ in range(2)]
        horot = Rot([abf(512) for _ in range(2)], "ho")
        PTrot = Rot([abf(1024) for _ in range(3)], "PT")
        sqrot = Rot([abf(512) for _ in range(2)], "sq")
        rstdF = af32(512)
        rstdB = af32(512)
        tmprot = Rot([af32(512) for _ in range(2)], "tmp")
        kn_rot = Rot([af32(512) for _ in range(2)], "kn")
        knb_rot = Rot([abf(512) for _ in range(2)], "knb")
        t1_rot = Rot([af32(512) for _ in range(2)], "t1")
        recrot = Rot([af32(512) for _ in range(2)], "rec")
        rec2rot = Rot([af32(512) for _ in range(2)], "rec2")
        osbrot = Rot([af32(512) for _ in range(2)], "osb")
        cs_ = af32(1024).rearrange("p (a n) -> p a n", a=2)
        esk2 = af32(4)
        S.op("dve", lambda e: e.tensor_copy(out=esk2[0:64, :], in_=esink[0:64, 0:4]), reads=["esink"], writes=["esk2"])
        S.op("dve", lambda e: e.tensor_copy(out=esk2[64:128, :], in_=esink[64:128, 4:8]), reads=["esink"], writes=["esk2"])
        psG = Rot(PS[4:6], "PS", ["PS4", "PS5"])
        psP = Rot([0, 1], "pair")
        cur["rot"] = psG
        load_w(wq, None, 8, "wq", pc="wq")
        load_w(wo, None, 8, "wo", pc="wo")
        S.op("pool", lambda e: e.dma_start(out=masks, in_=c_mask.rearrange("a p n -> p a n")), writes=["masks"], dma="c1")
        precast_all()
        xFk = ["xF_%d" % c for c in range(8)]
        xBk = ["xB_%d" % c for c in range(8)]
        hk = ["hT%d" % c for c in range(8)]

        def stats_gen(xch, xk, rstd_, rkey):
            pb, pk = bank()
            prev = []
            for c0 in range(0, 8, 2):
                curl = []
                for c in (c0, c0 + 1):
                    sq, sk = sqrot.next()
                    S.op("pool", lambda e, sq=sq, c=c: e.tensor_tensor(out=sq, in0=xch[c], in1=xch[c], op=ALU.mult), reads=[xk[c]], writes=[sk])
                    curl.append((c, sq, sk))
                yield
                for c, sq, sk in curl:
                    S.op("pe", lambda e, sq=sq, c=c: e.matmul(pb, lhsT=ones_bf, rhs=sq, start=(c == 0), stop=(c == 7)), reads=[sk, "mats"], writes=[pk])
            yield
            S.op("act", lambda e: e.activation(out=rstd_, in_=pb, func=AF.Ln, scale=1.0 / D, bias=EPS), reads=[pk], writes=[rkey])
            S.op("act", lambda e: e.activation(out=rstd_, in_=rstd_, func=AF.Exp, scale=-0.5), reads=[rkey], writes=[rkey])
            yield

        def front(t):
            ci = tile_cond(t)
            QT = QT_b[t % 2]
            S.op("sp", lambda e: e.dma_start(out=xF, in_=xT[0][t]), writes=xFk, dma="ld_xF")
            if t < 8:
                S.op("sp", lambda e: e.dma_start(out=cs_, in_=c_cs0[:, :, t * 512:(t + 1) * 512].rearrange("a p n -> p a n")), writes=["cs"], dma="ld_cs")
            yield
            xch = [xF[:, c, :] for c in range(8)]
            yield from stats_gen(xch, xFk, rstdF, "rstdF")
            for c_ in range(8):
                modulate(xch, xFk, rstdF, "rstdF", 0, 0, ci, tmprot, [hT[:, c, :] for c in range(8)], hk, chunks=[c_], add_eng="dve")
                if c_ % 2 == 1:
                    yield
            for qc in range(8):
                pb, pk = bank()
                for k in range(8):
                    S.op("pe", lambda e, k=k, qc=qc, pb=pb: e.matmul(pb, lhsT=wq[:, k, qc * 128:(qc + 1) * 128], rhs=hT[:, k, :], start=(k == 0), stop=(k == 7)),
                         reads=[hk[k], "wq"], writes=[pk])
                qn, qnk = kn_rot.next()
                S.op("dve", lambda e, pb=pb, qn=qn: e.tensor_copy(out=qn, in_=pb), reads=[pk], writes=[qnk])
                if qc < 4:
                    sq, sk = sqrot.next()
                    S.op("pool", lambda e, sq=sq, qn=qn: e.tensor_tensor(out=sq, in0=qn, in1=qn, op=ALU.mult), reads=[qnk], writes=[sk])
                    yield
                    pbs, pks = bank()
                    S.op("pe", lambda e, sq=sq, pbs=pbs: e.matmul(pbs, lhsT=bones_bf, rhs=sq, start=True, stop=True), reads=[sk, "mats"], writes=[pks])
                    yield
                    rs, rsk = t1_rot.next()
                    S.op("act", lambda e, pbs=pbs, rs=rs: e.activation(out=rs, in_=pbs, func=AF.Ln, scale=1.0 / 64, bias=EPS), reads=[pks], writes=[rsk])
                    S.op("act", lambda e, rs=rs: e.activation(out=rs, in_=rs, func=AF.Exp, scale=-0.5), reads=[rsk], writes=[rsk])
                    S.op("dve", lambda e, qn=qn, rs=rs: e.scalar_tensor_tensor(out=qn, in0=qn, scalar=qkg_sb[:, 0:1], in1=rs, op0=ALU.mult, op1=ALU.mult),
                         reads=[qnk, rsk, "qkg"], writes=[qnk])
                if t < 8:
                    knb, knbk = knb_rot.next()
                    S.op("dve", lambda e, qn=qn, knb=knb: e.tensor_copy(out=knb, in_=qn), reads=[qnk], writes=[knbk])
                    yield
                    pb2, pk2 = bank()
                    S.op("pe", lambda e, pb2=pb2, knb=knb: e.matmul(pb2, lhsT=rot0_bf, rhs=knb, start=True, stop=True), reads=[knbk, "mats"], writes=[pk2])
                    S.op("pool", lambda e, qn=qn: e.tensor_tensor(out=qn, in0=qn, in1=cs_[:, 0, :], op=ALU.mult), reads=[qnk, "cs"], writes=[qnk])
                    yield
                    t1, t1k = t1_rot.next()
                    S.op("dve", lambda e, t1=t1, pb2=pb2: e.tensor_tensor(out=t1, in0=pb2, in1=cs_[:, 1, :], op=ALU.mult), reads=[pk2, "cs"], writes=[t1k])
                    S.op("pool", lambda e, qn=qn, t1=t1, qc=qc: e.tensor_tensor(out=QT[:, qc, :], in0=qn, in1=t1, op=ALU.add),
                         reads=[qnk, t1k], writes=[("QT", t % 2, qc)])
                else:
                    S.op("pool", lambda e, qn=qn, qc=qc: e.tensor_copy(out=QT[:, qc, :], in_=qn), reads=[qnk], writes=[("QT", t % 2, qc)])
                yield

        def back(t):
            ci = tile_cond(t)
            OT = OT_b[t % 2]
            S.op("sp", lambda e: e.dma_start(out=xB, in_=xT[0][t]), writes=xBk, dma="ld_xB")
            yield
            otk = [("OT", t % 2, typ, g, qb) for typ in range(2) for g in range(2) for qb in range(4)]
            for m in range(8):
                pb, pk = bank()
                for ch in range(8):
                    S.op("pe", lambda e, ch=ch, m=m, pb=pb: e.matmul(pb, lhsT=wo[:, ch, m * 128:(m + 1) * 128], rhs=OT[:, ch, :], start=(ch == 0), stop=(ch == 7)),
                         reads=otk + ["wo"], writes=[pk])
                S.op("dve", lambda e, m=m, pb=pb: e.scalar_tensor_tensor(out=xB[:, m, :], in0=pb, scalar=modv[:, 0, 2, m, ci:ci + 1], in1=xB[:, m, :],
                                                                       op0=ALU.mult, op1=ALU.add),
                     reads=[pk, xBk[m], "modv"], writes=[xBk[m]])
                yield
            S.op("sp", lambda e: e.dma_start(out=xT[1][t], in_=xB), reads=xBk, writes=[("xT1", t)], dma="st_xB")
            xch = [xB[:, c, :] for c in range(8)]
            yield from stats_gen(xch, xBk, rstdB, "rstdB")
            bufs = [horot.next() for _ in range(8)]

            def after(c):
                S.op("sp", lambda e, c=c: e.dma_start(out=hTs[0][t][:, c, :], in_=bufs[c][0]), reads=[bufs[c][1]], writes=[("hTs0", t, c)], dma="st2_" + bufs[c][1])
            for c_ in range(8):
                modulate(xch, xBk, rstdB, "rstdB", 0, 3, ci, tmprot, [b_[0] for b_ in bufs], [b_[1] for b_ in bufs], after=after, chunks=[c_], add_eng="dve")
                if c_ % 2 == 1:
                    yield

        def attn(t, filler):
            QT = QT_b[t % 2]
            OT = OT_b[t % 2]
            cnt = [0]

            def fill():
                cnt[0] += 1
                if cnt[0] % 2 == 0:
                    next(filler, None)
            for qb in range(4):
                for typ in range(2):
                    Ksrc, Vsrc = (KA, VA) if typ == 0 else (KS, VS)
                    kname = "K%d" % typ
                    base = 4 * typ
                    kl = []
                    if t < 8:
                        for j in range(4):
                            kl.append((j * 128, j, None, ["Kctx%d" % typ, "Vctx%d" % typ, "Vctx%da" % typ]))
                        if typ == 0:
                            for j in range(32):
                                kl.append((512 + j * 128, 4 + j, None, [(kname, j // 4)] + [("V%d_%d" % (typ, g_), 4 + j) for g_ in range(2)]))
                        else:
                            qbg = t * 4 + qb
                            for dj, mi in ((-1, 0), (0, None), (1, 1)):
                                j = qbg + dj
                                if 0 <= j < 32:
                                    kl.append((512 + j * 128, 4 + j, mi, [(kname, j // 4)] + [("V%d_%d" % (typ, g_), 4 + j) for g_ in range(2)]))
                    else:
                        sq_ = qb // 2
                        for j in range(2):
                            jj = 32 + sq_ * 2 + j
                            kl.append((512 + jj * 128, 4 + jj, None, [(kname, 8)] + [("V%d_%d" % (typ, g_), 4 + jj) for g_ in range(2)]))

                    def attn_pair(qb=qb, typ=typ, Ksrc=Ksrc, Vsrc=Vsrc, kname=kname, base=base, kl=kl):
                        nk_ = len(kl)
                        obs = [(PS[6], "PS6"), (PS[7], "PS7")]
                        pts = [None] * nk_

                        def isk(d_):
                            return (isinstance(d_, tuple) and d_[0] == kname) or (isinstance(d_, str) and d_.startswith("Kctx"))

                        def emit_s(j):
                            kcol, vt, mi, deps = kl[j]
                            pi = psP.next()[0]
                            keys = ["PS%d" % (2 * pi), "PS%d" % (2 * pi + 1)]
                            for g in range(2):
                                rows = slice(64 * g, 64 * g + 64)
                                S.op("pe", lambda e, g=g, rows=rows, kcol=kcol, pi=pi: e.matmul(PS[2 * pi + g].rearrange("p (h n) -> p h n", h=4), lhsT=Ksrc[rows, kcol:kcol + 128],
                                                                                              rhs=QT[rows, base:base + 4, qb * 128:(qb + 1) * 128], start=True, stop=True),
                                     reads=[d_ for d_ in deps if isk(d_)] + [("QT", t % 2, base + h_) for h_ in range(4)], writes=[keys[g]])
                            pt, ptk = PTrot.next()
                            S.op("act", lambda e, pi=pi, pt=pt: e.activation(out=pt, in_=PP[pi], func=AF.Exp, scale=0.125), reads=keys, writes=[ptk])
                            if mi is not None:
                                S.op("dve", lambda e, pt=pt, mi=mi: e.tensor_tensor(out=pt.rearrange("p (g n) -> p g n", g=2), in0=pt.rearrange("p (g n) -> p g n", g=2),
                                                                                 in1=masks[:, mi, :].unsqueeze(1).to_broadcast([128, 2, 512]), op=ALU.mult),
                                     reads=[ptk, "masks"], writes=[ptk])
                            pts[j] = (pt, ptk)

                        def emit_pv(j):
                            kcol, vt, mi, deps = kl[j]
                            pt, ptk = pts[j]
                            vdeps = [d_ for d_ in deps if not isk(d_)]
                            for g in range(2):
                                S.op("pe", lambda e, g=g, pt=pt, vt=vt, j=j: e.matmul(obs[g][0], lhsT=Vsrc[:, vt, 64 * g:64 * g + 128], rhs=pt[:, g * 512:(g + 1) * 512],
                                                                                   start=(j == 0), stop=(j == nk_ - 1)),
                                     reads=[ptk] + vdeps + ["V%sones" % ("A" if typ == 0 else "S")], writes=[obs[g][1]])
                        LA = 2
                        for j in range(min(LA, nk_)):
                            emit_s(j)
                        for j in range(nk_):
                            if j + LA < nk_:
                                emit_s(j + LA)
                            emit_pv(j)
                            fill()
                        rec, reck = recrot.next()
                        osb, osbk = osbrot.next()
                        for g in range(2):
                            rows = slice(64 * g, 64 * g + 64)
                            drows = slice(64 * (1 - g), 64 * (1 - g) + 64)
                            ob, obk = obs[g]
                            S.op("dve", lambda e, ob=ob, rows=rows, drows=drows: e.tensor_copy(out=rec[rows, :], in_=ob[drows, :]), reads=[obk], writes=[(reck, g)])
                            S.op("dve", lambda e, ob=ob, rows=rows: e.tensor_copy(out=osb[rows, :], in_=ob[rows, :]), reads=[obk], writes=[(osbk, g)])
                        if typ == 1:
                            S.op("dve", lambda e: e.tensor_tensor(out=rec.rearrange("p (h n) -> p h n", h=4), in0=rec.rearrange("p (h n) -> p h n", h=4),
                                                                  in1=esk2.unsqueeze(2).to_broadcast([128, 4, 128]), op=ALU.add),
                                 reads=[(reck, 0), (reck, 1), "esk2"], writes=[(reck, 0), (reck, 1)])
                        rec2, rec2k = rec2rot.next()
                        S.op("dve", lambda e: e.reciprocal(out=rec2, in_=rec), reads=[(reck, 0), (reck, 1)], writes=[rec2k])
                        S.op("pool", lambda e: e.tensor_tensor(out=OT[:, base:base + 4, qb * 128:(qb + 1) * 128],
                                                               in0=osb.rearrange("p (h n) -> p h n", h=4),
                                                               in1=rec2.rearrange("p (h n) -> p h n", h=4), op=ALU.mult),
                             reads=[(osbk, 0), (osbk, 1), rec2k], writes=[("OT", t % 2, typ, 0, qb), ("OT", t % 2, typ, 1, qb)])
                    attn_pair()

        def chain(*gens):
            for g_ in gens:
                if g_ is not None:
                    yield from g_

        for _ in front(0):
            pass
        for t in range(NT):
            filler = chain(back(t - 1) if t >= 1 else None, front(t + 1) if t + 1 < NT else None)
            attn(t, filler)
            for _ in filler:
                pass
        for _ in back(NT - 1):
            pass
        cur["rot"] = psrot
        S.barrier()

    def phase_mlp(l):
        Arena.top = PERSIST_TOP
        w1s = abf(8 * 4096).rearrange("p (k n) -> p k n", k=8)
        w2s = abf(32 * 1024).rearrange("p (k n) -> p k n", k=32)
        hT = abf(4096).rearrange("p (c n) -> p c n", c=8)
        aT = abf(32 * 512).rearrange("p (c n) -> p c n", c=32)
        xt_ = af32(4096).rearrange("p (c n) -> p c n", c=8)
        rrot = Rot([abf(512) for _ in range(2)], "rr")
        sqrot = Rot([abf(512) for _ in range(2)], "sq")
        rstd = af32(512)
        if l == 0:
            tmprot = Rot([af32(512) for _ in range(2)], "tmp")
            horot = Rot([abf(512) for _ in range(2)], "ho")
        else:
            ytrot = Rot([af32(1024) for _ in range(2)], "yt")
        load_w(w1s, None, 8, "w1s", col_groups=[(i * 1024, (i + 1) * 1024) for i in range(4)], pc="w1_%d" % l)
        load_w(w2s, None, 32, "w2s", per_k=True, pc="w2_%d" % l)
        hsrc = hTs[0] if l == 0 else hTs[2]
        xsrc = xT[1] if l == 0 else xT[3]
        xtk = ["xt_%d" % c for c in range(8)]
        hk = ["hT%d" % c for c in range(8)]

        def L1(t):
            S.op("sp", lambda e: e.dma_start(out=hT, in_=hsrc[t]), writes=hk, dma="ld_h")
            for f in range(32):
                pb, pk = bank()
                for k in range(8):
                    S.op("pe", lambda e, k=k, f=f, pb=pb: e.matmul(pb, lhsT=w1s[:, k, f * 128:(f + 1) * 128], rhs=hT[:, k, :], start=(k == 0), stop=(k == 7)),
                         reads=[hk[k], ("w1s", f // 8)], writes=[pk])
                r, rk_ = rrot.next()
                S.op("act", lambda e, pb=pb, r=r: e.activation(out=r, in_=pb, func=AF.Relu), reads=[pk], writes=[rk_])
                S.op("dve" if f % 2 == 0 else "pool", lambda e, r=r, f=f: e.tensor_tensor(out=aT[:, f, :], in0=r, in1=r, op=ALU.mult), reads=[rk_], writes=[("aT", f)])

        def L2(t):
            ci = tile_cond(t)
            S.op("sp", lambda e: e.dma_start(out=xt_, in_=xsrc[t]), writes=xtk, dma="ld_x")
            for m in range(8):
                pb, pk = bank()
                for f in range(32):
                    S.op("pe", lambda e, f=f, m=m, pb=pb: e.matmul(pb, lhsT=w2s[:, f, m * 128:(m + 1) * 128], rhs=aT[:, f, :], start=(f == 0), stop=(f == 31)),
                         reads=[("aT", f), ("w2s", f // 8)], writes=[pk])
                S.op("dve", lambda e, m=m, pb=pb: e.scalar_tensor_tensor(out=xt_[:, m, :], in0=pb, scalar=modv[:, l, 5, m, ci:ci + 1], in1=xt_[:, m, :],
                                                                       op0=ALU.mult, op1=ALU.add),
                     reads=[pk, xtk[m], "modv"], writes=[xtk[m]])

        def epi(t):
            ci = tile_cond(t)
            xch = [xt_[:, c, :] for c in range(8)]
            if l == 0:
                S.op("pool", lambda e: e.dma_start(out=xT[2][t], in_=xt_), reads=xtk, writes=[("xT2", t)], dma="st_x")
                rms_rstd(xch, xtk, 8, 1.0 / D, sqrot, rstd, "rstd")
                bufs = [horot.next() for _ in range(8)]

                def after(c):
                    S.op("pool", lambda e, c=c: e.dma_start(out=hTs[1][t][:, c, :], in_=bufs[c][0]), reads=[bufs[c][1]], writes=[("hTs1", t, c)], dma="st_" + bufs[c][1])
                modulate(xch, xtk, rstd, "rstd", 1, 0, ci, tmprot, [b_[0] for b_ in bufs], [b_[1] for b_ in bufs], after=after)
            else:
                rms_rstd(xch, xtk, 8, 1.0 / D, sqrot, rstd, "rstd")
                for c in range(8):
                    S.op("dve", lambda e, c=c: e.scalar_tensor_tensor(out=xt_[:, c, :], in0=xt_[:, c, :], scalar=gains_sb[:, 4, c:c + 1], in1=rstd, op0=ALU.mult, op1=ALU.mult),
                         reads=[xtk[c], "rstd", "gains"], writes=[xtk[c]])
                for s_ in range(4):
                    yt, ytk = ytrot.next()
                    for hf in range(2):
                        pb, pk = bank()
                        for cc in range(4):
                            c = 4 * hf + cc
                            S.op("pe", lambda e, c=c, cc=cc, pb=pb, s_=s_: e.transpose(pb[:, cc * 128:(cc + 1) * 128], xt_[:, c, s_ * 128:(s_ + 1) * 128], ident),
                                 reads=[xtk[c], "ident"], writes=[pk])
                        if hf == 0:
                            S.op("act", lambda e, pb=pb, yt=yt: e.copy(out=yt[:, 0:512], in_=pb), reads=[pk], writes=[ytk + "a"])
                        else:
                            S.op("dve", lambda e, pb=pb, yt=yt: e.tensor_copy(out=yt[:, 512:1024], in_=pb), reads=[pk], writes=[ytk + "b"])
                    r0 = t * 512 + s_ * 128
                    S.op("pool", lambda e, yt=yt, r0=r0: e.dma_start(out=y_out[r0:r0 + 128, :], in_=yt), reads=[ytk + "a", ytk + "b"],
                         writes=[("y", t, s_)], dma="st_" + ytk)

        L1(0)
        L2(0)
        for t in range(1, NT):
            L1(t)
            epi(t - 1)
            L2(t)
        epi(NT - 1)
        S.barrier()


    def phase_B1():
        Arena.top = PERSIST_TOP
        wi = abf(8 * 6144).rearrange("p (k n) -> p k n", k=8)
        hT_b = [abf(4096).rearrange("p (c n) -> p c n", c=8) for _ in range(2)]
        cs_b = [af32(2048).rearrange("p (a r n) -> p a r n", a=2, r=2)]
        kn_rot = Rot([af32(512) for _ in range(4)], "kn")
        knb_rot = Rot([abf(512) for _ in range(3)], "knb")
        t1_rot = Rot([af32(512) for _ in range(2)], "t1")
        qo_rot = Rot([abf(512) for _ in range(4)], "qo")
        go_rot = Rot([abf(512) for _ in range(4)], "go")
        ktok_b = [abf(4096).rearrange("p (s n) -> p s n", s=4)]
        vb_rot = Rot([abf(2048) for _ in range(2)], "vb")
        wi_groups = [(0, 1024), (1024, 2048), (2048, 3072), (3072, 4096), (4096, 5120), (5120, 6144)]
        load_w(wi, None, 8, "wi", col_groups=wi_groups, pc="w_in1")

        def b1_tile(t):
            hT, hk = hT_b[t % 2], ["hT%d_%d" % (t % 2, c) for c in range(8)]
            S.op("sp", lambda e: e.dma_start(out=hT, in_=hTs[1][t]), writes=hk, dma="ld_h%d" % (t % 2))
            cs_ = cs_b[0]
            if t < 8:
                for a_ in range(2):
                    S.op("sp", lambda e, a_=a_: e.dma_start(out=cs_[:, a_], in_=c_cs1[a_][:, :, t * 512:(t + 1) * 512]), writes=["cs"], dma="ld_cs")
            ktok = ktok_b[0]
            chunks = [(typ, qc) for typ in range(2) for qc in range(8)]
            st = {}

            def stage_proj(i):
                typ, qc = chunks[i]
                pb, pk = bank()
                c0 = typ * 1024 + qc * 128
                for k in range(8):
                    S.op("pe", lambda e, k=k, pb=pb, c0=c0: e.matmul(pb, lhsT=wi[:, k, c0:c0 + 128], rhs=hT[:, k, :], start=(k == 0), stop=(k == 7)),
                         reads=[hk[k], ("wi", c0 // 1024)], writes=[pk])
                qn, qnk = kn_rot.next()
                sc = 1.0 if typ == 0 else 1.0 / 16.0
                S.op("act", lambda e, pb=pb, qn=qn, sc=sc: e.activation(out=qn, in_=pb, func=AF.Identity, scale=sc), reads=[pk], writes=[qnk])
                st[i] = dict(qn=qn, qnk=qnk)
                if t < 8:
                    knb, knbk = knb_rot.next()
                    S.op("dve", lambda e, qn=qn, knb=knb: e.tensor_copy(out=knb, in_=qn), reads=[qnk], writes=[knbk])
                    st[i].update(knb=knb, knbk=knbk)

            def stage_rope(i):
                typ, qc = chunks[i]
                dc = qc % 2
                qn, qnk = st[i]["qn"], st[i]["qnk"]
                qo, qok = qo_rot.next()
                if t < 8:
                    knb, knbk = st[i]["knb"], st[i]["knbk"]
                    pb2, pk2 = bank()
                    S.op("pe", lambda e, pb2=pb2, knb=knb: e.matmul(pb2, lhsT=rot1_bf, rhs=knb, start=True, stop=True), reads=[knbk, "mats"], writes=[pk2])
                    t1, t1k = t1_rot.next()
                    S.op("dve", lambda e, t1=t1, pb2=pb2, dc=dc: e.tensor_tensor(out=t1, in0=pb2, in1=cs_[:, 1, dc, :], op=ALU.mult), reads=[pk2, "cs"], writes=[t1k])
                    S.op("pool", lambda e, qn=qn, dc=dc: e.tensor_tensor(out=qn, in0=qn, in1=cs_[:, 0, dc, :], op=ALU.mult), reads=[qnk, "cs"], writes=[qnk])
                    if typ == 0:
                        S.op("dve", lambda e, qn=qn, t1=t1, qo=qo: e.tensor_tensor(out=qo, in0=qn, in1=t1, op=ALU.add), reads=[qnk, t1k], writes=[qok])
                    else:
                        S.op("dve", lambda e, qn=qn, t1=t1: e.tensor_tensor(out=qn, in0=qn, in1=t1, op=ALU.add), reads=[qnk, t1k], writes=[qnk])
                        S.op("act", lambda e, qn=qn, qo=qo: e.copy(out=qo, in_=qn), reads=[qnk], writes=[qok])
                else:
                    S.op("act", lambda e, qn=qn, qo=qo: e.copy(out=qo, in_=qn), reads=[qnk], writes=[qok])
                dst = (qTs if typ == 0 else kTs)
                S.op("act", lambda e, qo=qo, dst=dst, qc=qc: e.dma_start(out=dst[t][qc], in_=qo), reads=[qok], writes=[("qk", typ, t, qc)], dma="st_" + qok)

            def stage_tr(i):
                typ, qc = chunks[i]
                if typ != 1:
                    return
                qn, qnk = st[i]["qn"], st[i]["qnk"]
                pb3, pk3 = bank()
                for s_ in range(4):
                    S.op("pe", lambda e, s_=s_, pb3=pb3, qn=qn: e.transpose(pb3[:, s_ * 128:(s_ + 1) * 128], qn[:, s_ * 128:(s_ + 1) * 128], ident),
                         reads=[qnk, "ident"], writes=[pk3])
                S.op("dve", lambda e, pb3=pb3, qc=qc: e.tensor_copy(out=ktok[:, :, qc * 128:(qc + 1) * 128], in_=pb3.rearrange("p (s n) -> p s n", s=4)),
                     reads=[pk3], writes=[("ktok", qc)])
            nch = len(chunks)
            for i in range(nch + 2):
                if i < nch:
                    stage_proj(i)
                if 0 <= i - 1 < nch:
                    stage_rope(i - 1)
                if 0 <= i - 2 < nch:
                    stage_tr(i - 2)
            S.op("act", lambda e: e.dma_start(out=kts[t], in_=ktok), reads=[("ktok", qc) for qc in range(8)], writes=[("kts", t)], dma="st_ktok")
            for s_ in range(4):
                vb, vbk = vb_rot.next()
                for vg in range(4):
                    pb, pk = bank()
                    for k in range(8):
                        S.op("pe", lambda e, k=k, pb=pb, s_=s_, vg=vg: e.matmul(pb, lhsT=hT[:, k, s_ * 128:(s_ + 1) * 128], rhs=wi[:, k, 2048 + vg * 512:2048 + (vg + 1) * 512],
                                                                            start=(k == 0), stop=(k == 7)),
                             reads=[hk[k], ("wi", 2 + vg // 2)], writes=[pk])
                    if vg % 2 == 0:
                        S.op("act", lambda e, pb=pb, vb=vb, vg=vg: e.copy(out=vb[:, vg * 512:(vg + 1) * 512], in_=pb), reads=[pk], writes=[(vbk, vg)])
                    else:
                        S.op("dve", lambda e, pb=pb, vb=vb, vg=vg: e.tensor_copy(out=vb[:, vg * 512:(vg + 1) * 512], in_=pb), reads=[pk], writes=[(vbk, vg)])
                S.op("act", lambda e, vb=vb, s_=s_: e.dma_start(out=vs_[t][s_], in_=vb), reads=[(vbk, vg) for vg in range(4)], writes=[("vs", t, s_)], dma="st_" + vbk)
            for gc in range(16):
                pb, pk = bank()
                c0 = 4096 + gc * 128
                for k in range(8):
                    S.op("pe", lambda e, k=k, pb=pb, c0=c0: e.matmul(pb, lhsT=wi[:, k, c0:c0 + 128], rhs=hT[:, k, :], start=(k == 0), stop=(k == 7)),
                         reads=[hk[k], ("wi", c0 // 1024)], writes=[pk])
                go, gok = go_rot.next()
                S.op("act", lambda e, pb=pb, go=go: e.activation(out=go, in_=pb, func=AF.Silu), reads=[pk], writes=[gok])
                S.op("act", lambda e, go=go, gc=gc: e.dma_start(out=gTs[t][gc], in_=go), reads=[gok], writes=[("gTs", t, gc)], dma="st_" + gok)
        for t in range(NT):
            b1_tile(t)
        S.barrier()

    def ret_tables():
        rt = af32(770)
        lg = af32(8)
        c128 = af32(1)
        decT = af32(8 * 128).rearrange("p (a n) -> p a n", a=8)
        qdec = af32(8 * 128).rearrange("p (a n) -> p a n", a=8)
        kdec = af32(8)
        cdec = af32(8)
        S.op("sp", lambda e: e.dma_start(out=rt, in_=c_ret), writes=["rt"], dma="c0")
        S.op("sp", lambda e: e.dma_start(out=lg, in_=decr), writes=["lg"], dma="c0")
        S.op("dve", lambda e: e.memset(c128, 128.0), writes=["c128"])
        S.op("act", lambda e: e.activation(out=lg, in_=lg, func=AF.Exp, scale=-1.0), reads=["lg"], writes=["lg"])
        S.op("dve", lambda e: e.tensor_scalar(out=lg, in0=lg, scalar1=1.0, scalar2=None, op0=ALU.add), reads=["lg"], writes=["lg"])
        S.op("act", lambda e: e.activation(out=lg, in_=lg, func=AF.Ln), reads=["lg"], writes=["lg"])
        S.op("dve", lambda e: e.tensor_scalar(out=lg, in0=lg, scalar1=-1.0, scalar2=None, op0=ALU.mult), reads=["lg"], writes=["lg"])
        for d in range(2):
            for h in range(4):
                a = 4 * d + h
                S.op("act", lambda e, a=a, d=d: e.activation(out=decT[:, a, :], in_=rt[:, d * 128:(d + 1) * 128], func=AF.Exp, scale=lg[:, a:a + 1]), reads=["rt", "lg"], writes=[("decT", a)])
                S.op("dve", lambda e, a=a, d=d: e.tensor_tensor(out=decT[:, a, :], in0=decT[:, a, :], in1=rt[:, 512 + d * 128:512 + (d + 1) * 128], op=ALU.mult),
                     reads=[("decT", a), "rt"], writes=[("decT", a)])
                S.op("act", lambda e, a=a, d=d: e.activation(out=qdec[:, a, :], in_=rt[:, 256 + d * 128:256 + (d + 1) * 128], func=AF.Exp, scale=lg[:, a:a + 1]), reads=["rt", "lg"], writes=[("qdec", a)])
                S.op("act", lambda e, a=a, d=d: e.activation(out=kdec[:, a:a + 1], in_=rt[:, 768 + d:769 + d], func=AF.Exp, scale=lg[:, a:a + 1]), reads=["rt", "lg"], writes=[("kdec", a)])
                S.op("act", lambda e, a=a: e.activation(out=cdec[:, a:a + 1], in_=c128, func=AF.Exp, scale=lg[:, a:a + 1]), reads=["c128", "lg"], writes=[("cdec", a)])
        return decT, qdec, kdec, cdec

    def phase_scan(d):
        Arena.top = PERSIST_TOP
        decT, qdec, kdec, cdec = ret_tables()
        S32 = af32(4096).rearrange("p (h c e) -> p h c e", h=4, c=2)
        Sbf = [abf(4096).rearrange("p (h c e) -> p h c e", h=4, c=2) for _ in range(2)]
        qT_b = [abf(4096).rearrange("p (c n) -> p c n", c=8) for _ in range(2)]
        kT_b = [abf(4096).rearrange("p (c n) -> p c n", c=8) for _ in range(2)]
        kt_b = [abf(4096).rearrange("p (s n) -> p s n", s=4) for _ in range(2)]
        v_b = [abf(8192).rearrange("p (s n) -> p s n", s=4) for _ in range(2)]
        attm_rot = Rot([abf(128) for _ in range(4)], "attm")
        qs_rot = Rot([abf(256).rearrange("p (c n) -> p c n", c=2) for _ in range(4)], "qs")
        kf_rot = Rot([abf(256) for _ in range(4)], "kf")
        if d == 0:
            of_rot = Rot([af32(2048).rearrange("p (c n) -> p c n", c=16) for _ in range(2)], "ofst")
        else:
            gT = abf(16 * 512).rearrange("p (c n) -> p c n", c=16)
            of_rot = Rot([af32(2048).rearrange("p (c n) -> p c n", c=16) for _ in range(2)], "ofld")
            osum_b = [af32(2048).rearrange("p (c n) -> p c n", c=16) for _ in range(2)]
            pending = []
            ocnt = [0]
            sq4 = [abf(512).rearrange("p (c n) -> p c n", c=4) for _ in range(4)]
            rs_rot = Rot([af32(128) for _ in range(4)], "rsh")
            tmp_rot = Rot([af32(512).rearrange("p (c n) -> p c n", c=4) for _ in range(4)], "gtmp")
            u_rot = Rot([abf(2048).rearrange("p (c n) -> p c n", c=16) for _ in range(2)], "ust")
        sidx = [0]

        def scan_tile(t, n):
            b = n % 2
            qT, kT, kt, v = qT_b[b], kT_b[b], kt_b[b], v_b[b]
            S.op("sp", lambda e: e.dma_start(out=qT, in_=qTs[t].rearrange("c p n -> p c n")), writes=["qT%d" % b], dma="ld_q%d" % b)
            S.op("sp", lambda e: e.dma_start(out=kT, in_=kTs[t].rearrange("c p n -> p c n")), writes=["kT%d" % b], dma="ld_k%d" % b)
            S.op("sp", lambda e: e.dma_start(out=kt, in_=kts[t]), writes=["kt%d" % b], dma="ld_kt%d" % b)
            S.op("sp", lambda e: e.dma_start(out=v, in_=vs_[t].rearrange("s p n -> p s n")), writes=["v%d" % b], dma="ld_v%d" % b)
            if d == 1:
                while pending:
                    pending.pop(0)()
                S.op("sp", lambda e: e.dma_start(out=gT, in_=gTs[t].rearrange("c p n -> p c n")), writes=["gT"], dma="ld_g")
                S.op("pool", lambda e: e.tensor_tensor(out=gT, in0=gT, in1=gng_sb.unsqueeze(2).to_broadcast([128, 16, 512]), op=ALU.mult), reads=["gT", "gng"], writes=["gT"])
            if t < 8:
                seqs = [([0, 1, 2, 3] if d == 0 else [3, 2, 1, 0], None)]
            else:
                seqs = [([0, 1] if d == 0 else [1, 0], 0), ([2, 3] if d == 0 else [3, 2], 1)]
            def chunk(order, pseq, ci_, s_):
                if True:
                    first = (t == (0 if d == 0 else 7) and ci_ == 0) if t < 8 else (ci_ == 0)
                    has_state = True if t < 8 else (ci_ > 0)
                    last_sample = (t == (7 if d == 0 else 0)) and ci_ == len(order) - 1 and t < 8
                    if t < 8 and first:
                        S.op("sp", lambda e: e.dma_start(out=S32, in_=s0[d].rearrange("h (c p) e -> p h c e", p=128)), writes=[("S32", h, c) for h in range(4) for c in range(2)], dma="ld_s0")
                        nb = Sbf[sidx[0] % 2]
                        for h in range(4):
                            S.op("act", lambda e, h=h, nb=nb: e.copy(out=nb[:, h], in_=S32[:, h]), reads=[("S32", h, 0), ("S32", h, 1)], writes=[("Sbf", sidx[0] % 2, h)])
                    curS = Sbf[sidx[0] % 2]
                    curk = sidx[0] % 2
                    nxtS = Sbf[(sidx[0] + 1) % 2]
                    nxtk = (sidx[0] + 1) % 2
                    sidx[0] += 1
                    cols = slice(s_ * 128, (s_ + 1) * 128)
                    per_h = []
                    for h in range(4):
                        a = 4 * d + h
                        pb, pk = bank()
                        for dc in range(2):
                            S.op("pe", lambda e, pb=pb, h=h, dc=dc: e.matmul(pb[:, 0:128], lhsT=kT[:, 2 * h + dc, cols], rhs=qT[:, 2 * h + dc, cols], start=(dc == 0), stop=(dc == 1)),
                                 reads=["kT%d" % b, "qT%d" % b], writes=[pk])
                        am, amk = attm_rot.next()
                        S.op("dve", lambda e, pb=pb, am=am, a=a: e.tensor_tensor(out=am, in0=pb[:, 0:128], in1=decT[:, a, :], op=ALU.mult), reads=[pk, ("decT", a)], writes=[amk])
                        qs, qsk = qs_rot.next()
                        if has_state:
                            S.op("pool", lambda e, qs=qs, h=h, a=a: e.tensor_tensor(out=qs, in0=qT[:, 2 * h:2 * h + 2, cols], in1=qdec[:, a, :].unsqueeze(1).to_broadcast([128, 2, 128]), op=ALU.mult),
                                 reads=["qT%d" % b, ("qdec", a)], writes=[qsk])
                        kf, kfk = kf_rot.next()
                        if not last_sample:
                            S.op("act", lambda e, kf=kf, h=h, a=a: e.activation(out=kf, in_=kt[:, s_, h * 256:(h + 1) * 256], func=AF.Identity, scale=kdec[:, a:a + 1]),
                                 reads=["kt%d" % b, ("kdec", a)], writes=[kfk])
                        per_h.append((am, amk, qs, qsk, kf, kfk))
                    if d == 0:
                        ost, ostk = of_rot.next()
                    else:
                        ob_i = ocnt[0] % 2
                        ocnt[0] += 1
                        osum = osum_b[ob_i]
                        ofl, oflk = of_rot.next()
                        S.op("sp", lambda e, ofl=ofl: e.dma_start(out=ofl, in_=ofs[t][s_]), writes=[oflk], dma="ld_" + oflk)
                    for h in range(4):
                        am, amk, qs, qsk, kf, kfk = per_h[h]
                        po, pok = bank()
                        for ec in range(4):
                            S.op("pe", lambda e, po=po, ec=ec, h=h, am=am: e.matmul(po[:, ec * 128:(ec + 1) * 128], lhsT=v[:, s_, h * 512 + ec * 128:h * 512 + (ec + 1) * 128], rhs=am,
                                                                                start=True, stop=(not has_state)),
                                 reads=["v%d" % b, amk], writes=[pok])
                            if has_state:
                                for dc in range(2):
                                    S.op("pe", lambda e, po=po, ec=ec, h=h, dc=dc, qs=qs: e.matmul(po[:, ec * 128:(ec + 1) * 128], lhsT=curS[:, h, dc, ec * 128:(ec + 1) * 128], rhs=qs[:, dc, :],
                                                                                              start=False, stop=(dc == 1)),
                                         reads=[("Sbf", curk, h), qsk], writes=[pok])
                        pov = po.rearrange("p (c n) -> p c n", c=4)
                        if d == 0:
                            S.op("act", lambda e, pov=pov, ost=ost, h=h: e.copy(out=ost[:, 4 * h:4 * h + 4, :], in_=pov), reads=[pok], writes=[(ostk, h)])
                        else:
                            S.op("dve", lambda e, pov=pov, ofl=ofl, h=h: e.tensor_tensor(out=osum[:, 4 * h:4 * h + 4, :], in0=pov, in1=ofl[:, 4 * h:4 * h + 4, :], op=ALU.add),
                                 reads=[pok, oflk], writes=[("osum", ob_i, h)])
                    if d == 0:
                        S.op("act", lambda e, ost=ost: e.dma_start(out=ofs[t][s_], in_=ost), reads=[(ostk, h) for h in range(4)], writes=[("ofs", t, s_)], dma="st_" + ostk)
                    else:
                        while pending:
                            pending.pop(0)()
                    if not last_sample:
                        for h in range(4):
                            a = 4 * d + h
                            am, amk, qs, qsk, kf, kfk = per_h[h]
                            for dc in range(2):
                                pb, pk = bank()
                                S.op("pe", lambda e, pb=pb, kf=kf, h=h, dc=dc: e.matmul(pb, lhsT=kf[:, dc * 128:(dc + 1) * 128], rhs=v[:, s_, h * 512:(h + 1) * 512], start=True, stop=True),
                                     reads=[kfk, "v%d" % b], writes=[pk])
                                if has_state:
                                    S.op("dve", lambda e, pb=pb, h=h, dc=dc, a=a: e.scalar_tensor_tensor(out=S32[:, h, dc, :], in0=S32[:, h, dc, :], scalar=cdec[:, a:a + 1], in1=pb, op0=ALU.mult, op1=ALU.add),
                                         reads=[pk, ("S32", h, dc), ("cdec", a)], writes=[("S32", h, dc)])
                                else:
                                    S.op("dve", lambda e, pb=pb, h=h, dc=dc: e.tensor_copy(out=S32[:, h, dc, :], in_=pb), reads=[pk], writes=[("S32", h, dc)])
                            S.op("act", lambda e, h=h, nxtS=nxtS: e.copy(out=nxtS[:, h], in_=S32[:, h]), reads=[("S32", h, 0), ("S32", h, 1)], writes=[("Sbf", nxtk, h)])
                    if t == 8 and ci_ == len(order) - 1:
                        S.op("pool", lambda e, pseq=pseq: e.dma_start(out=ns[d][pseq].rearrange("h (c p) e -> p h c e", p=128), in_=S32),
                             reads=[("S32", h, c) for h in range(4) for c in range(2)], writes=[("ns", d, pseq)], dma="st_ns")
                    def gn_emit():
                        ust, ustk = u_rot.next()
                        sqs = []
                        for h in range(4):
                            sqh, sqk = sq4[h], "gsq%d" % h
                            S.op("act", lambda e, h=h, sqh=sqh: e.activation(out=sqh, in_=osum[:, 4 * h:4 * h + 4, :], func=AF.Square), reads=[("osum", ob_i, h)], writes=[sqk])
                            sqs.append((sqh, sqk))
                        pbs = []
                        for h in range(4):
                            sqh, sqk = sqs[h]
                            pb, pk = bank()
                            for ec in range(4):
                                S.op("pe", lambda e, pb=pb, ec=ec, sqh=sqh: e.matmul(pb[:, 0:128], lhsT=ones_bf, rhs=sqh[:, ec, :], start=(ec == 0), stop=(ec == 3)), reads=[sqk, "mats"], writes=[pk])
                            pbs.append((pb, pk))
                        rss = []
                        for h in range(4):
                            pb, pk = pbs[h]
                            rs, rsk = rs_rot.next()
                            S.op("act", lambda e, pb=pb, rs=rs: e.activation(out=rs, in_=pb[:, 0:128], func=AF.Ln, scale=1.0 / 512, bias=EPS), reads=[pk], writes=[rsk])
                            S.op("act", lambda e, rs=rs: e.activation(out=rs, in_=rs, func=AF.Exp, scale=-0.5), reads=[rsk], writes=[rsk])
                            rss.append((rs, rsk))
                        for h in range(4):
                            rs, rsk = rss[h]
                            tp, tpk = tmp_rot.next()
                            S.op("dve", lambda e, tp=tp, rs=rs, h=h: e.tensor_tensor(out=tp, in0=osum[:, 4 * h:4 * h + 4, :], in1=rs.unsqueeze(1).to_broadcast([128, 4, 128]), op=ALU.mult),
                                 reads=[("osum", ob_i, h), rsk], writes=[tpk])
                            S.op("dve", lambda e, tp=tp, ust=ust, h=h: e.tensor_tensor(out=ust[:, 4 * h:4 * h + 4, :], in0=tp, in1=gT[:, 4 * h:4 * h + 4, cols], op=ALU.mult),
                                 reads=[tpk, "gT"], writes=[(ustk, h)])
                        S.op("act", lambda e, ust=ust: e.dma_start(out=uTs[t][s_], in_=ust), reads=[(ustk, h) for h in range(4)], writes=[("uTs", t, s_)], dma="st_" + ustk)
                    if d == 1:
                        pending.append(gn_emit)
            for order, pseq in seqs:
                for ci_, s_ in enumerate(order):
                    chunk(order, pseq, ci_, s_)

        order_t = list(range(8)) if d == 0 else list(range(7, -1, -1))
        for n, t in enumerate(order_t + [8]):
            scan_tile(t, n)
        if d == 1:
            while pending:
                pending.pop(0)()
        S.barrier()

    def phase_B2c():
        Arena.top = PERSIST_TOP
        wo = abf(16 * 1024).rearrange("p (k n) -> p k n", k=16)
        uT_b = [abf(16 * 512).rearrange("p (c n) -> p c n", c=16) for _ in range(2)]
        xt_b = [af32(4096).rearrange("p (c n) -> p c n", c=8) for _ in range(2)]
        hout = abf(4096).rearrange("p (c n) -> p c n", c=8)
        sqrot = Rot([abf(512) for _ in range(2)], "sq")
        rstd = af32(512)
        tmprot = Rot([af32(512) for _ in range(2)], "tmp")
        load_w(wo, None, 16, "wo1", pc="w_out1")

        def c_tile(t):
            ci = tile_cond(t)
            b = t % 2
            uT, xt_ = uT_b[b], xt_b[b]
            xtk = ["xt%d_%d" % (b, c) for c in range(8)]
            for s_ in range(4):
                S.op("sp", lambda e, s_=s_: e.dma_start(out=uT[:, :, s_ * 128:(s_ + 1) * 128], in_=uTs[t][s_]), writes=[("uT", b, s_)], dma="ld_u%d" % b)
            S.op("sp", lambda e: e.dma_start(out=xt_, in_=xT[2][t]), writes=xtk, dma="ld_xt%d" % b)
            for m in range(8):
                pb, pk = bank()
                for ch in range(16):
                    S.op("pe", lambda e, ch=ch, m=m, pb=pb: e.matmul(pb, lhsT=wo[:, ch, m * 128:(m + 1) * 128], rhs=uT[:, ch, :], start=(ch == 0), stop=(ch == 15)),
                         reads=[("uT", b, s_) for s_ in range(4)] + ["wo1"], writes=[pk])
                S.op("dve", lambda e, m=m, pb=pb: e.scalar_tensor_tensor(out=xt_[:, m, :], in0=pb, scalar=modv[:, 1, 2, m, ci:ci + 1], in1=xt_[:, m, :], op0=ALU.mult, op1=ALU.add),
                     reads=[pk, xtk[m], "modv"], writes=[xtk[m]])
            S.op("pool", lambda e: e.dma_start(out=xT[3][t], in_=xt_), reads=xtk, writes=[("xT3", t)], dma="st_xt%d" % b)
            xch = [xt_[:, c, :] for c in range(8)]
            rms_rstd(xch, xtk, 8, 1.0 / D, sqrot, rstd, "rstd")
            hok = ["ho%d" % c for c in range(8)]
            modulate(xch, xtk, rstd, "rstd", 1, 3, ci, tmprot, [hout[:, c, :] for c in range(8)], hok)
            S.op("pool", lambda e: e.dma_start(out=hTs[2][t], in_=hout), reads=hok, writes=[("hTs2", t)], dma="st_ho")
        for t in range(NT):
            c_tile(t)
        S.barrier()

    allp = [("mod", phase_mod), ("A1", phase_A1), ("A2", phase_A2), ("A3", lambda: phase_mlp(0)), ("B1", phase_B1), ("B2f", lambda: phase_scan(0)),
            ("B2b", lambda: phase_scan(1)), ("B2c", phase_B2c), ("B3", lambda: phase_mlp(1))]
    for nm, fnp in allp:
        if phases is None or nm in phases:
            fnp()
    S.emit()
    return nc, es


def _prep_inputs(inp, b, consts):
    f = lambda a: np.ascontiguousarray(np.asarray(a, np.float32))
    m = {}
    m["xin"] = f(np.concatenate([inp["x_sample"][b], inp["x_prompt"][2 * b].reshape(256, D), inp["x_prompt"][2 * b + 1].reshape(256, D)], 0))
    m["cond"] = f(np.stack([_fm(inp["c"][b], 8), _fm(inp["c_ctx"], 8)], -1))
    m["kctx"] = f(np.stack([inp["cache_l0_attn_k"][b].reshape(512, 128), inp["cache_l0_swa_k"][b].reshape(512, 128)]))
    m["vctx"] = f(np.stack([inp["cache_l0_attn_v"][b].reshape(512, 128), inp["cache_l0_swa_v"][b].reshape(512, 128)]))
    m["s0"] = f(np.stack([inp["state_l1_ret_fwd"][b], inp["state_l1_ret_bwd"][b]]))
    m["adaw0"] = f(inp["l0_ada_w"])
    m["adaw1"] = f(inp["l1_ada_w"])
    m["adab"] = f(np.stack([_fm(inp["l0_ada_b"], 48), _fm(inp["l1_ada_b"], 48)], 1))
    m["gains"] = f(np.stack([_fm(inp[k], 8) for k in ("l0_norm_mix", "l0_norm_mlp", "l1_norm_mix", "l1_norm_mlp", "final_norm")], 1))
    m["qkg"] = f(np.stack([np.tile(inp["l0_q_norm"], 2), np.tile(inp["l0_k_norm"], 2)], -1))
    m["sinkr"] = f(np.tile(np.asarray(inp["l0_sink"])[None, :], (128, 1)))
    m["decr"] = f(np.tile(np.concatenate([inp["l1_ret_decay_fwd"], inp["l1_ret_decay_bwd"]])[None, :], (128, 1)))
    m["gng"] = f(_fm(np.asarray(inp["l1_ret_gn"]).reshape(-1), 16))
    w = np.asarray(inp["l0_w_in"], np.float32)
    cols = []
    for base in (0, 768):
        for c in range(4):
            cols += list(range(base + c * 64, base + (c + 1) * 64)) + list(range(base + (4 + c) * 64, base + (5 + c) * 64))
    cols += list(range(512, 640)) + list(range(1280, 1408)) + list(range(640, 768)) + list(range(1408, 1536))
    m["w_in0"] = f(w[:, cols])
    rows = []
    for base in (0, 512):
        for c in range(4):
            rows += list(range(base + c * 64, base + (c + 1) * 64)) + list(range(base + (4 + c) * 64, base + (5 + c) * 64))
    m["w_out0"] = f(np.asarray(inp["l0_w_out"], np.float32)[rows, :])
    m["w1_0"] = f(inp["l0_mlp_w1"])
    m["w2_0"] = f(inp["l0_mlp_w2"])
    m["w1_1"] = f(inp["l1_mlp_w1"])
    m["w2_1"] = f(inp["l1_mlp_w2"])
    m["w_in1"] = f(inp["l1_w_in"])
    m["w_out1"] = f(inp["l1_w_out"])
    m.update(consts)
    return m


_CACHE = {}


def kernel(**inputs):
    inp = {k: np.asarray(v) for k, v in inputs.items()}
    consts = _consts()
    if "nc" not in _CACHE:
        _CACHE["nc"] = build()
    nc, _es = _CACHE["nc"]
    in_maps = [_prep_inputs(inp, b, consts) for b in range(8)]
    r = run_bass_kernel_spmd(nc, in_maps, core_ids=list(range(8)))
    outs = r.results
    y_prompt = np.zeros((16, 256, D), np.float32)
    y_sample = np.zeros((8, 4096, D), np.float32)
    nk = [np.zeros((16, 256, 2, 64), np.float32) for _ in range(2)]
    nv = [np.zeros((16, 256, 2, 64), np.float32) for _ in range(2)]
    ns = [np.zeros((16, 4, 256, 512), np.float32) for _ in range(2)]
    for b in range(8):
        o = outs[b]
        y_sample[b] = o["y_out"][:4096]
        y_prompt[2 * b] = o["y_out"][4096:4352]
        y_prompt[2 * b + 1] = o["y_out"][4352:4608]
        for a in range(2):
            nk[a][2 * b:2 * b + 2] = o["nk"][a].reshape(2, 256, 2, 64)
            nv[a][2 * b:2 * b + 2] = o["nv"][a].reshape(2, 256, 2, 64)
            ns[a][2 * b:2 * b + 2] = o["ns"][a]
    return (y_prompt, y_sample, nk[0], nv[0], nk[1], nv[1], ns[0], ns[1])
```
